# Optimizing a Trainium2 kernel written in Bass

```python
import math
import jax, jax.numpy as jnp
from jax import lax
import numpy as np

D_MODEL = 1024
BATCH = 8
SEQ = 2048
DEPTH = 1
DEC_BATCH = 128
DEC_SEQ = 1
PAST_LEN = 16384
PAGE_SIZE = 128

N_META = 16
D_MIX = D_MODEL
S5_DIM = D_MIX // 2
S5_GROUP = 16
S5_GROUPS = S5_DIM // S5_GROUP
S5_STATE = 64
HG_DIM = D_MIX - S5_DIM
HG_HEAD_DIM = 128
HG_HEADS = HG_DIM // HG_HEAD_DIM
HG_CHUNK = 64
D_FF = ((8 * D_MODEL // 3 + 127) // 128) * 128
CONV_W = 3
IN_COLS = S5_DIM + 4 * HG_DIM
EPS = 1e-6

kernel_name = "hymba_s5_hgrn2_convffn_step"


def rms_norm(x, g):
    xf = x.astype(jnp.float32)
    y = xf * lax.rsqrt(jnp.mean(xf * xf, axis=-1, keepdims=True) + EPS)
    return (y * g.astype(jnp.float32)).astype(x.dtype)


def s5_discretize(lam_re, lam_im, log_dt, b_re, b_im):
    lam_re = lam_re.astype(jnp.float32)
    lam_im = lam_im.astype(jnp.float32)
    dt = jnp.exp(log_dt.astype(jnp.float32))[:, None]
    mag = jnp.exp(lam_re * dt)
    ar = mag * jnp.cos(lam_im * dt)
    ai = mag * jnp.sin(lam_im * dt)
    nr = ar - 1.0
    den = lam_re * lam_re + lam_im * lam_im
    cr = (nr * lam_re + ai * lam_im) / den
    ci = (ai * lam_re - nr * lam_im) / den
    b_re = b_re.astype(jnp.float32)
    b_im = b_im.astype(jnp.float32)
    bb_re = cr[..., None] * b_re - ci[..., None] * b_im
    bb_im = cr[..., None] * b_im + ci[..., None] * b_re
    return ar, ai, bb_re, bb_im


def _complex_scan_op(e1, e2):
    a1r, a1i, b1r, b1i = e1
    a2r, a2i, b2r, b2i = e2
    return (a2r * a1r - a2i * a1i,
            a2r * a1i + a2i * a1r,
            a2r * b1r - a2i * b1i + b2r,
            a2r * b1i + a2i * b1r + b2i)


def s5_mix(u, h0_re, h0_im, lam_re, lam_im, log_dt, b_re, b_im, c_re, c_im, d, w_glu, b_glu):
    n, l, _ = u.shape
    uf = u.astype(jnp.float32)
    ug = uf.reshape(n, l, S5_GROUPS, S5_GROUP)
    ar, ai, bbr, bbi = s5_discretize(lam_re, lam_im, log_dt, b_re, b_im)
    xr = jnp.einsum('nlgc,gpc->nlgp', ug, bbr)
    xi = jnp.einsum('nlgc,gpc->nlgp', ug, bbi)
    h0r = h0_re.astype(jnp.float32)
    h0i = h0_im.astype(jnp.float32)
    xr = xr.at[:, 0].add(ar * h0r - ai * h0i)
    xi = xi.at[:, 0].add(ar * h0i + ai * h0r)
    a_r = jnp.broadcast_to(ar, xr.shape)
    a_i = jnp.broadcast_to(ai, xi.shape)
    _, _, hr, hi = lax.associative_scan(_complex_scan_op, (a_r, a_i, xr, xi), axis=1)
    y = (jnp.einsum('nlgp,gcp->nlgc', hr, c_re.astype(jnp.float32))
         - jnp.einsum('nlgp,gcp->nlgc', hi, c_im.astype(jnp.float32)))
    y = y.reshape(n, l, S5_DIM) + d.astype(jnp.float32) * uf
    y = jax.nn.gelu(y)
    y = y * jax.nn.sigmoid(y @ w_glu.astype(jnp.float32) + b_glu.astype(jnp.float32))
    return y.astype(u.dtype), hr[:, -1], hi[:, -1]


def hgrn_chunks(q, k, logf, v, s0, chunk):
    n, l, h, _ = q.shape
    nc = l // chunk

    def blk(t):
        return t.reshape(n, nc, chunk, h, t.shape[-1]).transpose(1, 0, 3, 2, 4)

    mask = jnp.tril(jnp.ones((chunk, chunk), dtype=bool))

    def step(S, inp):
        qc, kc, lc, vc = inp
        b = jnp.cumsum(lc, axis=2)
        qd = qc * jnp.exp(b)
        kd = kc * jnp.exp(-b)
        att = jnp.where(mask, jnp.einsum('nhck,nhsk->nhcs', qd, kd), 0.0)
        o = jnp.einsum('nhck,nhkv->nhcv', qd, S) + jnp.einsum('nhcs,nhsv->nhcv', att, vc)
        bl = b[:, :, -1:, :]
        S = (jnp.exp(bl[:, :, 0, :])[..., None] * S
             + jnp.einsum('nhsk,nhsv->nhkv', kc * jnp.exp(bl - b), vc))
        return S, o

    S, o = lax.scan(step, s0, (blk(q), blk(k), blk(logf), blk(v)))
    o = o.transpose(1, 0, 3, 2, 4).reshape(n, l, h, v.shape[-1])
    return o, S


def hgrn_mix(q, f_logit, i_in, g, lb, norm_g, s0, n_lead):
    n, l, _ = q.shape
    heads = lambda t: t.astype(jnp.float32).reshape(n, l, HG_HEADS, HG_HEAD_DIM)
    f = lb + (1.0 - lb) * jax.nn.sigmoid(f_logit.astype(jnp.float32))
    logf = heads(jnp.log(f))
    k = heads(1.0 - f)
    qh = heads(q)
    vh = heads(i_in)
    S = s0.astype(jnp.float32)
    if n_lead > 0:
        o1, S = hgrn_chunks(qh[:, :n_lead], k[:, :n_lead], logf[:, :n_lead], vh[:, :n_lead], S, n_lead)
        rest = l - n_lead
        o2, S = hgrn_chunks(qh[:, n_lead:], k[:, n_lead:], logf[:, n_lead:], vh[:, n_lead:], S,
                            math.gcd(rest, HG_CHUNK))
        o = jnp.concatenate([o1, o2], axis=1)
    else:
        o, S = hgrn_chunks(qh, k, logf, vh, S, math.gcd(l, HG_CHUNK))
    o = o * lax.rsqrt(jnp.mean(o * o, axis=-1, keepdims=True) + EPS)
    o = o.reshape(n, l, HG_DIM) * norm_g.astype(jnp.float32) * jax.nn.silu(g.astype(jnp.float32))
    return o.astype(q.dtype), S


def conv_ffn(x, w_up, conv_w, conv_b, w_down, buf):
    l = x.shape[1]
    hid = x @ w_up
    a, v = hid[..., :D_FF], hid[..., D_FF:]
    ext = jnp.concatenate([buf.astype(a.dtype), a], axis=1)
    c = conv_b + sum(ext[:, j:j + l] * conv_w[j] for j in range(CONV_W))
    y = (jax.nn.silu(c) * v) @ w_down
    return y, ext[:, -(CONV_W - 1):]


def block(h, n_lead, s5r0, s5i0, hg0, conv0, lb,
          norm_mix_g, w_in, s5_lambda_re, s5_lambda_im, s5_log_dt, s5_b_re, s5_b_im,
          s5_c_re, s5_c_im, s5_d, s5_w_glu, s5_b_glu, hg_norm_g, w_out,
          norm_ffn_g, ffn_w_up, ffn_conv_w, ffn_conv_b, ffn_w_down):
    hn = rms_norm(h, norm_mix_g)
    z = hn @ w_in
    o = S5_DIM
    u = z[..., :o]
    q = z[..., o:o + HG_DIM]
    fl = z[..., o + HG_DIM:o + 2 * HG_DIM]
    iv = z[..., o + 2 * HG_DIM:o + 3 * HG_DIM]
    g = z[..., o + 3 * HG_DIM:]
    y5, s5r, s5i = s5_mix(u, s5r0, s5i0, s5_lambda_re, s5_lambda_im, s5_log_dt, s5_b_re, s5_b_im,
                          s5_c_re, s5_c_im, s5_d, s5_w_glu, s5_b_glu)
    yh, hg = hgrn_mix(q, fl, iv, g, lb, hg_norm_g, hg0, n_lead)
    h = h + jnp.concatenate([y5, yh], axis=-1) @ w_out
    yf, conv = conv_ffn(rms_norm(h, norm_ffn_g), ffn_w_up, ffn_conv_w, ffn_conv_b, ffn_w_down, conv0)
    return h + yf, s5r, s5i, hg, conv


def setup_inputs(seed: int = 0) -> dict:
    key = jax.random.key(seed)
    ks = jax.random.split(key, 32)
    f32 = jnp.float32
    nrm = lambda k, shape, s: jax.random.normal(k, shape, f32) * s
    n_arange = jnp.arange(S5_STATE, dtype=f32)
    return {
        "x_prompt": nrm(ks[0], (BATCH, SEQ, D_MODEL), 1.0),
        "x_sample": nrm(ks[1], (DEC_BATCH, DEC_SEQ, D_MODEL), 1.0),
        "state_s5_re": nrm(ks[2], (DEPTH, DEC_BATCH, S5_GROUPS, S5_STATE), 0.1),
        "state_s5_im": nrm(ks[3], (DEPTH, DEC_BATCH, S5_GROUPS, S5_STATE), 0.1),
        "state_hgrn": nrm(ks[4], (DEPTH, DEC_BATCH, HG_HEADS, HG_HEAD_DIM, HG_HEAD_DIM), 0.5),
        "state_ffn_conv": nrm(ks[5], (DEPTH, DEC_BATCH, CONV_W - 1, D_FF), 1.0),
        "meta_tokens": nrm(ks[6], (N_META, D_MODEL), 1.0),
        "norm_mix_g": 1.0 + nrm(ks[7], (DEPTH, D_MODEL), 0.02),
        "w_in": nrm(ks[8], (DEPTH, D_MODEL, IN_COLS), D_MODEL ** -0.5),
        "s5_lambda_re": -0.5 + nrm(ks[9], (DEPTH, S5_GROUPS, S5_STATE), 0.01),
        "s5_lambda_im": jnp.pi * n_arange + nrm(ks[10], (DEPTH, S5_GROUPS, S5_STATE), 0.01),
        "s5_log_dt": jax.random.uniform(ks[11], (DEPTH, S5_GROUPS), f32,
                                        minval=math.log(1e-3), maxval=math.log(1e-1)),
        "s5_b_re": nrm(ks[12], (DEPTH, S5_GROUPS, S5_STATE, S5_GROUP), (2 * S5_GROUP) ** -0.5),
        "s5_b_im": nrm(ks[13], (DEPTH, S5_GROUPS, S5_STATE, S5_GROUP), (2 * S5_GROUP) ** -0.5),
        "s5_c_re": nrm(ks[14], (DEPTH, S5_GROUPS, S5_GROUP, S5_STATE), S5_STATE ** -0.5),
        "s5_c_im": nrm(ks[15], (DEPTH, S5_GROUPS, S5_GROUP, S5_STATE), S5_STATE ** -0.5),
        "s5_d": nrm(ks[16], (DEPTH, S5_DIM), 1.0),
        "s5_w_glu": nrm(ks[17], (DEPTH, S5_DIM, S5_DIM), S5_DIM ** -0.5),
        "s5_b_glu": nrm(ks[18], (DEPTH, S5_DIM), 0.01),
        "hg_lower_bounds": nrm(ks[19], (DEPTH + 1, HG_DIM), 0.1),
        "hg_norm_g": 1.0 + nrm(ks[20], (DEPTH, HG_DIM), 0.02),
        "w_out": nrm(ks[21], (DEPTH, D_MIX, D_MODEL), D_MIX ** -0.5),
        "norm_ffn_g": 1.0 + nrm(ks[22], (DEPTH, D_MODEL), 0.02),
        "ffn_w_up": nrm(ks[23], (DEPTH, D_MODEL, 2 * D_FF), D_MODEL ** -0.5),
        "ffn_conv_w": nrm(ks[24], (DEPTH, CONV_W, D_FF), CONV_W ** -0.5),
        "ffn_conv_b": nrm(ks[25], (DEPTH, D_FF), 0.01),
        "ffn_w_down": nrm(ks[26], (DEPTH, D_FF, D_MODEL), D_FF ** -0.5),
        "final_norm_g": 1.0 + nrm(ks[27], (D_MODEL,), 0.02),
    }


def reference(x_prompt, x_sample, state_s5_re, state_s5_im, state_hgrn, state_ffn_conv,
              meta_tokens, norm_mix_g, w_in, s5_lambda_re, s5_lambda_im, s5_log_dt,
              s5_b_re, s5_b_im, s5_c_re, s5_c_im, s5_d, s5_w_glu, s5_b_glu,
              hg_lower_bounds, hg_norm_g, w_out, norm_ffn_g, ffn_w_up, ffn_conv_w,
              ffn_conv_b, ffn_w_down, final_norm_g):
    nb = x_prompt.shape[0]
    meta = jnp.broadcast_to(meta_tokens[None].astype(x_prompt.dtype), (nb, N_META, D_MODEL))
    hp = jnp.concatenate([meta, x_prompt], axis=1)
    hs = x_sample
    lbs = jnp.cumsum(jax.nn.softmax(hg_lower_bounds.astype(jnp.float32), axis=0), axis=0)
    p_s5r, p_s5i, p_hg, p_cv = [], [], [], []
    s_s5r, s_s5i, s_hg, s_cv = [], [], [], []
    for li in range(DEPTH):
        wts = (norm_mix_g[li], w_in[li], s5_lambda_re[li], s5_lambda_im[li], s5_log_dt[li],
               s5_b_re[li], s5_b_im[li], s5_c_re[li], s5_c_im[li], s5_d[li], s5_w_glu[li],
               s5_b_glu[li], hg_norm_g[li], w_out[li], norm_ffn_g[li], ffn_w_up[li],
               ffn_conv_w[li], ffn_conv_b[li], ffn_w_down[li])
        z5 = jnp.zeros((nb, S5_GROUPS, S5_STATE), jnp.float32)
        zh = jnp.zeros((nb, HG_HEADS, HG_HEAD_DIM, HG_HEAD_DIM), jnp.float32)
        zc = jnp.zeros((nb, CONV_W - 1, D_FF), hp.dtype)
        hp, a, b, c, d = block(hp, N_META, z5, z5, zh, zc, lbs[li], *wts)
        p_s5r.append(a); p_s5i.append(b); p_hg.append(c); p_cv.append(d)
        hs, a, b, c, d = block(hs, 0, state_s5_re[li], state_s5_im[li], state_hgrn[li],
                               state_ffn_conv[li], lbs[li], *wts)
        s_s5r.append(a); s_s5i.append(b); s_hg.append(c); s_cv.append(d)
    y_prompt = rms_norm(hp, final_norm_g)[:, N_META:]
    y_sample = rms_norm(hs, final_norm_g)
    return (y_prompt, y_sample,
            jnp.stack(p_s5r), jnp.stack(p_s5i), jnp.stack(p_hg), jnp.stack(p_cv),
            jnp.stack(s_s5r), jnp.stack(s_s5i), jnp.stack(s_hg), jnp.stack(s_cv))
```

```python
import math
K_STOP = ''
import numpy as np
import concourse.bass as bass
import concourse.mybir as mybir
from concourse.bass_utils import run_bass_kernel_spmd
from contextlib import ExitStack

F32 = mybir.dt.float32
BF16 = mybir.dt.bfloat16
AF = mybir.ActivationFunctionType
ALU = mybir.AluOpType
NDS = 48
NCORES = 8
D = 1024
NT = 2080
NTILES = 17
DFF = 2816
NJ = 22
EPS = 1e-6
SB_LO = 16512
SB_HI = 229344
SE = "pool"


class _Stop(Exception):
    pass


class Reg:
    __slots__ = ("w", "r")

    def __init__(self):
        self.w = None
        self.r = {}


class Sched:
    def __init__(self, nc, stack):
        self.nc = nc
        self.eng = {"pe": nc.tensor, "act": nc.scalar, "dve": nc.vector,
                    "pool": nc.gpsimd, "sp": nc.sync}
        self.sem = {k: stack.enter_context(nc.semaphore("s_" + k)) for k in self.eng}
        self.cnt = {k: 0 for k in self.eng}
        self.waited = {k: {} for k in self.eng}
        self.dsem = [stack.enter_context(nc.semaphore("d%d" % i)) for i in range(NDS)]
        self.dcnt = [0] * NDS
        self.dpool = {"sp": list(range(0, 24)), "pool": list(range(24, 40)), "act": list(range(40, NDS))}
        self.dnext = {"sp": 0, "pool": 0, "act": 0}
        self.pending = []
        self.nwaits = 0
        self.nops = 0

    def _wait(self, e, tok):
        sem, val, _ = tok
        assert val is not None, "dependency on unsignaled PE op"
        key = id(sem)
        if self.waited[e].get(key, 0) >= val:
            return
        self.waited[e][key] = val
        self.eng[e].wait_ge(sem, val)
        self.nwaits += 1

    def _deps(self, e, reads, writes):
        for R in reads:
            t = R.w
            if t is not None and not (t[2] == e and e == "pe"):
                self._wait(e, t)
        for R in writes:
            t = R.w
            if t is not None and not (t[2] == e and e == "pe"):
                self._wait(e, t)
            for t in R.r.values():
                if not (t[2] == e and e == "pe"):
                    self._wait(e, t)

    def _register(self, tok, reads, writes):
        for R in reads:
            R.r[id(tok[0])] = tok
        for R in writes:
            R.w = tok
            R.r = {}

    def op(self, e, fn, reads=(), writes=(), signal=True):
        self._deps(e, reads, writes)
        ins = fn(self.eng[e])
        self.nops += 1
        if signal:
            self.cnt[e] += 1
            ins.then_inc(self.sem[e], 1)
            tok = (self.sem[e], self.cnt[e], e)
            if e == "pe":
                for p in self.pending:
                    p[1] = self.cnt[e]
                self.pending = []
        else:
            assert e == "pe"
            tok = [self.sem[e], None, e]
            self.pending.append(tok)
        self._register(tok, reads, writes)
        return tok

    def dma(self, e, out, in_, reads=(), writes=(), **kw):
        pool_ = self.dpool[e]
        k = pool_[self.dnext[e]]
        self.dnext[e] = (self.dnext[e] + 1) % len(pool_)
        if self.dcnt[k] > 0:
            self._wait(e, (self.dsem[k], self.dcnt[k], None))
        self._deps(e, reads, writes)
        ins = self.eng[e].dma_start(out=out, in_=in_, **kw)
        self.dcnt[k] += 16
        ins.then_inc(self.dsem[k], 16)
        tok = (self.dsem[k], self.dcnt[k], None)
        self._register(tok, reads, writes)
        self.nops += 1
        return tok

    def barrier(self):
        for e in self.eng:
            for x in self.eng:
                if x != e and self.cnt[x] > 0:
                    self._wait(e, (self.sem[x], self.cnt[x], x))
            for k in range(NDS):
                if self.dcnt[k] > 0:
                    self._wait(e, (self.dsem[k], self.dcnt[k], None))

    def finish(self):
        e = "sp"
        for k in range(NDS):
            if self.dcnt[k] > 0:
                self._wait(e, (self.dsem[k], self.dcnt[k], None))
        for x in self.eng:
            if x != e and self.cnt[x] > 0:
                self._wait(e, (self.sem[x], self.cnt[x], x))


class Arena:
    def __init__(self, nc):
        self.nc = nc
        self.l = SB_LO
        self.r = SB_HI
        self.n = 0
        self.peak = 0

    def alloc(self, shape, dt, right=False):
        sz = 4 if dt == F32 else 2
        nb = sz
        for s in shape[1:]:
            nb *= s
        nb = (nb + 63) // 64 * 64
        if right:
            self.r -= nb
            off = self.r
        else:
            off = self.l
            self.l += nb
        assert self.l <= self.r, ("SBUF overflow", self.l, self.r)
        self.peak = max(self.peak, self.l + (SB_HI - self.r))
        self.n += 1
        return self.nc.alloc_sbuf_tensor_at("t%d" % self.n, list(shape), dt, offset=off)


def tile_cols(ti):
    return (2048, 2080) if ti == 16 else (128 * ti, 128 * ti + 128)


BLOCKS = [(0, 512), (512, 1024), (1024, 1536), (1536, 2048), (2048, 2080)]


CONST_OFF = {}


def make_consts():
    cols = []
    off = 0

    def add(name, arr):
        nonlocal off
        a = np.zeros((128, arr.shape[1]), np.float32)
        a[:arr.shape[0]] = arr
        CONST_OFF[name] = (off, arr.shape[1])
        cols.append(a)
        off += arr.shape[1]

    add("ident", np.eye(128, dtype=np.float32))
    s = np.arange(128)[:, None] % 64
    c = np.arange(64)[None, :]
    add("attmask", (s <= c).astype(np.float32))
    s = np.arange(128)[:, None]
    c = np.arange(128)[None, :]
    add("mrev", ((s > c) & (s // 64 == c // 64)).astype(np.float32))
    s = np.arange(32)[:, None]
    c = np.arange(32)[None, :]
    add("msmall", ((s > c) & (s < 16) & (c < 16)).astype(np.float32))
    p = np.arange(128)[:, None]
    col = np.arange(128)[None, :]
    add("blockmask", (((col // 16) % 2) == (p // 64)).astype(np.float32))
    col = np.arange(32)[None, :]
    add("blockident", ((p % 32) == col).astype(np.float32))
    n = np.arange(16)[None, :]
    add("onehot", ((p - 16) == n).astype(np.float32))
    col = np.arange(128)[None, :]
    add("qmask", ((col // 32) == (p // 32)).astype(np.float32))
    add("ones", np.ones((128, 128), np.float32))
    return np.concatenate(cols, axis=1)


CONSTS = make_consts()
NCONST = CONSTS.shape[1]

VEC_OFF = {}


def make_vecs(inp):
    cols = []
    off = 0

    def add(name, arr):
        nonlocal off
        VEC_OFF[name] = (off, arr.shape[1])
        cols.append(np.ascontiguousarray(arr, dtype=np.float32))
        off += arr.shape[1]

    def fm(v):
        return np.asarray(v, np.float32).reshape(-1, 128).T

    add("gmix", fm(inp["norm_mix_g"][0]))
    add("gffn", fm(inp["norm_ffn_g"][0]))
    add("hlb0", fm(inp["hg_lower_bounds"][0]))
    add("hlb1", fm(inp["hg_lower_bounds"][1]))
    add("s5d", fm(inp["s5_d"][0]))
    add("bglu", fm(inp["s5_b_glu"][0]))
    add("hgng", fm(inp["hg_norm_g"][0]))
    cw = np.asarray(inp["ffn_conv_w"][0], np.float32)
    add("convw", np.concatenate([fm(cw[r]) for r in range(3)], axis=1))
    add("convb", fm(inp["ffn_conv_b"][0]))
    return np.concatenate(cols, axis=1)


def build_nc(nvec):
    nc = bass.Bass("TRN2", target_bir_lowering=False)

    def din(name, shape):
        return nc.dram_tensor(name, list(shape), F32, kind="ExternalInput").ap()

    def dout(name, shape):
        return nc.dram_tensor(name, list(shape), F32, kind="ExternalOutput").ap()

    xtok = din("xtok", [NT, D])
    w_in = din("w_in", [D, 2560])
    w_out = din("w_out", [D, D])
    w_up = din("w_up", [D, 2 * DFF])
    w_down = din("w_down", [DFF, D])
    w_glu = din("w_glu", [512, 512])
    lamPP = din("lamPP", [3, 16, 128])
    bst = din("bst", [2, 16, 128, 16])
    cch = din("cch", [2, 512, 64])
    vecs = din("vecs", [128, nvec])
    hlbrows = din("hlbrows", [2, 512])
    gfinal = din("gfinal", [D])
    s5s = din("s5s", [2, 16, 2048])
    hgs = din("hgs", [16, 4, 128, 128])
    cvs = din("cvs", [16, 2, DFF])
    consts = din("consts", [128, NCONST])

    y = dout("y", [NT, D])
    o_s5p = dout("o_s5p", [2, 16, 128])
    o_hgp = dout("o_hgp", [4, 128, 128])
    o_cvp = dout("o_cvp", [2, DFF])
    o_s5s = dout("o_s5s", [2, 16, 2048])
    o_hgs = dout("o_hgs", [16, 4, 128, 128])
    o_cvs = dout("o_cvs", [16, 2, DFF])

    with ExitStack() as st:
        S = Sched(nc, st)
        A = Arena(nc)
        PS = nc.alloc_psum_tensor("ps", [128, 4096], F32)
        PR = [Reg() for _ in range(8)]
        pb_state = {"i": 0}

        reserved = set()

        def bank():
            i = pb_state["i"]
            while i in reserved:
                i = (i + 1) % 8
            pb_state["i"] = (i + 1) % 8
            return i

        def bank2():
            i = pb_state["i"]
            if i % 2:
                i = (i + 1) % 8
            while i in reserved or (i + 1) in reserved:
                i = (i + 2) % 8
            pb_state["i"] = (i + 2) % 8
            return i

        def PB(i, w=512, n=1):
            return PS[:, 512 * i:512 * i + w] if n == 1 else PS[:, 512 * i:512 * (i + n)]

        def PBb(i):
            return PS[:, 512 * i:512 * i + 512].bitcast(BF16)

        def mm(out, lhsT, rhs, start, stop, reads, writes, tp=None, signal=None):
            sig = stop if signal is None else signal
            kw = {}
            if tp is not None:
                kw["tile_position"] = tp
            return S.op("pe", lambda e: e.matmul(out, lhsT=lhsT, rhs=rhs, start=start, stop=stop,
                                                 skip_group_check=True, **kw),
                        reads=reads, writes=writes, signal=sig)

        def tr(out, in_, ident, reads, writes, signal=True):
            return S.op("pe", lambda e: e.transpose(out=out, in_=in_, identity=ident),
                        reads=reads, writes=writes, signal=signal)

        def act(out, in_, func, reads, writes, **kw):
            return S.op("act", lambda e: e.activation(out=out, in_=in_, func=func, **kw),
                        reads=reads, writes=writes)

        def tt(eng, out, in0, in1, op, reads, writes):
            return S.op(eng, lambda e: e.tensor_tensor(out=out, in0=in0, in1=in1, op=op),
                        reads=reads, writes=writes)

        def ts(eng, out, in0, s1, s2, op0, op1, reads, writes):
            if op1 is None:
                return S.op(eng, lambda e: e.tensor_scalar(out=out, in0=in0, scalar1=s1, scalar2=None, op0=op0),
                            reads=reads, writes=writes)
            return S.op(eng, lambda e: e.tensor_scalar(out=out, in0=in0, scalar1=s1, scalar2=s2, op0=op0, op1=op1),
                        reads=reads, writes=writes)

        def stt(eng, out, in0, scalar, in1, op0, op1, reads, writes):
            return S.op(eng, lambda e: e.scalar_tensor_tensor(out=out, in0=in0, scalar=scalar, in1=in1,
                                                              op0=op0, op1=op1),
                        reads=reads, writes=writes)

        def cp(eng, out, in_, reads, writes):
            if eng == "act":
                return S.op("act", lambda e: e.copy(out=out, in_=in_), reads=reads, writes=writes)
            return S.op(eng, lambda e: e.tensor_copy(out=out, in_=in_), reads=reads, writes=writes)

        def recip(out, in_, reads, writes):
            return S.op("dve", lambda e: e.reciprocal(out=out, in_=in_), reads=reads, writes=writes)

        def mset(eng, ap, val, writes):
            return S.op(eng, lambda e: e.memset(ap, val), writes=writes)

        def bc(ap, shape):
            return ap.to_broadcast(list(shape))

        cst = A.alloc([128, NCONST], F32)
        R_cst = Reg()
        S.dma("sp", cst[:], consts, writes=[R_cst])
        vec = A.alloc([128, nvec], F32)
        R_vec = Reg()
        S.dma("sp", vec[:], vecs, writes=[R_vec])

        def C(name, rows=128):
            o, w = CONST_OFF[name]
            return cst[0:rows, o:o + w]

        def V(name):
            o, w = VEC_OFF[name]
            return vec[:, o:o + w]

        identb = A.alloc([128, 128], BF16)
        onesb = A.alloc([128, 128], BF16)
        mrevb = A.alloc([128, 128], BF16)
        msmallb = A.alloc([32, 32], BF16)
        R_cb = Reg()
        cp("dve", identb[:], C("ident"), [R_cst], [R_cb])
        cp("dve", onesb[:], C("ones"), [R_cst], [R_cb])
        cp("dve", mrevb[:], C("mrev"), [R_cst], [R_cb])
        cp("dve", msmallb[:], C("msmall", 32), [R_cst], [R_cb])
        identf = C("ident")

        lbT = A.alloc([128, 4], F32)
        omlT = A.alloc([128, 4], F32)
        R_lb = Reg()
        tt("dve", lbT[:], V("hlb1"), V("hlb0"), ALU.subtract, [R_vec], [R_lb])
        act(lbT[:], lbT[:], AF.Exp, [R_lb], [R_lb])
        ts("dve", lbT[:], lbT[:], 1.0, None, ALU.add, None, [R_lb], [R_lb])
        recip(lbT[:], lbT[:], [R_lb], [R_lb])
        ts("dve", omlT[:], lbT[:], -1.0, 1.0, ALU.mult, ALU.add, [R_lb], [R_lb])

        mixT = A.alloc([128, 8, NT], BF16, right=True)
        R_mix = [Reg() for _ in range(NTILES)]

        def creg(regs, c0, c1):
            out = []
            for ti in range(NTILES):
                a, b = tile_cols(ti)
                if a < c1 and c0 < b:
                    out.append(regs[ti])
            return out

        mark_G = A.l

        uT = A.alloc([128, 4, NT], BF16)
        mark_R = A.r
        qT = A.alloc([128, 4, NT], BF16, right=True)
        kT = A.alloc([128, 4, NT], BF16, right=True)
        v_tok = A.alloc([128, NTILES, 512], BF16, right=True)
        kk_tok = A.alloc([128, NTILES, 512], BF16, right=True)
        ebl = A.alloc([128, 4, 33], F32)
        fsamp = A.alloc([128, 4, 16], F32)
        R_u = [Reg() for _ in range(NTILES)]
        R_q = [[Reg() for _ in range(NTILES)] for _ in range(4)]
        R_k = [[Reg() for _ in range(NTILES)] for _ in range(4)]
        R_v = [Reg() for _ in range(NTILES)]
        R_kk = [Reg() for _ in range(NTILES)]
        R_ebl = Reg()
        mark_P1out = A.l

        xnT = A.alloc([128, 8, NT], BF16)
        R_xn = [Reg() for _ in range(NTILES)]
        wtok = A.alloc([128, 8, 512], BF16)
        R_wtok = Reg()
        w_in_v = w_in.rearrange("(kt p) c -> p kt c", p=128)
        S.dma("pool", wtok[:], w_in_v[:, :, 1536:2048], writes=[R_wtok])

        def v_proj(ti):
            c0, c1 = tile_cols(ti)
            R = c1 - c0
            pi = bank()
            for kt in range(8):
                mm(PB(pi)[0:R, :], xnT[:, kt, c0:c1], wtok[:, kt, :], kt == 0, kt == 7,
                   [R_xn[ti], R_wtok], [PR[pi]])
            cp("act", v_tok[0:R, ti, :], PB(pi)[0:R, :], [PR[pi]], [R_v[ti]])

        mark_tmp = A.l
        xt = [A.alloc([128, D], F32) for _ in range(4)]
        xnb = [A.alloc([128, D], BF16) for _ in range(4)]
        junk = A.alloc([128, D], BF16)
        ssb = [A.alloc([128, 2], F32) for _ in range(4)]
        R_xt = [Reg() for _ in range(4)]
        R_xnb = [Reg() for _ in range(4)]
        R_junk = Reg()
        R_ss = [Reg() for _ in range(4)]
        gmixT = V("gmix")
        def p0_A(ti):
            c0, c1 = tile_cols(ti)
            R = c1 - c0
            b = ti % 4
            S.dma("sp", xt[b][0:R, :], xtok[c0:c1, :], writes=[R_xt[b]])
            act(junk[0:R, :], xt[b][0:R, :], AF.Square, [R_xt[b]], [R_junk, R_ss[b]], accum_out=ssb[b][0:R, 0:1])
            act(ssb[b][0:R, 1:2], ssb[b][0:R, 0:1], AF.Sqrt, [R_ss[b]], [R_ss[b]], scale=1.0 / D, bias=EPS)
            recip(ssb[b][0:R, 1:2], ssb[b][0:R, 1:2], [R_ss[b]], [R_ss[b]])
            ts("pool", xnb[b][0:R, :], xt[b][0:R, :], ssb[b][0:R, 1:2], 1.0, ALU.mult, ALU.mult, [R_xt[b], R_ss[b]], [R_xnb[b]])

        def p0_B(ti):
            c0, c1 = tile_cols(ti)
            R = c1 - c0
            b = ti % 4
            pi = bank()
            pv = PBb(pi).rearrange("p (k r) -> p k r", r=128)
            for kt in range(8):
                tr(pv[:, kt, 0:R], xnb[b][0:R, kt * 128:(kt + 1) * 128], identb[0:R, 0:R],
                   [R_xnb[b], R_cb], [PR[pi]], signal=(kt == 7))
            tt("dve", xnT[:, :, c0:c1], pv[:, :, 0:R], bc(gmixT.unsqueeze(2), [128, 8, R]), ALU.mult,
               [PR[pi], R_vec], [R_xn[ti]])

        p0_A(0)
        p0_A(1)
        for ti in range(2, NTILES):
            p0_A(ti)
            p0_B(ti - 2)
            if ti >= 3:
                v_proj(ti - 3)
        p0_B(NTILES - 2)
        v_proj(NTILES - 3)
        p0_B(NTILES - 1)
        v_proj(NTILES - 2)
        v_proj(NTILES - 1)
        S.barrier()
        A.l = mark_tmp

        if K_STOP == 'P0':
            S.finish()
            return nc
        wft = [A.alloc([128, 8, 128], BF16) for _ in range(2)]
        R_wft = [Reg(), Reg()]
        def fm_tile(col0, evac):
            b = fm_tile.n % 2
            fm_tile.n += 1
            S.dma("pool", wft[b][:], w_in_v[:, :, col0:col0 + 128], writes=[R_wft[b]])
            for bi, (c0, c1) in enumerate(BLOCKS):
                W = c1 - c0
                pi = bank()
                for kt in range(8):
                    mm(PB(pi, W), wft[b][:, kt, :], xnT[:, kt, c0:c1], kt == 0, kt == 7,
                       creg(R_xn, c0, c1) + [R_wft[b]], [PR[pi]])
                evac(bi, c0, c1, W, pi)
        fm_tile.n = 0

        def u_tile(t):
            fm_tile(0 + 128 * t, lambda bi, c0, c1, W, pi, t=t:
                    cp("act", uT[:, t, c0:c1], PB(pi, W), [PR[pi]], creg(R_u, c0, c1)))

        def g_tile(h):
            fm_tile(2048 + 128 * h, lambda bi, c0, c1, W, pi, h=h:
                    act(mixT[:, 4 + h, c0:c1], PB(pi, W), AF.Silu, [PR[pi]], creg(R_mix, c0, c1)))

        for h in range(4):
            fm_tile(512 + 128 * h, lambda bi, c0, c1, W, pi, h=h:
                    cp("dve", qT[:, h, c0:c1], PB(pi, W), [PR[pi]], creg(R_q[h], c0, c1)))
        HW_ = 1056
        lgf = [A.alloc([128, HW_], F32) for _ in range(2)]
        bTt = [A.alloc([128, HW_], F32) for _ in range(2)]
        etmp_ = [A.alloc([128, HW_], F32) for _ in range(2)]
        etmp = [etmp_, etmp_]
        smask = A.alloc([128, HW_], BF16)
        R_lgfb = [[Reg(), Reg()], [Reg(), Reg(), Reg()]]
        R_bT = [Reg(), Reg()]
        R_et_ = [Reg(), Reg()]
        R_et = [R_et_, R_et_]
        R_sm = Reg()
        mset("pool", smask[:], 1.0, [R_sm])
        mset("pool", smask[:, 0:1024:64], 0.0, [R_sm])
        mset("pool", smask[:, 1024:1025], 0.0, [R_sm])
        mset("pool", smask[:, 1040:1056], 0.0, [R_sm])

        def f_chains(h):
            ns = (1024, 1056)
            nprs = (1024, 1040)
            for half in range(2):
                n = ns[half]
                rl = R_lgfb[half]
                act(lgf[half][:, 0:n], lgf[half][:, 0:n], AF.Ln, rl, rl)
            for half in range(2):
                n = ns[half]
                S.op("dve", lambda e, half=half, n=n: e.tensor_tensor_scan(
                    out=bTt[half][:, 0:n], data0=smask[:, 0:n], data1=lgf[half][:, 0:n],
                    initial=0.0, op0=ALU.mult, op1=ALU.add),
                     reads=R_lgfb[half] + [R_sm], writes=[R_bT[half]])
            for half in range(2):
                npr, g0 = nprs[half], 1024 * half
                act(etmp_[half][:, 0:npr], bTt[half][:, 0:npr], AF.Exp, [R_bT[half]], [R_et_[half]])
            for half in range(2):
                npr, g0 = nprs[half], 1024 * half
                tt("dve", qT[:, h, g0:g0 + npr], qT[:, h, g0:g0 + npr], etmp_[half][:, 0:npr], ALU.mult,
                   [R_et_[half]] + creg(R_q[h], g0, g0 + npr), creg(R_q[h], g0, g0 + npr))
            for half in range(2):
                npr, g0 = nprs[half], 1024 * half
                act(etmp_[half][:, 0:npr], bTt[half][:, 0:npr], AF.Exp, [R_bT[half]], [R_et_[half]], scale=-1.0)
            for half in range(2):
                npr, g0 = nprs[half], 1024 * half
                tt("dve", kT[:, h, g0:g0 + npr], kT[:, h, g0:g0 + npr], etmp_[half][:, 0:npr], ALU.mult,
                   [R_et_[half]] + creg(R_k[h], g0, g0 + npr), creg(R_k[h], g0, g0 + npr))
            for half in range(2):
                act(ebl[:, h, 1 + 16 * half:17 + 16 * half], bTt[half][:, 63:1024:64], AF.Exp, [R_bT[half]], [R_ebl])
            act(ebl[:, h, 0:1], bTt[1][:, 1039:1040], AF.Exp, [R_bT[1]], [R_ebl])
            act(fsamp[:, h, :], bTt[1][:, 1040:1056], AF.Exp, [R_bT[1]], [R_ebl])

        for h in range(4):
            def evac_f(bi, c0, c1, W, pi, h=h):
                half = 0 if bi < 2 else 1
                l0 = c0 - 1024 * half
                rb = [R_lgfb[half][bi - 2 * half]]
                fl = lgf[half][:, l0:l0 + W]
                act(fl, PB(pi, W), AF.Sigmoid, [PR[pi]], rb)
                ts("dve", fl, fl, omlT[:, h:h + 1], lbT[:, h:h + 1], ALU.mult, ALU.add, rb + [R_lb], rb)
                ts("pool", kT[:, h, c0:c1], fl, -1.0, 1.0, ALU.mult, ALU.add, rb, creg(R_k[h], c0, c1))
            fm_tile(1024 + 128 * h, evac_f)
            f_chains(h)
            u_tile(h)
            g_tile(h)

        kkT = [A.alloc([128, NT], BF16) for _ in range(2)]
        R_kkT = [Reg(), Reg()]
        for h in range(4):
            kb = h % 2
            tt("dve", kkT[kb][:, 0:2048].rearrange("p (c s) -> p c s", s=64),
               kT[:, h, 0:2048].rearrange("p (c s) -> p c s", s=64),
               bc(ebl[:, h, 1:33].unsqueeze(2), [128, 32, 64]), ALU.mult,
               creg(R_k[h], 0, 2048) + [R_ebl], [R_kkT[kb]])
            ts("dve", kkT[kb][:, 2048:2064], kT[:, h, 2048:2064], ebl[:, h, 0:1], None, ALU.mult, None,
               [R_k[h][16], R_ebl], [R_kkT[kb]])
            cp("dve", kkT[kb][:, 2064:2080], kT[:, h, 2064:2080], [R_k[h][16]], [R_kkT[kb]])
            for grp, tiles in enumerate(([0, 1, 2, 3, 4, 5, 6, 7], [8, 9, 10, 11, 12, 13, 14, 15], [16])):
                pi = bank()
                pv = PBb(pi).rearrange("p (j c) -> p j c", c=128)
                for j, ti in enumerate(tiles):
                    c0, c1 = tile_cols(ti)
                    R = c1 - c0
                    tr(pv[0:R, j, :], kkT[kb][:, c0:c1], identb[:, :], [R_kkT[kb], R_cb], [PR[pi]],
                       signal=(j == len(tiles) - 1))
                R = 32 if grp == 2 else 128
                nt_ = len(tiles)
                cp("act", kk_tok[0:R, tiles[0]:tiles[0] + nt_, 128 * h:128 * h + 128], pv[0:R, 0:nt_, :],
                   [PR[pi]], [R_kk[ti_] for ti_ in tiles])

        S.barrier()
        A.l = mark_P1out

        if K_STOP == 'P1':
            S.finish()
            return nc
        def s5t(shape):
            return A.alloc(shape, F32)
        lr = s5t([128, 16]); li = s5t([128, 16]); dtt = s5t([128, 16])
        ar = s5t([128, 16]); ai = s5t([128, 16]); cr = s5t([128, 16]); ci = s5t([128, 16])
        mg16 = s5t([128, 16])
        t1 = s5t([128, 16]); t2 = s5t([128, 16]); t3 = s5t([128, 16]); t4 = s5t([128, 16])
        R_s5 = Reg()
        lpp = A.alloc([16, 3, 128], F32, right=True)
        R_lpp = Reg()
        S.dma("sp", lpp[:], lamPP.rearrange("a r c -> r a c"), writes=[R_lpp])
        pi = bank()
        for a_i in range(3):
            tr(PB(pi)[:, 16 * a_i:16 * a_i + 16], lpp[0:16, a_i, :], identf[0:16, 0:16], [R_lpp, R_cst], [PR[pi]],
               signal=(a_i == 2))
        cp("act", lr[:], PB(pi)[:, 0:16], [PR[pi]], [R_s5])
        cp("act", li[:], PB(pi)[:, 16:32], [PR[pi]], [R_s5])
        negone = s5t([128, 16])
        mset(SE, negone[:], -1.0, [R_s5])
        act(dtt[:], PB(pi)[:, 32:48], AF.Exp, [PR[pi]], [R_s5])
        R_ct = Reg()
        cdup = A.alloc([128, 2, 4, 128], F32, right=True)
        R_cdup = Reg()
        for ri in range(2):
            src = cch[ri].rearrange("(t p) s -> p t s", p=128)
            S.dma("sp", cdup[:, ri, :, 0:64], src, writes=[R_cdup])
            S.dma("sp", cdup[:, ri, :, 64:128], src, writes=[R_cdup])
        ct_re = A.alloc([128, 512], F32)
        nct_im = A.alloc([128, 512], F32)
        bbb = A.alloc([128, 16, 2, 32], BF16)
        for t in range(4):
            pi = bank()
            tr(PB(pi)[:, 0:128], cdup[:, 0, t, :], identf, [R_cdup, R_cst], [PR[pi]], signal=False)
            tr(PB(pi)[:, 128:256], cdup[:, 1, t, :], identf, [R_cdup, R_cst], [PR[pi]])
            cp("act", ct_re[:, 128 * t:128 * t + 128], PB(pi)[:, 0:128], [PR[pi]], [R_ct])
            act(nct_im[:, 128 * t:128 * t + 128], PB(pi)[:, 128:256], AF.Copy, [PR[pi]], [R_ct], scale=-1.0)
            tt(SE, ct_re[:, 128 * t:128 * t + 128], ct_re[:, 128 * t:128 * t + 128], C("blockmask"), ALU.mult,
               [R_ct, R_cst], [R_ct])
            tt(SE, nct_im[:, 128 * t:128 * t + 128], nct_im[:, 128 * t:128 * t + 128], C("blockmask"), ALU.mult,
               [R_ct, R_cst], [R_ct])
        s5r = [R_s5]

        def e_tt(out, a, b, op):
            return tt(SE, out, a, b, op, s5r, s5r)

        e_tt(t1[:], lr[:], dtt[:], ALU.mult)
        act(t2[:], t1[:], AF.Exp, s5r, s5r)
        act(mg16[:], t1[:], AF.Exp, s5r, s5r, scale=16.0)
        e_tt(t1[:], li[:], dtt[:], ALU.mult)
        act(ai[:], t1[:], AF.Sin, s5r, s5r, scale=1.0 / 8)
        act(t3[:], t1[:], AF.Sin, s5r, s5r, scale=1.0 / 16)
        e_tt(t3[:], t3[:], t3[:], ALU.mult)
        ts(SE, ar[:], t3[:], -2.0, 1.0, ALU.mult, ALU.add, s5r, s5r)
        for _ in range(3):
            e_tt(t3[:], ar[:], ar[:], ALU.mult)
            e_tt(t4[:], ai[:], ai[:], ALU.mult)
            e_tt(ai[:], ar[:], ai[:], ALU.mult)
            ts(SE, ai[:], ai[:], 2.0, None, ALU.mult, None, s5r, s5r)
            e_tt(ar[:], t3[:], t4[:], ALU.subtract)
        e_tt(ar[:], ar[:], t2[:], ALU.mult)
        e_tt(ai[:], ai[:], t2[:], ALU.mult)
        ts(SE, t1[:], ar[:], -1.0, None, ALU.add, None, s5r, s5r)
        e_tt(t3[:], lr[:], lr[:], ALU.mult)
        e_tt(t4[:], li[:], li[:], ALU.mult)
        e_tt(t3[:], t3[:], t4[:], ALU.add)
        e_tt(t3[:], t3[:], negone[:], ALU.pow)
        e_tt(cr[:], t1[:], lr[:], ALU.mult)
        e_tt(t4[:], ai[:], li[:], ALU.mult)
        e_tt(cr[:], cr[:], t4[:], ALU.add)
        e_tt(cr[:], cr[:], t3[:], ALU.mult)
        e_tt(ci[:], ai[:], lr[:], ALU.mult)
        e_tt(t4[:], t1[:], li[:], ALU.mult)
        e_tt(ci[:], ci[:], t4[:], ALU.subtract)
        e_tt(ci[:], ci[:], t3[:], ALU.mult)

        def pow_table(tr_, ti_, nmax, tmpa, tmpb):
            mset(SE, tr_[:, :, 0:1], 1.0, s5r)
            mset(SE, ti_[:, :, 0:1], 0.0, s5r)
            n = 1
            while n < nmax:
                m = min(n, nmax - n)
                mr = bc(tr_[:, :, n:n + 1], [128, 16, m])
                mi = bc(ti_[:, :, n:n + 1], [128, 16, m])
                sr = tr_[:, :, 1:1 + m]
                si = ti_[:, :, 1:1 + m]
                ta = tmpa[:, :, 0:m]
                tb = tmpb[:, :, 0:m]
                e_tt(ta, sr, mr, ALU.mult)
                e_tt(tb, si, mi, ALU.mult)
                e_tt(tr_[:, :, n + 1:n + 1 + m], ta, tb, ALU.subtract)
                e_tt(ta, sr, mi, ALU.mult)
                e_tt(tb, si, mr, ALU.mult)
                e_tt(ti_[:, :, n + 1:n + 1 + m], ta, tb, ALU.add)
                n += m

        apr = A.alloc([128, 16, 17], F32)
        api = A.alloc([128, 16, 17], F32)
        tabc = A.alloc([128, 16, 130], F32)
        tabs = A.alloc([128, 16, 130], F32)
        wt1_off = A.l
        wt1 = A.alloc([128, 8, 128], F32)
        wt2_off = A.l
        wt2 = A.alloc([128, 8, 128], F32)
        ptA = wt1[:].rearrange("p k c -> p (k c)").rearrange("p (a n) -> p a n", n=64)
        ptB = wt2[:].rearrange("p k c -> p (k c)").rearrange("p (a n) -> p a n", n=64)
        wst0 = A.alloc([128, 16, 2, 128], BF16)
        pf0 = A.alloc([128, 17, 2, 128], BF16)
        cp(SE, apr[:, :, 1:2], ar[:].unsqueeze(2), s5r, s5r)
        cp(SE, api[:, :, 1:2], ai[:].unsqueeze(2), s5r, s5r)
        pow_table(apr, api, 16, ptA, ptB)
        e_tt(t1[:], mg16[:], negone[:], ALU.pow)
        e_tt(tabc[:, :, 1:2], apr[:, :, 16:17], t1[:].unsqueeze(2), ALU.mult)
        e_tt(tabs[:, :, 1:2], api[:, :, 16:17], t1[:].unsqueeze(2), ALU.mult)
        pow_table(tabc, tabs, 129, ptA, ptB)

        bld = A.alloc([128, 2, 16, 16], F32, right=True)
        R_bld = Reg()
        for ri in range(2):
            S.dma("sp", bld[:, ri, :, :], bst[ri].rearrange("a p c -> p a c"), writes=[R_bld])
        bb_re = A.alloc([128, 16, 32], F32)
        bb_im = A.alloc([128, 16, 32], F32)
        bt1 = A.alloc([128, 16, 16], F32, right=True)
        bt2 = A.alloc([128, 16, 16], F32, right=True)
        mset(SE, bb_re[:], 0.0, s5r)
        mset(SE, bb_im[:], 0.0, s5r)
        rb = s5r + [R_bld]
        for two in range(2):
            ps_ = slice(64 * two, 64 * two + 64)
            cs_ = slice(16 * two, 16 * two + 16)
            crb = bc(cr[ps_, :].unsqueeze(2), [64, 16, 16])
            cib = bc(ci[ps_, :].unsqueeze(2), [64, 16, 16])
            tt(SE, bt1[ps_], bld[ps_, 0], crb, ALU.mult, rb, s5r)
            tt(SE, bt2[ps_], bld[ps_, 1], cib, ALU.mult, rb, s5r)
            tt(SE, bb_re[ps_, :, cs_], bt1[ps_], bt2[ps_], ALU.subtract, s5r, s5r)
            tt(SE, bt1[ps_], bld[ps_, 1], crb, ALU.mult, rb, s5r)
            tt(SE, bt2[ps_], bld[ps_, 0], cib, ALU.mult, rb, s5r)
            tt(SE, bb_im[ps_, :, cs_], bt1[ps_], bt2[ps_], ALU.add, s5r, s5r)

        cp(SE, bbb[:, :, 0, :], bb_re[:], s5r, s5r)
        cp(SE, bbb[:, :, 1, :], bb_im[:], s5r, s5r)

        s5r = [R_s5, R_ct]
        wst = [wst0]
        pf = [pf0]
        R_wst = [Reg(), Reg()]; R_pf = [Reg(), Reg()]
        R_wt12 = Reg()
        def s5_tables(t, part):
            tb = t % 2
            prs = slice(4 * t, 4 * t + 4)
            wgroups = [(0, 8, "pool"), (8, 8, "pool")]
            for (k0_, nk_, eng_) in wgroups:
                if part not in ("wst", "wst_all_pool"):
                    continue
                ks = slice(k0_, k0_ + nk_)
                a_r = bc(apr[:, prs, ks].rearrange("p a k -> p k a").unsqueeze(3), [128, nk_, 4, 32])
                a_i = bc(api[:, prs, ks].rearrange("p a k -> p k a").unsqueeze(3), [128, nk_, 4, 32])
                b_r = bc(bb_re[:, prs, :].unsqueeze(1), [128, nk_, 4, 32])
                b_i = bc(bb_im[:, prs, :].unsqueeze(1), [128, nk_, 4, 32])
                wa, wb_, R_w = wt1, wt2, R_wt12
                w1 = wa[:, 0:nk_, :].rearrange("p k (a c) -> p k a c", c=32)
                w2 = wb_[:, 0:nk_, :].rearrange("p k (a c) -> p k a c", c=32)
                o_r = wst[tb][:, ks, 0, :].rearrange("p k (a c) -> p k a c", c=32)
                o_i = wst[tb][:, ks, 1, :].rearrange("p k (a c) -> p k a c", c=32)
                rw = s5r + [R_w]
                tt(eng_, w1, a_r, b_r, ALU.mult, s5r, [R_w])
                tt(eng_, w2, a_i, b_i, ALU.mult, s5r, [R_w])
                tt(eng_, o_r, w1, w2, ALU.subtract, rw, [R_wst[tb]])
                tt(eng_, w1, a_r, b_i, ALU.mult, rw, [R_w])
                tt(eng_, w2, a_i, b_r, ALU.mult, rw, [R_w])
                tt(eng_, o_i, w1, w2, ALU.add, rw, [R_wst[tb]])
            if part == "up":
                cp("act", uP[tb][:, :, 0:128], uT[:, t, 0:2048].rearrange("p (c s) -> p s c", s=16),
                   creg(R_u, 0, 2048), [R_uP[tb]])
                cp("act", uP[tb][:, :, 128], uT[:, t, 2048:2064], [R_u[16]], [R_uP[tb]])
            for (k0, nk) in ((0, 8), (8, 4), (12, 4), (16, 1)):
                on_pool = (k0 < 12)
                if part not in ("pfp", "dve") or on_pool != (part == "pfp"):
                    continue
                eng_ = "pool" if on_pool else "dve"
                if on_pool:
                    wa, wb_, R_w = wt1, wt2, R_wt12
                else:
                    wa, wb_, R_w = wt3, wt4, R_wt34
                ks = slice(k0, k0 + nk)
                a_r = bc(apr[:, prs, ks].rearrange("p a k -> p k a").unsqueeze(3), [128, nk, 4, 32])
                a_i = bc(api[:, prs, ks].rearrange("p a k -> p k a").unsqueeze(3), [128, nk, 4, 32])
                c_r = bc(ct_re[:, 128 * t:128 * t + 128].rearrange("p (a c) -> p a c", c=32).unsqueeze(1), [128, nk, 4, 32])
                c_i = bc(nct_im[:, 128 * t:128 * t + 128].rearrange("p (a c) -> p a c", c=32).unsqueeze(1), [128, nk, 4, 32])
                w1 = wa[:, 0:nk, :].rearrange("p k (a c) -> p k a c", c=32)
                w2 = wb_[:, 0:nk, :].rearrange("p k (a c) -> p k a c", c=32)
                o_r = pf[tb][:, ks, 0, :].rearrange("p k (a c) -> p k a c", c=32)
                o_i = pf[tb][:, ks, 1, :].rearrange("p k (a c) -> p k a c", c=32)
                rw = s5r + [R_w]
                tt(eng_, w1, a_r, c_r, ALU.mult, s5r, [R_w])
                tt(eng_, w2, a_i, c_i, ALU.mult, s5r, [R_w])
                tt(eng_, o_r, w1, w2, ALU.add, rw, [R_pf[tb]])
                tt(eng_, w1, a_r, c_i, ALU.mult, rw, [R_w])
                tt(eng_, w2, a_i, c_r, ALU.mult, rw, [R_w])
                tt(eng_, o_i, w1, w2, ALU.subtract, rw, [R_pf[tb]])

        s5_tables(0, "wst_all_pool")
        s5_tables(0, "pfp")
        if K_STOP == 'P2base':
            S.finish()
            return nc
        mark_base = A.l
        Sf = A.alloc([128, 4, 128], F32)
        Sb = A.alloc([128, 4, 128], BF16)
        attsb = A.alloc([128, 4, 64], BF16)
        mix_off = SB_HI - 128 * 0 - (8 * NT * 2)
        oblk = nc.alloc_sbuf_tensor_at("oblk", [128, 4, 512], F32, offset=mix_off)
        sqb = nc.alloc_sbuf_tensor_at("sqb", [128, 4, 512], BF16, offset=mix_off + 8192)
        rsb = nc.alloc_sbuf_tensor_at("rsb", [128, 512], F32, offset=mix_off + 12288)
        otmp = nc.alloc_sbuf_tensor_at("otmp", [128, 512], F32, offset=mix_off + 14336)
        R_Sf = Reg(); R_Sb = Reg(); R_att = Reg(); R_ob = Reg(); R_sq = Reg(); R_rs = Reg(); R_ot = Reg()
        hgngT = V("hgng")

        def hg_norm_block(c0, W, rd_extra):
            for h in range(4):
                act(sqb[:, h, 0:W], oblk[:, h, 0:W], AF.Square, [R_ob] + rd_extra, [R_sq])
                pi = bank()
                mm(PB(pi, W), onesb[:, :], sqb[:, h, 0:W], True, True, [R_cb, R_sq], [PR[pi]])
                act(rsb[:, 0:W], PB(pi, W), AF.Ln, [PR[pi]], [R_rs], scale=1.0 / 128, bias=EPS)
                act(rsb[:, 0:W], rsb[:, 0:W], AF.Exp, [R_rs], [R_rs], scale=-0.5)
                stt("dve", otmp[:, 0:W], oblk[:, h, 0:W], hgngT[:, h:h + 1], rsb[:, 0:W], ALU.mult, ALU.mult,
                    [R_ob, R_rs, R_vec], [R_ot])
                mr = creg(R_mix, c0, c0 + W)
                tt("dve", mixT[:, 4 + h, c0:c0 + W], mixT[:, 4 + h, c0:c0 + W], otmp[:, 0:W], ALU.mult,
                   [R_ot] + mr, mr)

        attsbs = [attsb] + [A.alloc([128, 4, 64], BF16) for _ in range(2)]
        R_atts = [R_att, Reg(), Reg()]

        def hg_geo(ci):
            if ci < 0:
                return 2048, 16, 16, 0, 0
            return 64 * ci, 64, ci // 2, 64 * (ci % 2), ci + 1

        hg_state = {}

        def hg_X(ci):
            c0, Wc, ti, r0, ei = hg_geo(ci)
            rows = slice(r0, r0 + Wc)
            sl = (ci + 1) % 3
            pa = bank()
            for h in range(4):
                mm(PB(pa)[rows, 64 * h:64 * h + Wc], kT[:, h, c0:c0 + Wc], qT[:, h, c0:c0 + Wc], True, True,
                   creg(R_k[h], c0, c0 + Wc) + creg(R_q[h], c0, c0 + Wc), [PR[pa]], tp=(0, r0), signal=(h == 3))
            tt("dve", attsbs[sl][rows, :, 0:Wc], PB(pa)[rows, 0:256].rearrange("p (h c) -> p h c", c=64)[:, :, 0:Wc],
               bc(C("attmask")[rows, 0:Wc].unsqueeze(1), [Wc, 4, Wc]), ALU.mult, [PR[pa], R_cst], [R_atts[sl]])

        def hg_Y(ci):
            c0, Wc, ti, r0, ei = hg_geo(ci)
            rows = slice(r0, r0 + Wc)
            pS = bank()
            for h in range(4):
                hc = slice(128 * h, 128 * h + 128)
                mm(PB(pS)[:, hc], kk_tok[rows, ti, hc], v_tok[rows, ti, hc], True, True,
                   [R_kk[ti], R_v[ti]], [PR[pS]], tp=(r0, 0), signal=(h == 3))
            hg_state[ci] = pS
            reserved.add(pS)

        def hg_Z(ci):
            c0, Wc, ti, r0, ei = hg_geo(ci)
            rows = slice(r0, r0 + Wc)
            sl = (ci + 1) % 3
            po = bank()
            for h in range(4):
                hc = slice(128 * h, 128 * h + 128)
                if ci >= 0:
                    mm(PB(po)[:, 64 * h:64 * h + Wc], Sb[:, h, :], qT[:, h, c0:c0 + Wc], True, False,
                       [R_Sb] + creg(R_q[h], c0, c0 + Wc), [PR[po]])
                mm(PB(po)[:, 64 * h:64 * h + Wc], v_tok[rows, ti, hc], attsbs[sl][rows, h, 0:Wc], ci < 0, True,
                   [R_v[ti], R_atts[sl]], [PR[po]], tp=(r0, 0), signal=(h == 3))
            ocol = (c0 % 512) if ci >= 0 else 0
            cp("act", oblk[:, :, ocol:ocol + Wc], PB(po)[:, 0:256].rearrange("p (h c) -> p h c", c=64)[:, :, 0:Wc],
               [PR[po]], [R_ob])
            pS = hg_state.pop(ci)
            reserved.discard(pS)
            if ci < 0:
                cp("dve", Sf[:].rearrange("p h v -> p (h v)"), PB(pS)[:, 0:512], [PR[pS]], [R_Sf])
            else:
                for h in range(4):
                    stt("dve", Sf[:, h, :], Sf[:, h, :], ebl[:, h, ei:ei + 1], PB(pS)[:, 128 * h:128 * h + 128],
                        ALU.mult, ALU.add, [R_Sf, R_ebl, PR[pS]], [R_Sf])
            cp("act", Sb[:].rearrange("p h v -> p (h v)"), Sf[:].rearrange("p h v -> p (h v)"), [R_Sf], [R_Sb])

        NSB = 4
        s0 = [A.alloc([128, 4, 128], F32) for _ in range(NSB)]
        s1b = [A.alloc([128, 4, 128], BF16) for _ in range(2)]
        kkm = [A.alloc([32, 512], BF16) for _ in range(3)]
        R_s0 = [Reg() for _ in range(NSB)]
        R_s1b = [Reg(), Reg()]
        R_kkm = [Reg(), Reg(), Reg()]

        def hg_sample_load(n):
            S.dma("sp", s0[n % NSB][:], hgs[n].rearrange("h k v -> k h v"), writes=[R_s0[n % NSB]])

        def hg_kkm(n):
            act(kkm[n % 3][0:32, :], kk_tok[0:32, 16, :], AF.Copy, [R_kk[16], R_cst], [R_kkm[n % 3]],
                scale=C("onehot", 32)[:, n:n + 1])

        def hg_s1(n):
            b = n % 2
            kb = n % 3
            sb_ = n % NSB
            if n >= NSB:
                hg_sample_load(n)
            if n + 2 < 16:
                hg_kkm(n + 2)
            pk = bank()
            for h in range(4):
                hc = slice(128 * h, 128 * h + 128)
                mm(PB(pk)[:, hc], kkm[kb][0:32, hc], v_tok[0:32, 16, hc], True, True,
                   [R_kkm[kb], R_v[16]], [PR[pk]], signal=(h == 3))
            s0f = s0[sb_][:].rearrange("p h v -> p (h v)")
            tt("dve", s0[sb_][:], s0[sb_][:], bc(fsamp[:, :, n:n + 1], [128, 4, 128]), ALU.mult,
               [R_s0[sb_], R_ebl], [R_s0[sb_]])
            tt("dve", s0f, s0f, PB(pk)[:, 0:512], ALU.add, [R_s0[sb_], PR[pk]], [R_s0[sb_]])
            cp("act", s1b[b][:].rearrange("p h v -> p (h v)"), s0f, [R_s0[sb_]], [R_s1b[b]])
            S.dma("act", o_hgs[n].rearrange("h k v -> k h v"), s0[sb_][:], reads=[R_s0[sb_]])

        def hg_s2(n, po):
            b = n % 2
            for h in range(4):
                mm(PB(po)[:, 16 * h + n:16 * h + n + 1], s1b[b][:, h, :], qT[:, h, 2064 + n:2065 + n], True, True,
                   [R_s1b[b], R_q[h][16]], [PR[po]], signal=(h == 3))

        def hg_samples(n_list, po):
            hg_kkm(0)
            hg_kkm(1)
            hg_s1(n_list[0])
            for i_ in range(1, len(n_list)):
                hg_s1(n_list[i_])
                hg_s2(n_list[i_ - 1], po)
            hg_s2(n_list[-1], po)

        for n_ in range(NSB):
            hg_sample_load(n_)

        ometa = A.alloc([128, 4, 32], F32)
        R_om = Reg()
        order = [-1] + list(range(32))
        LA = 2
        for i_ in range(min(LA, len(order))):
            hg_X(order[i_])
            hg_Y(order[i_])
        po_s = bank()
        reserved.add(po_s)
        hg_kkm(0)
        hg_kkm(1)
        ns_done = 0
        for i_, ci in enumerate(order):
            if i_ + LA < len(order):
                hg_X(order[i_ + LA])
                hg_Y(order[i_ + LA])
            hg_Z(ci)
            if ci < 0:
                cp("dve", ometa[:, :, 0:16], oblk[:, :, 0:16], [R_ob], [R_om])
            elif ci % 8 == 7:
                hg_norm_block(512 * (ci // 8), 512, [])
            if ci >= 0 and ci % 2 == 0 and ns_done < 16:
                hg_s1(ns_done)
                if ns_done >= 1:
                    hg_s2(ns_done - 1, po_s)
                ns_done += 1
        hg_s2(15, po_s)
        reserved.discard(po_s)
        for h in range(4):
            S.dma("sp", o_hgp[h], Sf[:, h, :], reads=[R_Sf])
        cp("act", ometa[:, :, 16:32], PB(po_s)[:, 0:64].rearrange("p (h n) -> p h n", n=16), [PR[po_s]], [R_om])
        cp("dve", oblk[:, :, 0:32], ometa[:, :, :], [R_om, R_ob], [R_ob])
        hg_norm_block(2048, 32, [])

        S.barrier()
        A.l = mark_base
        A.r = mark_R
        if K_STOP == 'P2a':
            S.finish()
            return nc
        s5tok = A.alloc([16, 2, 2048], F32)
        R_s5tok = Reg()
        S.dma("sp", s5tok[:], s5s.rearrange("r n c -> n r c"), writes=[R_s5tok])
        h0 = A.alloc([128, 16, 2, 16], F32)
        h0b = A.alloc([128, 16, 2, 16], BF16)
        R_h0 = Reg()
        for half in range(2):
            pi = bank()
            for pl in range(8):
                pair = 8 * half + pl
                for ri in range(2):
                    tr(PB(pi)[:, (pl * 2 + ri) * 16:(pl * 2 + ri) * 16 + 16], s5tok[0:16, ri, pair * 128:(pair + 1) * 128],
                       identf[0:16, 0:16], [R_s5tok, R_cst], [PR[pi]], signal=(pl == 7 and ri == 1))
            cp("act", h0[:, 8 * half:8 * half + 8, :, :].rearrange("p a r n -> p (a r n)"), PB(pi)[:, 0:256],
               [PR[pi]], [R_h0])
        cp("dve", h0b[:].rearrange("p a r n -> p (a r n)"), h0[:].rearrange("p a r n -> p (a r n)"), [R_h0], [R_h0])

        wst.append(A.alloc([128, 16, 2, 128], BF16))
        pf.append(A.alloc([128, 17, 2, 128], BF16))
        wd = A.alloc([128, 16, 2, 128], BF16)
        knf = [A.alloc([128, 16, 128], BF16) for _ in range(2)]
        wt3 = A.alloc([128, 4, 128], F32)
        wt4 = A.alloc([128, 4, 128], F32)
        dsb = A.alloc([128, 4, 2, 146], F32)
        drot = A.alloc([128, 4, 2, 128], F32)
        gsc = A.alloc([128, 4, 2, 128], F32)
        sbuf_ = A.alloc([128, 4, 2, 129], F32)
        sfar = A.alloc([128, 4, 2, 128], BF16)
        dt1 = A.alloc([128, 4, 128], F32)
        dt2 = A.alloc([128, 4, 128], F32)
        g1 = A.alloc([128, 4, 2], F32)
        sfin = A.alloc([128, 2, 16], F32)
        s1s = A.alloc([128, 16, 2, 16], F32)
        uP = [A.alloc([128, 16, 130], BF16) for _ in range(2)]
        R_uP = [Reg(), Reg()]
        gtmps = [A.alloc([128, 512], F32) for _ in range(2)]
        R_gts = [Reg(), Reg()]
        gtmp = gtmps[0]
        R_gt = R_gts[0]
        R_wd = Reg(); R_knf = [Reg(), Reg()]
        R_wt34 = Reg()
        ktmp_t = A.alloc([128, 512], F32)
        R_kt = Reg()
        R_d = Reg(); R_sfar = Reg(); R_sfin = Reg(); R_s1s = Reg()
        sdT = V("s5d")
        w_glu_sb = A.alloc([128, 4, 512], BF16)
        R_wglu = Reg()
        S.dma("pool", w_glu_sb[:], w_glu.rearrange("(kt p) c -> p kt c", p=128), writes=[R_wglu])

        def s5_F(t):
            tb = t % 2
            prs = slice(4 * t, 4 * t + 4)
            for grp in range(4):
                pi = bank()
                pv = PBb(pi).rearrange("p (j c) -> p j c", c=128)
                for j in range(8):
                    idx = grp * 8 + j
                    k, ri = idx // 2, idx % 2
                    tr(pv[:, j, :], wst[tb][:, k, ri, :], identb[:, :], [R_wst[tb], R_cb], [PR[pi]], signal=(j == 7))
                cp("act", wd[:, 4 * grp:4 * grp + 4, :, :].rearrange("p k r c -> p (k r c)"), PBb(pi)[:, :],
                   [PR[pi]], [R_wd])
            dbanks = [bank(), bank(), bank(), bank()]
            ur = R_u
            for q in range(4):
                for ri in range(2):
                    db = dbanks[q]
                    o0 = 160 * ri
                    for s_ in range(16):
                        mm(PB(db)[:, o0:o0 + 129], wd[32 * q:32 * q + 32, 15 - s_, ri, :],
                           uP[tb][32 * q:32 * q + 32, s_, 0:129], s_ == 0, s_ == 15, [R_wd, R_uP[tb]], [PR[db]],
                           tp=(32 * q, 0))
                    mm(PB(db)[:, o0 + 130:o0 + 146], wd[32 * q:32 * q + 32, 0, ri, :],
                       uT[32 * q:32 * q + 32, t, 2064:2080], True, True, [R_wd, R_u[16]], [PR[db]], tp=(32 * q, 0))
            for q in range(4):
                db = dbanks[q]
                cp("act", dsb[:, q, :, :], PB(db)[:, 0:320].rearrange("p (r c) -> p r c", c=160)[:, :, 0:146],
                   [PR[db]], [R_d])

        def s5_K(t):
            tb = t % 2
            prs = slice(4 * t, 4 * t + 4)
            pi = bank()
            for q in range(4):
                pair = 4 * t + q
                for ri in range(2):
                    mm(PB(pi)[32 * q:32 * q + 32, :], bbb[:, pair, ri, :], pf[tb][:, 0:16, ri, 32 * q:32 * q + 32],
                       ri == 0, ri == 1, [R_pf[tb], R_s5], [PR[pi]], tp=(0, 32 * q), signal=(q == 3 and ri == 1))
            ktmp = ktmp_t[:, :]
            cp("act", ktmp, PB(pi)[:, :], [PR[pi]], [R_kt])
            stt("dve", ktmp[:, 0:32], C("blockident"), sdT[:, t:t + 1], ktmp[:, 0:32], ALU.mult, ALU.add,
                [R_kt, R_cst, R_vec], [R_kt])
            tt("dve", knf[tb][:].rearrange("p k (a c) -> p k a c", c=32),
               bc(ktmp.rearrange("p (k c) -> p k c", c=32).unsqueeze(2), [128, 16, 4, 32]),
               bc(C("qmask").rearrange("p (a c) -> p a c", c=32).unsqueeze(1), [128, 16, 4, 32]), ALU.mult,
               [R_kt, R_cst], [R_knf[tb]])

        def s5_M(t):
            tb = t % 2
            prs = slice(4 * t, 4 * t + 4)
            rd = [R_d] + s5r
            tcs = tabc[:, prs, 2:130]
            tss = tabs[:, prs, 2:130]
            dre = dsb[:, :, 0, 0:128]
            dim_ = dsb[:, :, 1, 0:128]
            tt("dve", dt1[:], dre, tcs, ALU.mult, rd, [R_d])
            tt("dve", dt2[:], dim_, tss, ALU.mult, rd, [R_d])
            tt("dve", drot[:, :, 0, :], dt1[:], dt2[:], ALU.add, rd, [R_d])
            tt("dve", dt1[:], dim_, tcs, ALU.mult, rd, [R_d])
            tt("dve", dt2[:], dre, tss, ALU.mult, rd, [R_d])
            tt("dve", drot[:, :, 1, :], dt1[:], dt2[:], ALU.subtract, rd, [R_d])
            tc1 = tabc[:, prs, 1]
            ts1 = tabs[:, prs, 1]
            dmr = dsb[:, :, 0, 128]
            dmi = dsb[:, :, 1, 128]
            tt("dve", dt1[:, :, 0], dmr, tc1, ALU.mult, rd, [R_d])
            tt("dve", dt2[:, :, 0], dmi, ts1, ALU.mult, rd, [R_d])
            tt("dve", g1[:, :, 0], dt1[:, :, 0], dt2[:, :, 0], ALU.add, rd, [R_d])
            tt("dve", dt1[:, :, 0], dmi, tc1, ALU.mult, rd, [R_d])
            tt("dve", dt2[:, :, 0], dmr, ts1, ALU.mult, rd, [R_d])
            tt("dve", g1[:, :, 1], dt1[:, :, 0], dt2[:, :, 0], ALU.subtract, rd, [R_d])
            for q in range(4):
                pair = 4 * t + q
                for ri in range(2):
                    S.op("dve", lambda e, q=q, ri=ri, pair=pair: e.tensor_tensor_scan(
                        out=gsc[:, q, ri, :], data0=bc(mg16[:, pair:pair + 1], [128, 128]), data1=drot[:, q, ri, :],
                        initial=g1[:, q, ri:ri + 1], op0=ALU.mult, op1=ALU.add), reads=rd, writes=[R_d])
            gre = gsc[:, :, 0, :]
            gim = gsc[:, :, 1, :]
            tt("dve", dt1[:], gre, tcs, ALU.mult, rd, [R_d])
            tt("dve", dt2[:], gim, tss, ALU.mult, rd, [R_d])
            tt("dve", sbuf_[:, :, 0, 1:129], dt1[:], dt2[:], ALU.subtract, rd, [R_d])
            tt("dve", dt1[:], gim, tcs, ALU.mult, rd, [R_d])
            tt("dve", dt2[:], gre, tss, ALU.mult, rd, [R_d])
            tt("dve", sbuf_[:, :, 1, 1:129], dt1[:], dt2[:], ALU.add, rd, [R_d])
            cp("dve", sbuf_[:, :, :, 0], dsb[:, :, :, 128], rd, [R_d])
            cp("dve", sfar[:], sbuf_[:, :, :, 0:128], rd, [R_sfar])
            cp("dve", sfin[:, :, prs].rearrange("p r a -> p a r"), sbuf_[:, :, :, 128], rd, [R_sfin])
            arb = bc(ar[:, prs].unsqueeze(2), [128, 4, 16])
            aib = bc(ai[:, prs].unsqueeze(2), [128, 4, 16])
            h0r = h0[:, prs, 0, :]
            h0i = h0[:, prs, 1, :]
            e1 = dt1[:, :, 0:16]
            e2 = dt2[:, :, 0:16]
            rh = rd + [R_h0]
            tt("dve", e1, h0r, arb, ALU.mult, rh, [R_d])
            tt("dve", e2, h0i, aib, ALU.mult, rh, [R_d])
            tt("dve", e1, e1, e2, ALU.subtract, rd, [R_d])
            tt("dve", s1s[:, prs, 0, :], e1, dsb[:, :, 0, 130:146], ALU.add, rd, [R_s1s])
            tt("dve", e1, h0i, arb, ALU.mult, rh, [R_d])
            tt("dve", e2, h0r, aib, ALU.mult, rh, [R_d])
            tt("dve", e1, e1, e2, ALU.add, rd, [R_d])
            tt("dve", s1s[:, prs, 1, :], e1, dsb[:, :, 1, 130:146], ALU.add, rd, [R_s1s])

        def s5_B(t):
            tb = t % 2
            prs = slice(4 * t, 4 * t + 4)

            def gelu(x, g_a, mo, pr, mregs, R_g):
                act(g_a, x, AF.Square, [pr], [R_g])
                act(g_a, g_a, AF.Identity, [R_g], [R_g], scale=0.044715, bias=1.0)
                tt("dve", g_a, g_a, x, ALU.mult, [R_g, pr], [R_g])
                act(g_a, g_a, AF.Sigmoid, [R_g], [R_g], scale=1.5957691216057308)
                tt("dve", mo, g_a, x, ALU.mult, [R_g, pr], mregs)
            uv = uP[tb][:, :, 0:128]
            ureg = [R_uP[tb]]
            mview = mixT[:, t, 0:2048].rearrange("p (c s) -> p c s", s=16)
            for b_ in range(4):
                yb_ = bank()
                for k in range(0, 4 * b_ + 4):
                    s_lo = max(k, 4 * b_)
                    mm(PB(yb_)[:, (s_lo - 4 * b_) * 128:512], knf[tb][:, k, :], uv[:, s_lo - k:4 * b_ + 4 - k, :],
                       k == 0, False, [R_knf[tb]] + ureg, [PR[yb_]])
                for k in range(4 * b_, 4 * b_ + 4):
                    for q in range(4):
                        for ri in range(2):
                            last = (k == 4 * b_ + 3 and q == 3 and ri == 1)
                            mm(PB(yb_)[32 * q:32 * q + 32, (k - 4 * b_) * 128:(k - 4 * b_) * 128 + 128],
                               pf[tb][:, k + 1, ri, 32 * q:32 * q + 32], sfar[:, q, ri, :], False, last,
                               [R_pf[tb], R_sfar], [PR[yb_]], tp=(0, 32 * q), signal=last)
                gelu(PB(yb_).rearrange("p (s c) -> p c s", c=128),
                     gtmps[b_ % 2][:, :].rearrange("p (c s) -> p c s", s=4),
                     mview[:, :, 4 * b_:4 * b_ + 4], PR[yb_], creg(R_mix, 0, 2048), R_gts[b_ % 2])
            p4 = bank()
            ureg4 = [R_u[16]]
            mm(PB(p4, 32), knf[tb][:, 0, :], uT[:, t, 2048:2080], True, False, [R_knf[tb]] + ureg4, [PR[p4]])
            for k in range(1, 16):
                mm(PB(p4)[:, k:16], knf[tb][:, k, :], uT[:, t, 2048:2048 + 16 - k], False, False,
                   [R_knf[tb]] + ureg4, [PR[p4]])
            for q in range(4):
                pair = 4 * t + q
                for ri in range(2):
                    last = (q == 3 and ri == 1)
                    mm(PB(p4)[32 * q:32 * q + 32, 16:32], pf[tb][:, 1, ri, 32 * q:32 * q + 32],
                       h0b[:, pair, ri, :], False, last, [R_pf[tb], R_h0], [PR[p4]], tp=(0, 32 * q), signal=last)
            gelu(PB(p4, 32), gtmps[0][:, 0:32], mixT[:, t, 2048:2080], PR[p4], [R_mix[16]], R_gts[0])

        if K_STOP == 'T0':
            S.finish()
            return nc
        try:
            s5_tables(0, "up")
            s5_tables(0, "dve")
            s5_tables(1, "wst")
            s5_tables(1, "pfp")
            s5_tables(1, "up")
            s5_F(0)
            s5_K(0)
            for t in range(4):
                s5_M(t)
                if t + 1 < 4:
                    s5_tables(t + 1, "dve")
                    if t + 2 < 4:
                        s5_tables(t + 2, "wst")
                    s5_F(t + 1)
                s5_B(t)
                if t + 2 < 4:
                    s5_tables(t + 2, "pfp")
                    s5_tables(t + 2, "up")
                if t + 1 < 4:
                    s5_K(t + 1)
        except _Stop:
            S.finish()
            return nc

        gates = [nc.alloc_sbuf_tensor_at("gate0", [128, 4, 512], BF16, offset=wt1_off),
                 nc.alloc_sbuf_tensor_at("gate1", [128, 4, 512], BF16, offset=wt2_off)]
        R_gates = [R_wt12, Reg()]
        bgT = V("bglu")
        for bi, (c0, c1) in enumerate(BLOCKS):
            W = c1 - c0
            mr = creg(R_mix, c0, c1)
            gate = gates[bi % 2]
            R_gate = R_gates[bi % 2]
            for to in range(4):
                pi = bank()
                for kt in range(4):
                    mm(PB(pi, W), w_glu_sb[:, kt, 128 * to:128 * to + 128], mixT[:, kt, c0:c1], kt == 0, kt == 3,
                       [R_wglu] + mr, [PR[pi]])
                act(gate[:, to, 0:W], PB(pi, W), AF.Sigmoid, [PR[pi], R_vec], [R_gate], bias=bgT[:, to:to + 1])
            tt("dve", mixT[:, 0:4, c0:c1], mixT[:, 0:4, c0:c1], gate[:, :, 0:W], ALU.mult, [R_gate] + mr, mr)

        pi = bank()
        for ri in range(2):
            tr(PB(pi)[0:16, 128 * ri:128 * ri + 128], sfin[:, ri, :], identf, [R_sfin, R_cst], [PR[pi]], signal=(ri == 1))
        so = A.alloc([16, 256], F32)
        R_so = Reg()
        cp("act", so[:], PB(pi)[0:16, 0:256], [PR[pi]], [R_so])
        for ri in range(2):
            S.dma("sp", o_s5p[ri], so[0:16, 128 * ri:128 * ri + 128], reads=[R_so])
        sos = s5tok
        R_sos = R_s5tok
        for ri in range(2):
            for qd_ in range(4):
                pi = bank()
                for j in range(4):
                    pair = 4 * qd_ + j
                    tr(PB(pi)[0:16, 128 * j:128 * j + 128], s1s[:, pair, ri, :], identf, [R_s1s, R_cst], [PR[pi]],
                       signal=(j == 3))
                cp("act", sos[0:16, ri, 512 * qd_:512 * qd_ + 512], PB(pi)[0:16, :], [PR[pi]], [R_sos])
        S.dma("sp", o_s5s.rearrange("r n c -> n r c"), sos[:], reads=[R_sos])

        S.barrier()
        A.l = mark_G

        if K_STOP == 'P2b':
            S.finish()
            return nc
        h1 = A.alloc([128, NTILES, D], F32)
        R_h1 = [Reg() for _ in range(NTILES)]
        hn2T = A.alloc([128, 8, NT], BF16)
        R_hn2 = [Reg() for _ in range(NTILES)]
        junk = A.alloc([128, D], BF16)
        ssb = [A.alloc([128, 2], F32) for _ in range(4)]
        R_junk = Reg()
        R_ss = [Reg() for _ in range(4)]
        wup_off = [A.l, A.l + 8192]
        wupb = [A.alloc([128, 8, 2, 256], BF16) for _ in range(2)]
        R_wup = [Reg() for _ in range(2)]
        wdnb = [nc.alloc_sbuf_tensor_at("wdn%d" % i, [128, NJ, 128], BF16, offset=wup_off[i]) for i in range(2)]
        R_wdn = R_wup
        w_up_v = w_up.rearrange("(kt p) c -> p kt c", p=128)

        def wup_load(j):
            wb_ = (j // 2) % 2
            S.dma("pool", wupb[wb_][:, :, 0, :], w_up_v[:, :, 128 * j:128 * j + 256], writes=[R_wup[wb_]])
            S.dma("pool", wupb[wb_][:, :, 1, :], w_up_v[:, :, DFF + 128 * j:DFF + 128 * j + 256],
                  writes=[R_wup[wb_]])
        mark_p3 = A.l
        w_out_sb = A.alloc([128, 8, D], BF16)
        R_woutk = [Reg() for _ in range(4)]
        w_out_v = w_out.rearrange("(kt p) c -> p kt c", p=128)
        for g_ in range(4):
            S.dma("pool", w_out_sb[:, 2 * g_:2 * g_ + 2, :], w_out_v[:, 2 * g_:2 * g_ + 2, :], writes=[R_woutk[g_]])
        xt = [A.alloc([128, D], F32) for _ in range(4)]
        xnb = [A.alloc([128, D], BF16) for _ in range(4)]
        R_xt = [Reg() for _ in range(4)]
        R_xnb = [Reg() for _ in range(4)]
        gffnT = V("gffn")
        def p3_A(ti):
            c0, c1 = tile_cols(ti)
            R = c1 - c0
            b = ti % 4
            S.dma("sp", xt[b][0:R, :], xtok[c0:c1, :], writes=[R_xt[b]])
            p2 = bank2()
            for hf in range(2):
                for kt in range(8):
                    mm(PB(p2 + hf)[0:R, :], mixT[:, kt, c0:c1], w_out_sb[:, kt, 512 * hf:512 * hf + 512], kt == 0, kt == 7,
                       [R_mix[ti], R_woutk[kt // 2]], [PR[p2 + hf]])
            hh = h1[0:R, ti, :]
            tt("dve", hh, PB(p2, n=2)[0:R, :], xt[b][0:R, :], ALU.add, [PR[p2], PR[p2 + 1], R_xt[b]], [R_h1[ti]])
            act(junk[0:R, :], hh, AF.Square, [R_h1[ti]], [R_junk, R_ss[b]], accum_out=ssb[b][0:R, 0:1])
            act(ssb[b][0:R, 1:2], ssb[b][0:R, 0:1], AF.Sqrt, [R_ss[b]], [R_ss[b]], scale=1.0 / D, bias=EPS)
            recip(ssb[b][0:R, 1:2], ssb[b][0:R, 1:2], [R_ss[b]], [R_ss[b]])
            ts("pool", xnb[b][0:R, :], hh, ssb[b][0:R, 1:2], 1.0, ALU.mult, ALU.mult, [R_h1[ti], R_ss[b]], [R_xnb[b]])

        def p3_B(ti):
            c0, c1 = tile_cols(ti)
            R = c1 - c0
            b = ti % 4
            pi = bank()
            pv = PBb(pi).rearrange("p (k r) -> p k r", r=128)
            for kt in range(8):
                tr(pv[:, kt, 0:R], xnb[b][0:R, kt * 128:(kt + 1) * 128], identb[0:R, 0:R],
                   [R_xnb[b], R_cb], [PR[pi]], signal=(kt == 7))
            tt("dve", hn2T[:, :, c0:c1], pv[:, :, 0:R], bc(gffnT.unsqueeze(2), [128, 8, R]), ALU.mult,
               [PR[pi], R_vec], [R_hn2[ti]])

        p3_A(0)
        p3_A(1)
        for ti in range(2, NTILES):
            p3_A(ti)
            p3_B(ti - 2)
        p3_B(NTILES - 2)
        p3_B(NTILES - 1)
        wup_load(0)
        wup_load(2)
        S.barrier()
        A.l = mark_p3
        A.r = SB_HI

        if K_STOP == 'P3':
            S.finish()
            return nc
        HWM = 1056
        mT_off = A.l
        mT = A.alloc([128, NJ, HWM], BF16)
        R_mT = Reg()
        cbuf = nc.alloc_sbuf_tensor_at("cbuf", [32, DFF], F32, offset=mT_off)
        arow = [A.alloc([128, 2 + HWM], F32) for _ in range(2)]
        crow = [A.alloc([128, 1040], F32) for _ in range(2)]
        vrow = [A.alloc([128, HWM], BF16) for _ in range(2)]
        R_ar = [Reg(), Reg()]
        R_cr = [Reg(), Reg()]
        R_vr = [Reg(), Reg()]
        carry = A.alloc([128, NJ, 2], F32)
        convp = A.alloc([128, 2, NJ], F32)
        aS = A.alloc([128, NJ, 16], F32)
        vS = A.alloc([128, NJ, 16], BF16)
        R_carry = Reg(); R_convp = Reg(); R_aS = Reg()
        yts = [A.alloc([128, 512], F32) for _ in range(2)]
        R_yt = [Reg(), Reg()]
        gfbc = A.alloc([128, D], F32)
        R_gf = Reg()
        S.dma("sp", gfbc[:], gfinal.partition_broadcast(128), writes=[R_gf])
        cwT = V("convw")
        cbT = V("convb")
        S.dma("sp", cbuf[:], cvs.rearrange("n r c -> (n r) c"), writes=[R_mT])
        bufT = A.alloc([128, NJ, 32], F32)
        R_bufT = Reg()
        for grp in range(2):
            pi = bank2()
            for jj in range(11):
                j = grp * 11 + jj
                tr(PB(pi, n=2)[:, 32 * jj:32 * jj + 32], cbuf[0:32, 128 * j:128 * j + 128], identf[0:32, 0:32],
                   [R_mT, R_cst], [PR[pi], PR[pi + 1]], signal=(jj == 10))
            cp("act", bufT[:, 11 * grp:11 * grp + 11, :].rearrange("p j c -> p (j c)"), PB(pi, n=2)[:, 0:352],
               [PR[pi], PR[pi + 1]], [R_bufT])
        S.dma("sp", o_cvs[:, 0, :], cvs[:, 1, :])

        w_dn_v = w_down.rearrange("(j p) c -> p j c", p=128)

        def final_norm(ti):
            c0, c1 = tile_cols(ti)
            R = c1 - c0
            b = ti % 2
            hh = h1[0:R, ti, :]
            act(junk[0:R, :], hh, AF.Square, [R_h1[ti]], [R_junk, R_ss[b]], accum_out=ssb[b][0:R, 0:1])
            act(ssb[b][0:R, 1:2], ssb[b][0:R, 0:1], AF.Sqrt, [R_ss[b]], [R_ss[b]], scale=1.0 / D, bias=EPS)
            recip(ssb[b][0:R, 1:2], ssb[b][0:R, 1:2], [R_ss[b]], [R_ss[b]])
            stt("dve", hh, hh, ssb[b][0:R, 1:2], gfbc[0:R, :], ALU.mult, ALU.mult,
                [R_h1[ti], R_ss[b], R_gf], [R_h1[ti]])
            S.dma("sp", y[c0:c1, :], hh, reads=[R_h1[ti]])

        for half in range(2):
            if half == 0:
                segs = [(0, 2048, 32), (16, 0, 512), (528, 512, 512)]
                nconv = 1040
            else:
                segs = [(0, 1024, 512), (512, 1536, 512)]
                nconv = 1024
            if half == 0:
                for b_ in range(2):
                    mset("pool", arow[b_][:, 0:2], 0.0, [R_ar[b_]])
            for j in range(NJ):
                wb = (j // 2) % 2
                jj = j % 2
                b = j % 2
                if jj == 0 and not (half == 0 and j in (0, 2)):
                    wup_load(j)
                for (m0, g0, W) in segs:
                    hr = creg(R_hn2, g0, g0 + W)
                    pa = bank()
                    for kt in range(8):
                        mm(PB(pa, W), wupb[wb][:, kt, 0, 128 * jj:128 * jj + 128], hn2T[:, kt, g0:g0 + W], kt == 0, kt == 7,
                           [R_wup[wb]] + hr, [PR[pa]])
                    pv_ = bank()
                    for kt in range(8):
                        mm(PB(pv_, W), wupb[wb][:, kt, 1, 128 * jj:128 * jj + 128], hn2T[:, kt, g0:g0 + W], kt == 0, kt == 7,
                           [R_wup[wb]] + hr, [PR[pv_]])
                    if g0 == 2048:
                        cp("act", arow[b][:, 2:18], PB(pa)[:, 0:16], [PR[pa]], [R_ar[b]])
                        cp("act", arow[b][:, 2 + 1040:2 + 1056], PB(pa)[:, 16:32], [PR[pa]], [R_ar[b]])
                        cp("act", vrow[b][:, 0:16], PB(pv_)[:, 0:16], [PR[pv_]], [R_vr[b]])
                        cp("act", vrow[b][:, 1040:1056], PB(pv_)[:, 16:32], [PR[pv_]], [R_vr[b]])
                        continue
                    cp("act", arow[b][:, 2 + m0:2 + m0 + W], PB(pa, W), [PR[pa]], [R_ar[b]])
                    cp("act", vrow[b][:, m0:m0 + W], PB(pv_, W), [PR[pv_]], [R_vr[b]])
                if half == 1:
                    cp("act", arow[b][:, 0:2], carry[:, j, :], [R_carry], [R_ar[b]])
                w0 = cwT[:, 0 * NJ + j:0 * NJ + j + 1]
                w1_ = cwT[:, 1 * NJ + j:1 * NJ + j + 1]
                w2_ = cwT[:, 2 * NJ + j:2 * NJ + j + 1]
                cc = crow[b][:, 0:nconv]
                ts("dve", cc, arow[b][:, 2:2 + nconv], w2_, cbT[:, j:j + 1], ALU.mult, ALU.add, [R_ar[b], R_vec], [R_cr[b]])
                stt("dve", cc, arow[b][:, 1:1 + nconv], w1_, cc, ALU.mult, ALU.add, [R_ar[b], R_vec, R_cr[b]], [R_cr[b]])
                stt("dve", cc, arow[b][:, 0:nconv], w0, cc, ALU.mult, ALU.add, [R_ar[b], R_vec, R_cr[b]], [R_cr[b]])
                act(cc, cc, AF.Silu, [R_cr[b]], [R_cr[b]])
                if half == 0:
                    tt("dve", mT[:, j, 0:16], crow[b][:, 0:16], vrow[b][:, 0:16], ALU.mult, [R_cr[b], R_vr[b]], [R_mT])
                    tt("dve", mT[:, j, 32:1056], crow[b][:, 16:1040], vrow[b][:, 16:1040], ALU.mult,
                       [R_cr[b], R_vr[b]], [R_mT])
                    cp("act", carry[:, j, :], arow[b][:, 2 + 1038:2 + 1040], [R_ar[b]], [R_carry])
                    cp("act", aS[:, j, :], arow[b][:, 2 + 1040:2 + 1056], [R_ar[b]], [R_aS])
                    cp("act", vS[:, j, :], vrow[b][:, 1040:1056], [R_vr[b]], [R_aS])
                else:
                    tt("dve", mT[:, j, 0:1024], cc, vrow[b][:, 0:1024], ALU.mult, [R_cr[b], R_vr[b]], [R_mT])
                    cp("act", convp[:, :, j], arow[b][:, 2 + 1022:2 + 1024], [R_ar[b]], [R_convp])
            if half == 0:
                cs1 = crow[0][:, 0:352].rearrange("p (j n) -> p j n", n=16)
                cs2 = crow[1][:, 0:352].rearrange("p (j n) -> p j n", n=16)
                bufv = bufT[:].rearrange("p j (n r) -> p j n r", r=2)

                def wbc(r):
                    return bc(cwT[:, r * NJ:(r + 1) * NJ].unsqueeze(2), [128, NJ, 16])
                rs_ = [R_aS, R_bufT, R_vec, R_cr[0], R_cr[1]]
                ws_ = [R_cr[0], R_cr[1]]
                tt("dve", cs1, aS[:], wbc(2), ALU.mult, rs_, ws_)
                tt("dve", cs1, cs1, bc(cbT.unsqueeze(2), [128, NJ, 16]), ALU.add, rs_, ws_)
                tt("dve", cs2, bufv[:, :, :, 1], wbc(1), ALU.mult, rs_, ws_)
                tt("dve", cs1, cs1, cs2, ALU.add, rs_, ws_)
                tt("dve", cs2, bufv[:, :, :, 0], wbc(0), ALU.mult, rs_, ws_)
                tt("dve", cs1, cs1, cs2, ALU.add, rs_, ws_)
                act(cs1, cs1, AF.Silu, rs_, ws_)
                tt("dve", mT[:, :, 16:32], cs1, vS[:], ALU.mult, rs_, [R_mT])
            if half == 0:
                groups = [(32, 512, [0, 1, 2, 3]), (544, 512, [4, 5, 6, 7]), (0, 32, [16])]
            else:
                groups = [(0, 512, [8, 9, 10, 11]), (512, 512, [12, 13, 14, 15])]
            def pb_mm(ft, gi_):
                m0, Wt, tiles = groups[gi_]
                wb = ft % 2
                if gi_ == 0:
                    S.dma("pool", wdnb[wb][:], w_dn_v[:, :, 128 * ft:128 * ft + 128], writes=[R_wdn[wb]])
                pi = bank()
                yb = (ft * len(groups) + gi_) % 2
                for j in range(NJ):
                    mm(PB(pi, Wt), wdnb[wb][:, j, :], mT[:, j, m0:m0 + Wt], j == 0, j == NJ - 1,
                       [R_wdn[wb], R_mT], [PR[pi]])
                cp("act", yts[yb][:, 0:Wt], PB(pi, Wt), [PR[pi]], [R_yt[yb]])

            def pb_tr(ft, gi_):
                m0, Wt, tiles = groups[gi_]
                yb = (ft * len(groups) + gi_) % 2
                pt = bank()
                for n_, ti in enumerate(tiles):
                    R = 32 if ti == 16 else 128
                    tr(PB(pt)[0:R, 128 * n_:128 * n_ + 128], yts[yb][:, 128 * n_:128 * n_ + R], identf,
                       [R_yt[yb], R_cst], [PR[pt]], signal=(n_ == len(tiles) - 1))
                for n_, ti in enumerate(tiles):
                    R = 32 if ti == 16 else 128
                    hs = h1[0:R, ti, 128 * ft:128 * ft + 128]
                    tt("dve", hs, hs, PB(pt)[0:R, 128 * n_:128 * n_ + 128], ALU.add, [R_h1[ti], PR[pt]], [R_h1[ti]])

            steps = [(ft, gi_) for ft in range(8) for gi_ in range(len(groups))]
            pb_mm(*steps[0])
            for i_ in range(1, len(steps)):
                pb_mm(*steps[i_])
                pb_tr(*steps[i_ - 1])
            pb_tr(*steps[-1])
            for (m0, Wt, tiles) in groups:
                for ti in tiles:
                    final_norm(ti)

        pi = bank()
        tr(PB(pi)[0:44, 0:128], convp[:].rearrange("p r j -> p (r j)"), identf, [R_convp, R_cst], [PR[pi]])
        cvo = A.alloc([44, 128], F32)
        R_cvo = Reg()
        cp("act", cvo[:], PB(pi)[0:44, 0:128], [PR[pi]], [R_cvo])
        for r in range(2):
            S.dma("sp", o_cvp[r].rearrange("(j f) -> j f", f=128), cvo[22 * r:22 * r + 22, :], reads=[R_cvo])
        for grp in range(6):
            pi = bank()
            js = list(range(4 * grp, min(4 * grp + 4, NJ)))
            for n_, j in enumerate(js):
                tr(PB(pi)[0:16, 128 * n_:128 * n_ + 128], aS[:, j, :], identf, [R_aS, R_cst], [PR[pi]],
                   signal=(n_ == len(js) - 1))
            cp("act", cbuf[0:16, 512 * grp:512 * grp + 128 * len(js)], PB(pi)[0:16, 0:128 * len(js)], [PR[pi]], [R_mT])
        S.dma("sp", o_cvs[:, 1, :], cbuf[0:16, :], reads=[R_mT])

        S.finish()
        print("built: ops", S.nops, "waits", S.nwaits, "sbuf peak", A.peak - SB_LO, "/", SB_HI - SB_LO)
    return nc


_CACHE = {}


def kernel(**inp):
    f = lambda k: np.asarray(inp[k], np.float32)
    x_prompt = f("x_prompt")
    x_sample = f("x_sample")
    meta = f("meta_tokens")
    vecs = make_vecs(inp)
    nvec = vecs.shape[1]
    if "nc" not in _CACHE:
        _CACHE["nc"] = build_nc(nvec)
    nc = _CACHE["nc"]
    lam = np.stack([f("s5_lambda_re")[0].reshape(16, 128), f("s5_lambda_im")[0].reshape(16, 128),
                    np.repeat(f("s5_log_dt")[0].reshape(16, 2), 64, axis=1)], axis=0)
    bst = np.stack([f("s5_b_re")[0].reshape(16, 128, 16), f("s5_b_im")[0].reshape(16, 128, 16)], axis=0)
    cch = np.stack([f("s5_c_re")[0].reshape(512, 64), f("s5_c_im")[0].reshape(512, 64)], axis=0)
    shared = {
        "w_in": f("w_in")[0], "w_out": f("w_out")[0], "w_up": f("ffn_w_up")[0], "w_down": f("ffn_w_down")[0],
        "w_glu": f("s5_w_glu")[0], "lamPP": np.ascontiguousarray(lam), "bst": np.ascontiguousarray(bst),
        "cch": np.ascontiguousarray(cch), "vecs": vecs, "hlbrows": f("hg_lower_bounds"),
        "gfinal": f("final_norm_g"), "consts": CONSTS,
    }
    in_maps = []
    for c in range(NCORES):
        sl = slice(16 * c, 16 * c + 16)
        m = dict(shared)
        m["xtok"] = np.ascontiguousarray(np.concatenate([x_prompt[c], meta, x_sample[sl, 0, :]], axis=0))
        m["s5s"] = np.ascontiguousarray(np.stack([f("state_s5_re")[0, sl].reshape(16, 2048),
                                                  f("state_s5_im")[0, sl].reshape(16, 2048)], axis=0))
        m["hgs"] = np.ascontiguousarray(f("state_hgrn")[0, sl])
        m["cvs"] = np.ascontiguousarray(f("state_ffn_conv")[0, sl])
        in_maps.append(m)
    res = run_bass_kernel_spmd(nc, in_maps, core_ids=list(range(NCORES)))
    rs = res.results
    y_prompt = np.stack([rs[c]["y"][0:2048] for c in range(NCORES)], axis=0)
    y_sample = np.concatenate([rs[c]["y"][2064:2080] for c in range(NCORES)], axis=0)[:, None, :]
    s5p_re = np.stack([rs[c]["o_s5p"][0].reshape(32, 64) for c in range(NCORES)], axis=0)[None]
    s5p_im = np.stack([rs[c]["o_s5p"][1].reshape(32, 64) for c in range(NCORES)], axis=0)[None]
    hgp = np.stack([rs[c]["o_hgp"] for c in range(NCORES)], axis=0)[None]
    cvp = np.stack([rs[c]["o_cvp"] for c in range(NCORES)], axis=0)[None]
    s5s_re = np.concatenate([rs[c]["o_s5s"][0].reshape(16, 32, 64) for c in range(NCORES)], axis=0)[None]
    s5s_im = np.concatenate([rs[c]["o_s5s"][1].reshape(16, 32, 64) for c in range(NCORES)], axis=0)[None]
    hgs = np.concatenate([rs[c]["o_hgs"] for c in range(NCORES)], axis=0)[None]
    cvs = np.concatenate([rs[c]["o_cvs"] for c in range(NCORES)], axis=0)[None]
    out = (y_prompt, y_sample, s5p_re, s5p_im, hgp, cvp, s5s_re, s5s_im, hgs, cvs)
    return tuple(np.ascontiguousarray(o, dtype=np.float32) for o in out)
```

```python
import math
K_STOP = ''
import numpy as np
import concourse.bass as bass
import concourse.mybir as mybir
from concourse.bass_utils import run_bass_kernel_spmd
from contextlib import ExitStack

F32 = mybir.dt.float32
BF16 = mybir.dt.bfloat16
AF = mybir.ActivationFunctionType
ALU = mybir.AluOpType
NDS = 48
NCORES = 8
D = 1024
NT = 2080
NTILES = 17
DFF = 2816
NJ = 22
EPS = 1e-6
SB_LO = 16512
SB_HI = 229344
SE = "pool"


class _Stop(Exception):
    pass


class Reg:
    __slots__ = ("w", "r")

    def __init__(self):
        self.w = None
        self.r = {}


class Sched:
    def __init__(self, nc, stack):
        self.nc = nc
        self.eng = {"pe": nc.tensor, "act": nc.scalar, "dve": nc.vector,
                    "pool": nc.gpsimd, "sp": nc.sync}
        self.sem = {k: stack.enter_context(nc.semaphore("s_" + k)) for k in self.eng}
        self.cnt = {k: 0 for k in self.eng}
        self.waited = {k: {} for k in self.eng}
        self.dsem = [stack.enter_context(nc.semaphore("d%d" % i)) for i in range(NDS)]
        self.dcnt = [0] * NDS
        self.dpool = {"sp": list(range(0, 24)), "pool": list(range(24, 40)), "act": list(range(40, NDS))}
        self.dnext = {"sp": 0, "pool": 0, "act": 0}
        self.pending = []
        self.nwaits = 0
        self.nops = 0

    def _wait(self, e, tok):
        sem, val, _ = tok
        assert val is not None, "dependency on unsignaled PE op"
        key = id(sem)
        if self.waited[e].get(key, 0) >= val:
            return
        self.waited[e][key] = val
        self.eng[e].wait_ge(sem, val)
        self.nwaits += 1

    def _deps(self, e, reads, writes):
        for R in reads:
            t = R.w
            if t is not None and not (t[2] == e and e == "pe"):
                self._wait(e, t)
        for R in writes:
            t = R.w
            if t is not None and not (t[2] == e and e == "pe"):
                self._wait(e, t)
            for t in R.r.values():
                if not (t[2] == e and e == "pe"):
                    self._wait(e, t)

    def _register(self, tok, reads, writes):
        for R in reads:
            R.r[id(tok[0])] = tok
        for R in writes:
            R.w = tok
            R.r = {}

    def op(self, e, fn, reads=(), writes=(), signal=True):
        self._deps(e, reads, writes)
        ins = fn(self.eng[e])
        self.nops += 1
        if signal:
            self.cnt[e] += 1
            ins.then_inc(self.sem[e], 1)
            tok = (self.sem[e], self.cnt[e], e)
            if e == "pe":
                for p in self.pending:
                    p[1] = self.cnt[e]
                self.pending = []
        else:
            assert e == "pe"
            tok = [self.sem[e], None, e]
            self.pending.append(tok)
        self._register(tok, reads, writes)
        return tok

    def dma(self, e, out, in_, reads=(), writes=(), **kw):
        pool_ = self.dpool[e]
        k = pool_[self.dnext[e]]
        self.dnext[e] = (self.dnext[e] + 1) % len(pool_)
        if self.dcnt[k] > 0:
            self._wait(e, (self.dsem[k], self.dcnt[k], None))
        self._deps(e, reads, writes)
        ins = self.eng[e].dma_start(out=out, in_=in_, **kw)
        self.dcnt[k] += 16
        ins.then_inc(self.dsem[k], 16)
        tok = (self.dsem[k], self.dcnt[k], None)
        self._register(tok, reads, writes)
        self.nops += 1
        return tok

    def barrier(self):
        for e in self.eng:
            for x in self.eng:
                if x != e and self.cnt[x] > 0:
                    self._wait(e, (self.sem[x], self.cnt[x], x))
            for k in range(NDS):
                if self.dcnt[k] > 0:
                    self._wait(e, (self.dsem[k], self.dcnt[k], None))

    def finish(self):
        e = "sp"
        for k in range(NDS):
            if self.dcnt[k] > 0:
                self._wait(e, (self.dsem[k], self.dcnt[k], None))
        for x in self.eng:
            if x != e and self.cnt[x] > 0:
                self._wait(e, (self.sem[x], self.cnt[x], x))


class Arena:
    def __init__(self, nc):
        self.nc = nc
        self.l = SB_LO
        self.r = SB_HI
        self.n = 0
        self.peak = 0

    def alloc(self, shape, dt, right=False):
        sz = 4 if dt == F32 else 2
        nb = sz
        for s in shape[1:]:
            nb *= s
        nb = (nb + 63) // 64 * 64
        if right:
            self.r -= nb
            off = self.r
        else:
            off = self.l
            self.l += nb
        assert self.l <= self.r, ("SBUF overflow", self.l, self.r)
        self.peak = max(self.peak, self.l + (SB_HI - self.r))
        self.n += 1
        return self.nc.alloc_sbuf_tensor_at("t%d" % self.n, list(shape), dt, offset=off)


def tile_cols(ti):
    return (2048, 2080) if ti == 16 else (128 * ti, 128 * ti + 128)


BLOCKS = [(0, 512), (512, 1024), (1024, 1536), (1536, 2048), (2048, 2080)]


CONST_OFF = {}


def make_consts():
    cols = []
    off = 0

    def add(name, arr):
        nonlocal off
        a = np.zeros((128, arr.shape[1]), np.float32)
        a[:arr.shape[0]] = arr
        CONST_OFF[name] = (off, arr.shape[1])
        cols.append(a)
        off += arr.shape[1]

    add("ident", np.eye(128, dtype=np.float32))
    s = np.arange(128)[:, None] % 64
    c = np.arange(64)[None, :]
    add("attmask", (s <= c).astype(np.float32))
    s = np.arange(128)[:, None]
    c = np.arange(128)[None, :]
    add("mrev", ((s > c) & (s // 64 == c // 64)).astype(np.float32))
    s = np.arange(32)[:, None]
    c = np.arange(32)[None, :]
    add("msmall", ((s > c) & (s < 16) & (c < 16)).astype(np.float32))
    p = np.arange(128)[:, None]
    col = np.arange(128)[None, :]
    add("blockmask", (((col // 16) % 2) == (p // 64)).astype(np.float32))
    col = np.arange(32)[None, :]
    add("blockident", ((p % 32) == col).astype(np.float32))
    n = np.arange(16)[None, :]
    add("onehot", ((p - 16) == n).astype(np.float32))
    col = np.arange(128)[None, :]
    add("qmask", ((col // 32) == (p // 32)).astype(np.float32))
    add("ones", np.ones((128, 128), np.float32))
    return np.concatenate(cols, axis=1)


CONSTS = make_consts()
NCONST = CONSTS.shape[1]

VEC_OFF = {}


def make_vecs(inp):
    cols = []
    off = 0

    def add(name, arr):
        nonlocal off
        VEC_OFF[name] = (off, arr.shape[1])
        cols.append(np.ascontiguousarray(arr, dtype=np.float32))
        off += arr.shape[1]

    def fm(v):
        return np.asarray(v, np.float32).reshape(-1, 128).T

    add("gmix", fm(inp["norm_mix_g"][0]))
    add("gffn", fm(inp["norm_ffn_g"][0]))
    add("hlb0", fm(inp["hg_lower_bounds"][0]))
    add("hlb1", fm(inp["hg_lower_bounds"][1]))
    add("s5d", fm(inp["s5_d"][0]))
    add("bglu", fm(inp["s5_b_glu"][0]))
    add("hgng", fm(inp["hg_norm_g"][0]))
    cw = np.asarray(inp["ffn_conv_w"][0], np.float32)
    add("convw", np.concatenate([fm(cw[r]) for r in range(3)], axis=1))
    add("convb", fm(inp["ffn_conv_b"][0]))
    return np.concatenate(cols, axis=1)


def build_nc(nvec):
    nc = bass.Bass("TRN2", target_bir_lowering=False)

    def din(name, shape):
        return nc.dram_tensor(name, list(shape), F32, kind="ExternalInput").ap()

    def dout(name, shape):
        return nc.dram_tensor(name, list(shape), F32, kind="ExternalOutput").ap()

    xtok = din("xtok", [NT, D])
    w_in = din("w_in", [D, 2560])
    w_out = din("w_out", [D, D])
    w_up = din("w_up", [D, 2 * DFF])
    w_down = din("w_down", [DFF, D])
    w_glu = din("w_glu", [512, 512])
    lamPP = din("lamPP", [3, 16, 128])
    bst = din("bst", [2, 16, 128, 16])
    cch = din("cch", [2, 512, 64])
    vecs = din("vecs", [128, nvec])
    hlbrows = din("hlbrows", [2, 512])
    gfinal = din("gfinal", [D])
    s5s = din("s5s", [2, 16, 2048])
    hgs = din("hgs", [16, 4, 128, 128])
    cvs = din("cvs", [16, 2, DFF])
    consts = din("consts", [128, NCONST])

    y = dout("y", [NT, D])
    o_s5p = dout("o_s5p", [2, 16, 128])
    o_hgp = dout("o_hgp", [4, 128, 128])
    o_cvp = dout("o_cvp", [2, DFF])
    o_s5s = dout("o_s5s", [2, 16, 2048])
    o_hgs = dout("o_hgs", [16, 4, 128, 128])
    o_cvs = dout("o_cvs", [16, 2, DFF])

    with ExitStack() as st:
        S = Sched(nc, st)
        A = Arena(nc)
        PS = nc.alloc_psum_tensor("ps", [128, 4096], F32)
        PR = [Reg() for _ in range(8)]
        pb_state = {"i": 0}

        reserved = set()

        def bank():
            i = pb_state["i"]
            while i in reserved:
                i = (i + 1) % 8
            pb_state["i"] = (i + 1) % 8
            return i

        def bank2():
            i = pb_state["i"]
            if i % 2:
                i = (i + 1) % 8
            while i in reserved or (i + 1) in reserved:
                i = (i + 2) % 8
            pb_state["i"] = (i + 2) % 8
            return i

        def PB(i, w=512, n=1):
            return PS[:, 512 * i:512 * i + w] if n == 1 else PS[:, 512 * i:512 * (i + n)]

        def PBb(i):
            return PS[:, 512 * i:512 * i + 512].bitcast(BF16)

        def mm(out, lhsT, rhs, start, stop, reads, writes, tp=None, signal=None):
            sig = stop if signal is None else signal
            kw = {}
            if tp is not None:
                kw["tile_position"] = tp
            return S.op("pe", lambda e: e.matmul(out, lhsT=lhsT, rhs=rhs, start=start, stop=stop,
                                                 skip_group_check=True, **kw),
                        reads=reads, writes=writes, signal=sig)

        def tr(out, in_, ident, reads, writes, signal=True):
            return S.op("pe", lambda e: e.transpose(out=out, in_=in_, identity=ident),
                        reads=reads, writes=writes, signal=signal)

        def act(out, in_, func, reads, writes, **kw):
            return S.op("act", lambda e: e.activation(out=out, in_=in_, func=func, **kw),
                        reads=reads, writes=writes)

        def tt(eng, out, in0, in1, op, reads, writes):
            return S.op(eng, lambda e: e.tensor_tensor(out=out, in0=in0, in1=in1, op=op),
                        reads=reads, writes=writes)

        def ts(eng, out, in0, s1, s2, op0, op1, reads, writes):
            if op1 is None:
                return S.op(eng, lambda e: e.tensor_scalar(out=out, in0=in0, scalar1=s1, scalar2=None, op0=op0),
                            reads=reads, writes=writes)
            return S.op(eng, lambda e: e.tensor_scalar(out=out, in0=in0, scalar1=s1, scalar2=s2, op0=op0, op1=op1),
                        reads=reads, writes=writes)

        def stt(eng, out, in0, scalar, in1, op0, op1, reads, writes):
            return S.op(eng, lambda e: e.scalar_tensor_tensor(out=out, in0=in0, scalar=scalar, in1=in1,
                                                              op0=op0, op1=op1),
                        reads=reads, writes=writes)

        def cp(eng, out, in_, reads, writes):
            if eng == "act":
                return S.op("act", lambda e: e.copy(out=out, in_=in_), reads=reads, writes=writes)
            return S.op(eng, lambda e: e.tensor_copy(out=out, in_=in_), reads=reads, writes=writes)

        def recip(out, in_, reads, writes):
            return S.op("dve", lambda e: e.reciprocal(out=out, in_=in_), reads=reads, writes=writes)

        def mset(eng, ap, val, writes):
            return S.op(eng, lambda e: e.memset(ap, val), writes=writes)

        def bc(ap, shape):
            return ap.to_broadcast(list(shape))

        cst = A.alloc([128, NCONST], F32)
        R_cst = Reg()
        S.dma("sp", cst[:], consts, writes=[R_cst])
        vec = A.alloc([128, nvec], F32)
        R_vec = Reg()
        S.dma("sp", vec[:], vecs, writes=[R_vec])

        def C(name, rows=128):
            o, w = CONST_OFF[name]
            return cst[0:rows, o:o + w]

        def V(name):
            o, w = VEC_OFF[name]
            return vec[:, o:o + w]

        identb = A.alloc([128, 128], BF16)
        onesb = A.alloc([128, 128], BF16)
        mrevb = A.alloc([128, 128], BF16)
        msmallb = A.alloc([32, 32], BF16)
        R_cb = Reg()
        cp("dve", identb[:], C("ident"), [R_cst], [R_cb])
        cp("dve", onesb[:], C("ones"), [R_cst], [R_cb])
        cp("dve", mrevb[:], C("mrev"), [R_cst], [R_cb])
        cp("dve", msmallb[:], C("msmall", 32), [R_cst], [R_cb])
        identf = C("ident")

        lbT = A.alloc([128, 4], F32)
        omlT = A.alloc([128, 4], F32)
        R_lb = Reg()
        tt("dve", lbT[:], V("hlb1"), V("hlb0"), ALU.subtract, [R_vec], [R_lb])
        act(lbT[:], lbT[:], AF.Exp, [R_lb], [R_lb])
        ts("dve", lbT[:], lbT[:], 1.0, None, ALU.add, None, [R_lb], [R_lb])
        recip(lbT[:], lbT[:], [R_lb], [R_lb])
        ts("dve", omlT[:], lbT[:], -1.0, 1.0, ALU.mult, ALU.add, [R_lb], [R_lb])

        mixT = A.alloc([128, 8, NT], BF16, right=True)
        R_mix = [Reg() for _ in range(NTILES)]

        def creg(regs, c0, c1):
            out = []
            for ti in range(NTILES):
                a, b = tile_cols(ti)
                if a < c1 and c0 < b:
                    out.append(regs[ti])
            return out

        mark_G = A.l

        uT = A.alloc([128, 4, NT], BF16)
        mark_R = A.r
        qT = A.alloc([128, 4, NT], BF16, right=True)
        kT = A.alloc([128, 4, NT], BF16, right=True)
        v_tok = A.alloc([128, NTILES, 512], BF16, right=True)
        kk_tok = A.alloc([128, NTILES, 512], BF16, right=True)
        ebl = A.alloc([128, 4, 33], F32)
        fsamp = A.alloc([128, 4, 16], F32)
        R_u = [Reg() for _ in range(NTILES)]
        R_q = [[Reg() for _ in range(NTILES)] for _ in range(4)]
        R_k = [[Reg() for _ in range(NTILES)] for _ in range(4)]
        R_v = [Reg() for _ in range(NTILES)]
        R_kk = [Reg() for _ in range(NTILES)]
        R_ebl = Reg()
        mark_P1out = A.l

        xnT = A.alloc([128, 8, NT], BF16)
        R_xn = [Reg() for _ in range(NTILES)]
        wtok = A.alloc([128, 8, 512], BF16)
        R_wtok = Reg()
        w_in_v = w_in.rearrange("(kt p) c -> p kt c", p=128)
        S.dma("pool", wtok[:], w_in_v[:, :, 1536:2048], writes=[R_wtok])

        def v_proj(ti):
            c0, c1 = tile_cols(ti)
            R = c1 - c0
            pi = bank()
            for kt in range(8):
                mm(PB(pi)[0:R, :], xnT[:, kt, c0:c1], wtok[:, kt, :], kt == 0, kt == 7,
                   [R_xn[ti], R_wtok], [PR[pi]])
            cp("act", v_tok[0:R, ti, :], PB(pi)[0:R, :], [PR[pi]], [R_v[ti]])

        mark_tmp = A.l
        xt = [A.alloc([128, D], F32) for _ in range(4)]
        xnb = [A.alloc([128, D], BF16) for _ in range(4)]
        junk = A.alloc([128, D], BF16)
        ssb = [A.alloc([128, 2], F32) for _ in range(4)]
        R_xt = [Reg() for _ in range(4)]
        R_xnb = [Reg() for _ in range(4)]
        R_junk = Reg()
        R_ss = [Reg() for _ in range(4)]
        gmixT = V("gmix")
        def p0_A(ti):
            c0, c1 = tile_cols(ti)
            R = c1 - c0
            b = ti % 4
            S.dma("sp", xt[b][0:R, :], xtok[c0:c1, :], writes=[R_xt[b]])
            act(junk[0:R, :], xt[b][0:R, :], AF.Square, [R_xt[b]], [R_junk, R_ss[b]], accum_out=ssb[b][0:R, 0:1])
            act(ssb[b][0:R, 1:2], ssb[b][0:R, 0:1], AF.Sqrt, [R_ss[b]], [R_ss[b]], scale=1.0 / D, bias=EPS)
            recip(ssb[b][0:R, 1:2], ssb[b][0:R, 1:2], [R_ss[b]], [R_ss[b]])
            ts("pool", xnb[b][0:R, :], xt[b][0:R, :], ssb[b][0:R, 1:2], 1.0, ALU.mult, ALU.mult, [R_xt[b], R_ss[b]], [R_xnb[b]])

        def p0_B(ti):
            c0, c1 = tile_cols(ti)
            R = c1 - c0
            b = ti % 4
            pi = bank()
            pv = PBb(pi).rearrange("p (k r) -> p k r", r=128)
            for kt in range(8):
                tr(pv[:, kt, 0:R], xnb[b][0:R, kt * 128:(kt + 1) * 128], identb[0:R, 0:R],
                   [R_xnb[b], R_cb], [PR[pi]], signal=(kt == 7))
            tt("dve", xnT[:, :, c0:c1], pv[:, :, 0:R], bc(gmixT.unsqueeze(2), [128, 8, R]), ALU.mult,
               [PR[pi], R_vec], [R_xn[ti]])

        p0_A(0)
        p0_A(1)
        for ti in range(2, NTILES):
            p0_A(ti)
            p0_B(ti - 2)
            if ti >= 3:
                v_proj(ti - 3)
        p0_B(NTILES - 2)
        v_proj(NTILES - 3)
        p0_B(NTILES - 1)
        v_proj(NTILES - 2)
        v_proj(NTILES - 1)
        S.barrier()
        A.l = mark_tmp

        if K_STOP == 'P0':
            S.finish()
            return nc
        wft = [A.alloc([128, 8, 128], BF16) for _ in range(2)]
        R_wft = [Reg(), Reg()]
        def fm_tile(col0, evac):
            b = fm_tile.n % 2
            fm_tile.n += 1
            S.dma("pool", wft[b][:], w_in_v[:, :, col0:col0 + 128], writes=[R_wft[b]])
            for bi, (c0, c1) in enumerate(BLOCKS):
                W = c1 - c0
                pi = bank()
                for kt in range(8):
                    mm(PB(pi, W), wft[b][:, kt, :], xnT[:, kt, c0:c1], kt == 0, kt == 7,
                       creg(R_xn, c0, c1) + [R_wft[b]], [PR[pi]])
                evac(bi, c0, c1, W, pi)
        fm_tile.n = 0

        def u_tile(t):
            fm_tile(0 + 128 * t, lambda bi, c0, c1, W, pi, t=t:
                    cp("act", uT[:, t, c0:c1], PB(pi, W), [PR[pi]], creg(R_u, c0, c1)))

        def g_tile(h):
            fm_tile(2048 + 128 * h, lambda bi, c0, c1, W, pi, h=h:
                    act(mixT[:, 4 + h, c0:c1], PB(pi, W), AF.Silu, [PR[pi]], creg(R_mix, c0, c1)))

        for h in range(4):
            fm_tile(512 + 128 * h, lambda bi, c0, c1, W, pi, h=h:
                    cp("dve", qT[:, h, c0:c1], PB(pi, W), [PR[pi]], creg(R_q[h], c0, c1)))
        HW_ = 1056
        lgf = [A.alloc([128, HW_], F32) for _ in range(2)]
        bTt = [A.alloc([128, HW_], F32) for _ in range(2)]
        etmp_ = [A.alloc([128, HW_], F32) for _ in range(2)]
        etmp = [etmp_, etmp_]
        smask = A.alloc([128, HW_], BF16)
        R_lgfb = [[Reg(), Reg()], [Reg(), Reg(), Reg()]]
        R_bT = [Reg(), Reg()]
        R_et_ = [Reg(), Reg()]
        R_et = [R_et_, R_et_]
        R_sm = Reg()
        mset("pool", smask[:], 1.0, [R_sm])
        mset("pool", smask[:, 0:1024:64], 0.0, [R_sm])
        mset("pool", smask[:, 1024:1025], 0.0, [R_sm])
        mset("pool", smask[:, 1040:1056], 0.0, [R_sm])

        def f_chains(h):
            ns = (1024, 1056)
            nprs = (1024, 1040)
            for half in range(2):
                n = ns[half]
                rl = R_lgfb[half]
                act(lgf[half][:, 0:n], lgf[half][:, 0:n], AF.Ln, rl, rl)
            for half in range(2):
                n = ns[half]
                S.op("dve", lambda e, half=half, n=n: e.tensor_tensor_scan(
                    out=bTt[half][:, 0:n], data0=smask[:, 0:n], data1=lgf[half][:, 0:n],
                    initial=0.0, op0=ALU.mult, op1=ALU.add),
                     reads=R_lgfb[half] + [R_sm], writes=[R_bT[half]])
            for half in range(2):
                npr, g0 = nprs[half], 1024 * half
                act(etmp_[half][:, 0:npr], bTt[half][:, 0:npr], AF.Exp, [R_bT[half]], [R_et_[half]])
            for half in range(2):
                npr, g0 = nprs[half], 1024 * half
                tt("dve", qT[:, h, g0:g0 + npr], qT[:, h, g0:g0 + npr], etmp_[half][:, 0:npr], ALU.mult,
                   [R_et_[half]] + creg(R_q[h], g0, g0 + npr), creg(R_q[h], g0, g0 + npr))
            for half in range(2):
                npr, g0 = nprs[half], 1024 * half
                act(etmp_[half][:, 0:npr], bTt[half][:, 0:npr], AF.Exp, [R_bT[half]], [R_et_[half]], scale=-1.0)
            for half in range(2):
                npr, g0 = nprs[half], 1024 * half
                tt("dve", kT[:, h, g0:g0 + npr], kT[:, h, g0:g0 + npr], etmp_[half][:, 0:npr], ALU.mult,
                   [R_et_[half]] + creg(R_k[h], g0, g0 + npr), creg(R_k[h], g0, g0 + npr))
            for half in range(2):
                act(ebl[:, h, 1 + 16 * half:17 + 16 * half], bTt[half][:, 63:1024:64], AF.Exp, [R_bT[half]], [R_ebl])
            act(ebl[:, h, 0:1], bTt[1][:, 1039:1040], AF.Exp, [R_bT[1]], [R_ebl])
            act(fsamp[:, h, :], bTt[1][:, 1040:1056], AF.Exp, [R_bT[1]], [R_ebl])

        for h in range(4):
            def evac_f(bi, c0, c1, W, pi, h=h):
                half = 0 if bi < 2 else 1
                l0 = c0 - 1024 * half
                rb = [R_lgfb[half][bi - 2 * half]]
                fl = lgf[half][:, l0:l0 + W]
                act(fl, PB(pi, W), AF.Sigmoid, [PR[pi]], rb)
                ts("dve", fl, fl, omlT[:, h:h + 1], lbT[:, h:h + 1], ALU.mult, ALU.add, rb + [R_lb], rb)
                ts("pool", kT[:, h, c0:c1], fl, -1.0, 1.0, ALU.mult, ALU.add, rb, creg(R_k[h], c0, c1))
            fm_tile(1024 + 128 * h, evac_f)
            f_chains(h)
            u_tile(h)
            g_tile(h)

        kkT = [A.alloc([128, NT], BF16) for _ in range(2)]
        R_kkT = [Reg(), Reg()]
        for h in range(4):
            kb = h % 2
            tt("dve", kkT[kb][:, 0:2048].rearrange("p (c s) -> p c s", s=64),
               kT[:, h, 0:2048].rearrange("p (c s) -> p c s", s=64),
               bc(ebl[:, h, 1:33].unsqueeze(2), [128, 32, 64]), ALU.mult,
               creg(R_k[h], 0, 2048) + [R_ebl], [R_kkT[kb]])
            ts("dve", kkT[kb][:, 2048:2064], kT[:, h, 2048:2064], ebl[:, h, 0:1], None, ALU.mult, None,
               [R_k[h][16], R_ebl], [R_kkT[kb]])
            cp("dve", kkT[kb][:, 2064:2080], kT[:, h, 2064:2080], [R_k[h][16]], [R_kkT[kb]])
            for grp, tiles in enumerate(([0, 1, 2, 3, 4, 5, 6, 7], [8, 9, 10, 11, 12, 13, 14, 15], [16])):
                pi = bank()
                pv = PBb(pi).rearrange("p (j c) -> p j c", c=128)
                for j, ti in enumerate(tiles):
                    c0, c1 = tile_cols(ti)
                    R = c1 - c0
                    tr(pv[0:R, j, :], kkT[kb][:, c0:c1], identb[:, :], [R_kkT[kb], R_cb], [PR[pi]],
                       signal=(j == len(tiles) - 1))
                R = 32 if grp == 2 else 128
                nt_ = len(tiles)
                cp("act", kk_tok[0:R, tiles[0]:tiles[0] + nt_, 128 * h:128 * h + 128], pv[0:R, 0:nt_, :],
                   [PR[pi]], [R_kk[ti_] for ti_ in tiles])

        S.barrier()
        A.l = mark_P1out

        if K_STOP == 'P1':
            S.finish()
            return nc
        def s5t(shape):
            return A.alloc(shape, F32)
        lr = s5t([128, 16]); li = s5t([128, 16]); dtt = s5t([128, 16])
        ar = s5t([128, 16]); ai = s5t([128, 16]); cr = s5t([128, 16]); ci = s5t([128, 16])
        mg16 = s5t([128, 16])
        t1 = s5t([128, 16]); t2 = s5t([128, 16]); t3 = s5t([128, 16]); t4 = s5t([128, 16])
        R_s5 = Reg()
        lpp = A.alloc([16, 3, 128], F32, right=True)
        R_lpp = Reg()
        S.dma("sp", lpp[:], lamPP.rearrange("a r c -> r a c"), writes=[R_lpp])
        pi = bank()
        for a_i in range(3):
            tr(PB(pi)[:, 16 * a_i:16 * a_i + 16], lpp[0:16, a_i, :], identf[0:16, 0:16], [R_lpp, R_cst], [PR[pi]],
               signal=(a_i == 2))
        cp("act", lr[:], PB(pi)[:, 0:16], [PR[pi]], [R_s5])
        cp("act", li[:], PB(pi)[:, 16:32], [PR[pi]], [R_s5])
        negone = s5t([128, 16])
        mset(SE, negone[:], -1.0, [R_s5])
        act(dtt[:], PB(pi)[:, 32:48], AF.Exp, [PR[pi]], [R_s5])
        R_ct = Reg()
        cdup = A.alloc([128, 2, 4, 128], F32, right=True)
        R_cdup = Reg()
        for ri in range(2):
            src = cch[ri].rearrange("(t p) s -> p t s", p=128)
            S.dma("sp", cdup[:, ri, :, 0:64], src, writes=[R_cdup])
            S.dma("sp", cdup[:, ri, :, 64:128], src, writes=[R_cdup])
        ct_re = A.alloc([128, 512], F32)
        nct_im = A.alloc([128, 512], F32)
        bbb = A.alloc([128, 16, 2, 32], BF16)
        for t in range(4):
            pi = bank()
            tr(PB(pi)[:, 0:128], cdup[:, 0, t, :], identf, [R_cdup, R_cst], [PR[pi]], signal=False)
            tr(PB(pi)[:, 128:256], cdup[:, 1, t, :], identf, [R_cdup, R_cst], [PR[pi]])
            cp("act", ct_re[:, 128 * t:128 * t + 128], PB(pi)[:, 0:128], [PR[pi]], [R_ct])
            act(nct_im[:, 128 * t:128 * t + 128], PB(pi)[:, 128:256], AF.Copy, [PR[pi]], [R_ct], scale=-1.0)
            tt(SE, ct_re[:, 128 * t:128 * t + 128], ct_re[:, 128 * t:128 * t + 128], C("blockmask"), ALU.mult,
               [R_ct, R_cst], [R_ct])
            tt(SE, nct_im[:, 128 * t:128 * t + 128], nct_im[:, 128 * t:128 * t + 128], C("blockmask"), ALU.mult,
               [R_ct, R_cst], [R_ct])
        s5r = [R_s5]

        def e_tt(out, a, b, op):
            return tt(SE, out, a, b, op, s5r, s5r)

        e_tt(t1[:], lr[:], dtt[:], ALU.mult)
        act(t2[:], t1[:], AF.Exp, s5r, s5r)
        act(mg16[:], t1[:], AF.Exp, s5r, s5r, scale=16.0)
        e_tt(t1[:], li[:], dtt[:], ALU.mult)
        act(ai[:], t1[:], AF.Sin, s5r, s5r, scale=1.0 / 8)
        act(t3[:], t1[:], AF.Sin, s5r, s5r, scale=1.0 / 16)
        e_tt(t3[:], t3[:], t3[:], ALU.mult)
        ts(SE, ar[:], t3[:], -2.0, 1.0, ALU.mult, ALU.add, s5r, s5r)
        for _ in range(3):
            e_tt(t3[:], ar[:], ar[:], ALU.mult)
            e_tt(t4[:], ai[:], ai[:], ALU.mult)
            e_tt(ai[:], ar[:], ai[:], ALU.mult)
            ts(SE, ai[:], ai[:], 2.0, None, ALU.mult, None, s5r, s5r)
            e_tt(ar[:], t3[:], t4[:], ALU.subtract)
        e_tt(ar[:], ar[:], t2[:], ALU.mult)
        e_tt(ai[:], ai[:], t2[:], ALU.mult)
        ts(SE, t1[:], ar[:], -1.0, None, ALU.add, None, s5r, s5r)
        e_tt(t3[:], lr[:], lr[:], ALU.mult)
        e_tt(t4[:], li[:], li[:], ALU.mult)
        e_tt(t3[:], t3[:], t4[:], ALU.add)
        e_tt(t3[:], t3[:], negone[:], ALU.pow)
        e_tt(cr[:], t1[:], lr[:], ALU.mult)
        e_tt(t4[:], ai[:], li[:], ALU.mult)
        e_tt(cr[:], cr[:], t4[:], ALU.add)
        e_tt(cr[:], cr[:], t3[:], ALU.mult)
        e_tt(ci[:], ai[:], lr[:], ALU.mult)
        e_tt(t4[:], t1[:], li[:], ALU.mult)
        e_tt(ci[:], ci[:], t4[:], ALU.subtract)
        e_tt(ci[:], ci[:], t3[:], ALU.mult)

        def pow_table(tr_, ti_, nmax, tmpa, tmpb):
            mset(SE, tr_[:, :, 0:1], 1.0, s5r)
            mset(SE, ti_[:, :, 0:1], 0.0, s5r)
            n = 1
            while n < nmax:
                m = min(n, nmax - n)
                mr = bc(tr_[:, :, n:n + 1], [128, 16, m])
                mi = bc(ti_[:, :, n:n + 1], [128, 16, m])
                sr = tr_[:, :, 1:1 + m]
                si = ti_[:, :, 1:1 + m]
                ta = tmpa[:, :, 0:m]
                tb = tmpb[:, :, 0:m]
                e_tt(ta, sr, mr, ALU.mult)
                e_tt(tb, si, mi, ALU.mult)
                e_tt(tr_[:, :, n + 1:n + 1 + m], ta, tb, ALU.subtract)
                e_tt(ta, sr, mi, ALU.mult)
                e_tt(tb, si, mr, ALU.mult)
                e_tt(ti_[:, :, n + 1:n + 1 + m], ta, tb, ALU.add)
                n += m

        apr = A.alloc([128, 16, 17], F32)
        api = A.alloc([128, 16, 17], F32)
        tabc = A.alloc([128, 16, 130], F32)
        tabs = A.alloc([128, 16, 130], F32)
        wt1_off = A.l
        wt1 = A.alloc([128, 8, 128], F32)
        wt2_off = A.l
        wt2 = A.alloc([128, 8, 128], F32)
        ptA = wt1[:].rearrange("p k c -> p (k c)").rearrange("p (a n) -> p a n", n=64)
        ptB = wt2[:].rearrange("p k c -> p (k c)").rearrange("p (a n) -> p a n", n=64)
        wst0 = A.alloc([128, 16, 2, 128], BF16)
        pf0 = A.alloc([128, 17, 2, 128], BF16)
        cp(SE, apr[:, :, 1:2], ar[:].unsqueeze(2), s5r, s5r)
        cp(SE, api[:, :, 1:2], ai[:].unsqueeze(2), s5r, s5r)
        pow_table(apr, api, 16, ptA, ptB)
        e_tt(t1[:], mg16[:], negone[:], ALU.pow)
        e_tt(tabc[:, :, 1:2], apr[:, :, 16:17], t1[:].unsqueeze(2), ALU.mult)
        e_tt(tabs[:, :, 1:2], api[:, :, 16:17], t1[:].unsqueeze(2), ALU.mult)
        pow_table(tabc, tabs, 129, ptA, ptB)

        bld = A.alloc([128, 2, 16, 16], F32, right=True)
        R_bld = Reg()
        for ri in range(2):
            S.dma("sp", bld[:, ri, :, :], bst[ri].rearrange("a p c -> p a c"), writes=[R_bld])
        bb_re = A.alloc([128, 16, 32], F32)
        bb_im = A.alloc([128, 16, 32], F32)
        bt1 = A.alloc([128, 16, 16], F32, right=True)
        bt2 = A.alloc([128, 16, 16], F32, right=True)
        mset(SE, bb_re[:], 0.0, s5r)
        mset(SE, bb_im[:], 0.0, s5r)
        rb = s5r + [R_bld]
        for two in range(2):
            ps_ = slice(64 * two, 64 * two + 64)
            cs_ = slice(16 * two, 16 * two + 16)
            crb = bc(cr[ps_, :].unsqueeze(2), [64, 16, 16])
            cib = bc(ci[ps_, :].unsqueeze(2), [64, 16, 16])
            tt(SE, bt1[ps_], bld[ps_, 0], crb, ALU.mult, rb, s5r)
            tt(SE, bt2[ps_], bld[ps_, 1], cib, ALU.mult, rb, s5r)
            tt(SE, bb_re[ps_, :, cs_], bt1[ps_], bt2[ps_], ALU.subtract, s5r, s5r)
            tt(SE, bt1[ps_], bld[ps_, 1], crb, ALU.mult, rb, s5r)
            tt(SE, bt2[ps_], bld[ps_, 0], cib, ALU.mult, rb, s5r)
            tt(SE, bb_im[ps_, :, cs_], bt1[ps_], bt2[ps_], ALU.add, s5r, s5r)

        cp(SE, bbb[:, :, 0, :], bb_re[:], s5r, s5r)
        cp(SE, bbb[:, :, 1, :], bb_im[:], s5r, s5r)

        s5r = [R_s5, R_ct]
        wst = [wst0]
        pf = [pf0]
        R_wst = [Reg(), Reg()]; R_pf = [Reg(), Reg()]
        R_wt12 = Reg()
        def s5_tables(t, part):
            tb = t % 2
            prs = slice(4 * t, 4 * t + 4)
            wgroups = [(0, 8, "pool"), (8, 8, "pool")]
            for (k0_, nk_, eng_) in wgroups:
                if part not in ("wst", "wst_all_pool"):
                    continue
                ks = slice(k0_, k0_ + nk_)
                a_r = bc(apr[:, prs, ks].rearrange("p a k -> p k a").unsqueeze(3), [128, nk_, 4, 32])
                a_i = bc(api[:, prs, ks].rearrange("p a k -> p k a").unsqueeze(3), [128, nk_, 4, 32])
                b_r = bc(bb_re[:, prs, :].unsqueeze(1), [128, nk_, 4, 32])
                b_i = bc(bb_im[:, prs, :].unsqueeze(1), [128, nk_, 4, 32])
                wa, wb_, R_w = wt1, wt2, R_wt12
                w1 = wa[:, 0:nk_, :].rearrange("p k (a c) -> p k a c", c=32)
                w2 = wb_[:, 0:nk_, :].rearrange("p k (a c) -> p k a c", c=32)
                o_r = wst[tb][:, ks, 0, :].rearrange("p k (a c) -> p k a c", c=32)
                o_i = wst[tb][:, ks, 1, :].rearrange("p k (a c) -> p k a c", c=32)
                rw = s5r + [R_w]
                tt(eng_, w1, a_r, b_r, ALU.mult, s5r, [R_w])
                tt(eng_, w2, a_i, b_i, ALU.mult, s5r, [R_w])
                tt(eng_, o_r, w1, w2, ALU.subtract, rw, [R_wst[tb]])
                tt(eng_, w1, a_r, b_i, ALU.mult, rw, [R_w])
                tt(eng_, w2, a_i, b_r, ALU.mult, rw, [R_w])
                tt(eng_, o_i, w1, w2, ALU.add, rw, [R_wst[tb]])
            if part == "up":
                cp("act", uP[tb][:, :, 0:128], uT[:, t, 0:2048].rearrange("p (c s) -> p s c", s=16),
                   creg(R_u, 0, 2048), [R_uP[tb]])
                cp("act", uP[tb][:, :, 128], uT[:, t, 2048:2064], [R_u[16]], [R_uP[tb]])
            for (k0, nk) in ((0, 8), (8, 4), (12, 4), (16, 1)):
                on_pool = (k0 < 12)
                if part not in ("pfp", "dve") or on_pool != (part == "pfp"):
                    continue
                eng_ = "pool" if on_pool else "dve"
                if on_pool:
                    wa, wb_, R_w = wt1, wt2, R_wt12
                else:
                    wa, wb_, R_w = wt3, wt4, R_wt34
                ks = slice(k0, k0 + nk)
                a_r = bc(apr[:, prs, ks].rearrange("p a k -> p k a").unsqueeze(3), [128, nk, 4, 32])
                a_i = bc(api[:, prs, ks].rearrange("p a k -> p k a").unsqueeze(3), [128, nk, 4, 32])
                c_r = bc(ct_re[:, 128 * t:128 * t + 128].rearrange("p (a c) -> p a c", c=32).unsqueeze(1), [128, nk, 4, 32])
                c_i = bc(nct_im[:, 128 * t:128 * t + 128].rearrange("p (a c) -> p a c", c=32).unsqueeze(1), [128, nk, 4, 32])
                w1 = wa[:, 0:nk, :].rearrange("p k (a c) -> p k a c", c=32)
                w2 = wb_[:, 0:nk, :].rearrange("p k (a c) -> p k a c", c=32)
                o_r = pf[tb][:, ks, 0, :].rearrange("p k (a c) -> p k a c", c=32)
                o_i = pf[tb][:, ks, 1, :].rearrange("p k (a c) -> p k a c", c=32)
                rw = s5r + [R_w]
                tt(eng_, w1, a_r, c_r, ALU.mult, s5r, [R_w])
                tt(eng_, w2, a_i, c_i, ALU.mult, s5r, [R_w])
                tt(eng_, o_r, w1, w2, ALU.add, rw, [R_pf[tb]])
                tt(eng_, w1, a_r, c_i, ALU.mult, rw, [R_w])
                tt(eng_, w2, a_i, c_r, ALU.mult, rw, [R_w])
                tt(eng_, o_i, w1, w2, ALU.subtract, rw, [R_pf[tb]])

        s5_tables(0, "wst_all_pool")
        s5_tables(0, "pfp")
        if K_STOP == 'P2base':
            S.finish()
            return nc
        mark_base = A.l
        Sf = A.alloc([128, 4, 128], F32)
        Sb = A.alloc([128, 4, 128], BF16)
        attsb = A.alloc([128, 4, 64], BF16)
        mix_off = SB_HI - 128 * 0 - (8 * NT * 2)
        oblk = nc.alloc_sbuf_tensor_at("oblk", [128, 4, 512], F32, offset=mix_off)
        sqb = nc.alloc_sbuf_tensor_at("sqb", [128, 4, 512], BF16, offset=mix_off + 8192)
        rsb = nc.alloc_sbuf_tensor_at("rsb", [128, 512], F32, offset=mix_off + 12288)
        otmp = nc.alloc_sbuf_tensor_at("otmp", [128, 512], F32, offset=mix_off + 14336)
        R_Sf = Reg(); R_Sb = Reg(); R_att = Reg(); R_ob = Reg(); R_sq = Reg(); R_rs = Reg(); R_ot = Reg()
        hgngT = V("hgng")

        def hg_norm_block(c0, W, rd_extra):
            for h in range(4):
                act(sqb[:, h, 0:W], oblk[:, h, 0:W], AF.Square, [R_ob] + rd_extra, [R_sq])
                pi = bank()
                mm(PB(pi, W), onesb[:, :], sqb[:, h, 0:W], True, True, [R_cb, R_sq], [PR[pi]])
                act(rsb[:, 0:W], PB(pi, W), AF.Ln, [PR[pi]], [R_rs], scale=1.0 / 128, bias=EPS)
                act(rsb[:, 0:W], rsb[:, 0:W], AF.Exp, [R_rs], [R_rs], scale=-0.5)
                stt("dve", otmp[:, 0:W], oblk[:, h, 0:W], hgngT[:, h:h + 1], rsb[:, 0:W], ALU.mult, ALU.mult,
                    [R_ob, R_rs, R_vec], [R_ot])
                mr = creg(R_mix, c0, c0 + W)
                tt("dve", mixT[:, 4 + h, c0:c0 + W], mixT[:, 4 + h, c0:c0 + W], otmp[:, 0:W], ALU.mult,
                   [R_ot] + mr, mr)

        attsbs = [attsb] + [A.alloc([128, 4, 64], BF16) for _ in range(2)]
        R_atts = [R_att, Reg(), Reg()]

        def hg_geo(ci):
            if ci < 0:
                return 2048, 16, 16, 0, 0
            return 64 * ci, 64, ci // 2, 64 * (ci % 2), ci + 1

        hg_state = {}

        def hg_X(ci):
            c0, Wc, ti, r0, ei = hg_geo(ci)
            rows = slice(r0, r0 + Wc)
            sl = (ci + 1) % 3
            pa = bank()
            for h in range(4):
                mm(PB(pa)[rows, 64 * h:64 * h + Wc], kT[:, h, c0:c0 + Wc], qT[:, h, c0:c0 + Wc], True, True,
                   creg(R_k[h], c0, c0 + Wc) + creg(R_q[h], c0, c0 + Wc), [PR[pa]], tp=(0, r0), signal=(h == 3))
            tt("dve", attsbs[sl][rows, :, 0:Wc], PB(pa)[rows, 0:256].rearrange("p (h c) -> p h c", c=64)[:, :, 0:Wc],
               bc(C("attmask")[rows, 0:Wc].unsqueeze(1), [Wc, 4, Wc]), ALU.mult, [PR[pa], R_cst], [R_atts[sl]])

        def hg_Y(ci):
            c0, Wc, ti, r0, ei = hg_geo(ci)
            rows = slice(r0, r0 + Wc)
            pS = bank()
            for h in range(4):
                hc = slice(128 * h, 128 * h + 128)
                mm(PB(pS)[:, hc], kk_tok[rows, ti, hc], v_tok[rows, ti, hc], True, True,
                   [R_kk[ti], R_v[ti]], [PR[pS]], tp=(r0, 0), signal=(h == 3))
            hg_state[ci] = pS
            reserved.add(pS)

        def hg_Z(ci):
            c0, Wc, ti, r0, ei = hg_geo(ci)
            rows = slice(r0, r0 + Wc)
            sl = (ci + 1) % 3
            po = bank()
            for h in range(4):
                hc = slice(128 * h, 128 * h + 128)
                if ci >= 0:
                    mm(PB(po)[:, 64 * h:64 * h + Wc], Sb[:, h, :], qT[:, h, c0:c0 + Wc], True, False,
                       [R_Sb] + creg(R_q[h], c0, c0 + Wc), [PR[po]])
                mm(PB(po)[:, 64 * h:64 * h + Wc], v_tok[rows, ti, hc], attsbs[sl][rows, h, 0:Wc], ci < 0, True,
                   [R_v[ti], R_atts[sl]], [PR[po]], tp=(r0, 0), signal=(h == 3))
            ocol = (c0 % 512) if ci >= 0 else 0
            cp("act", oblk[:, :, ocol:ocol + Wc], PB(po)[:, 0:256].rearrange("p (h c) -> p h c", c=64)[:, :, 0:Wc],
               [PR[po]], [R_ob])
            pS = hg_state.pop(ci)
            reserved.discard(pS)
            if ci < 0:
                cp("dve", Sf[:].rearrange("p h v -> p (h v)"), PB(pS)[:, 0:512], [PR[pS]], [R_Sf])
            else:
                for h in range(4):
                    stt("dve", Sf[:, h, :], Sf[:, h, :], ebl[:, h, ei:ei + 1], PB(pS)[:, 128 * h:128 * h + 128],
                        ALU.mult, ALU.add, [R_Sf, R_ebl, PR[pS]], [R_Sf])
            cp("act", Sb[:].rearrange("p h v -> p (h v)"), Sf[:].rearrange("p h v -> p (h v)"), [R_Sf], [R_Sb])

        NSB = 4
        s0 = [A.alloc([128, 4, 128], F32) for _ in range(NSB)]
        s1b = [A.alloc([128, 4, 128], BF16) for _ in range(2)]
        kkm = [A.alloc([32, 512], BF16) for _ in range(3)]
        R_s0 = [Reg() for _ in range(NSB)]
        R_s1b = [Reg(), Reg()]
        R_kkm = [Reg(), Reg(), Reg()]

        def hg_sample_load(n):
            S.dma("sp", s0[n % NSB][:], hgs[n].rearrange("h k v -> k h v"), writes=[R_s0[n % NSB]])

        def hg_kkm(n):
            act(kkm[n % 3][0:32, :], kk_tok[0:32, 16, :], AF.Copy, [R_kk[16], R_cst], [R_kkm[n % 3]],
                scale=C("onehot", 32)[:, n:n + 1])

        def hg_s1(n):
            b = n % 2
            kb = n % 3
            sb_ = n % NSB
            if n >= NSB:
                hg_sample_load(n)
            if n + 2 < 16:
                hg_kkm(n + 2)
            pk = bank()
            for h in range(4):
                hc = slice(128 * h, 128 * h + 128)
                mm(PB(pk)[:, hc], kkm[kb][0:32, hc], v_tok[0:32, 16, hc], True, True,
                   [R_kkm[kb], R_v[16]], [PR[pk]], signal=(h == 3))
            s0f = s0[sb_][:].rearrange("p h v -> p (h v)")
            tt("dve", s0[sb_][:], s0[sb_][:], bc(fsamp[:, :, n:n + 1], [128, 4, 128]), ALU.mult,
               [R_s0[sb_], R_ebl], [R_s0[sb_]])
            tt("dve", s0f, s0f, PB(pk)[:, 0:512], ALU.add, [R_s0[sb_], PR[pk]], [R_s0[sb_]])
            cp("act", s1b[b][:].rearrange("p h v -> p (h v)"), s0f, [R_s0[sb_]], [R_s1b[b]])
            S.dma("act", o_hgs[n].rearrange("h k v -> k h v"), s0[sb_][:], reads=[R_s0[sb_]])

        def hg_s2(n, po):
            b = n % 2
            for h in range(4):
                mm(PB(po)[:, 16 * h + n:16 * h + n + 1], s1b[b][:, h, :], qT[:, h, 2064 + n:2065 + n], True, True,
                   [R_s1b[b], R_q[h][16]], [PR[po]], signal=(h == 3))

        def hg_samples(n_list, po):
            hg_kkm(0)
            hg_kkm(1)
            hg_s1(n_list[0])
            for i_ in range(1, len(n_list)):
                hg_s1(n_list[i_])
                hg_s2(n_list[i_ - 1], po)
            hg_s2(n_list[-1], po)

        for n_ in range(NSB):
            hg_sample_load(n_)

        ometa = A.alloc([128, 4, 32], F32)
        R_om = Reg()
        order = [-1] + list(range(32))
        LA = 2
        for i_ in range(min(LA, len(order))):
            hg_X(order[i_])
            hg_Y(order[i_])
        po_s = bank()
        reserved.add(po_s)
        hg_kkm(0)
        hg_kkm(1)
        ns_done = 0
        for i_, ci in enumerate(order):
            if i_ + LA < len(order):
                hg_X(order[i_ + LA])
                hg_Y(order[i_ + LA])
            hg_Z(ci)
            if ci < 0:
                cp("dve", ometa[:, :, 0:16], oblk[:, :, 0:16], [R_ob], [R_om])
            elif ci % 8 == 7:
                hg_norm_block(512 * (ci // 8), 512, [])
            if ci >= 0 and ci % 2 == 0 and ns_done < 16:
                hg_s1(ns_done)
                if ns_done >= 1:
                    hg_s2(ns_done - 1, po_s)
                ns_done += 1
        hg_s2(15, po_s)
        reserved.discard(po_s)
        for h in range(4):
            S.dma("sp", o_hgp[h], Sf[:, h, :], reads=[R_Sf])
        cp("act", ometa[:, :, 16:32], PB(po_s)[:, 0:64].rearrange("p (h n) -> p h n", n=16), [PR[po_s]], [R_om])
        cp("dve", oblk[:, :, 0:32], ometa[:, :, :], [R_om, R_ob], [R_ob])
        hg_norm_block(2048, 32, [])

        S.barrier()
        A.l = mark_base
        A.r = mark_R
        if K_STOP == 'P2a':
            S.finish()
            return nc
        s5tok = A.alloc([16, 2, 2048], F32)
        R_s5tok = Reg()
        S.dma("sp", s5tok[:], s5s.rearrange("r n c -> n r c"), writes=[R_s5tok])
        h0 = A.alloc([128, 16, 2, 16], F32)
        h0b = A.alloc([128, 16, 2, 16], BF16)
        R_h0 = Reg()
        for half in range(2):
            pi = bank()
            for pl in range(8):
                pair = 8 * half + pl
                for ri in range(2):
                    tr(PB(pi)[:, (pl * 2 + ri) * 16:(pl * 2 + ri) * 16 + 16], s5tok[0:16, ri, pair * 128:(pair + 1) * 128],
                       identf[0:16, 0:16], [R_s5tok, R_cst], [PR[pi]], signal=(pl == 7 and ri == 1))
            cp("act", h0[:, 8 * half:8 * half + 8, :, :].rearrange("p a r n -> p (a r n)"), PB(pi)[:, 0:256],
               [PR[pi]], [R_h0])
        cp("dve", h0b[:].rearrange("p a r n -> p (a r n)"), h0[:].rearrange("p a r n -> p (a r n)"), [R_h0], [R_h0])

        wst.append(A.alloc([128, 16, 2, 128], BF16))
        pf.append(A.alloc([128, 17, 2, 128], BF16))
        wd = A.alloc([128, 16, 2, 128], BF16)
        knf = [A.alloc([128, 16, 128], BF16) for _ in range(2)]
        wt3 = A.alloc([128, 4, 128], F32)
        wt4 = A.alloc([128, 4, 128], F32)
        dsb = A.alloc([128, 4, 2, 146], F32)
        drot = A.alloc([128, 4, 2, 128], F32)
        gsc = A.alloc([128, 4, 2, 128], F32)
        sbuf_ = A.alloc([128, 4, 2, 129], F32)
        sfar = A.alloc([128, 4, 2, 128], BF16)
        dt1 = A.alloc([128, 4, 128], F32)
        dt2 = A.alloc([128, 4, 128], F32)
        g1 = A.alloc([128, 4, 2], F32)
        sfin = A.alloc([128, 2, 16], F32)
        s1s = A.alloc([128, 16, 2, 16], F32)
        uP = [A.alloc([128, 16, 130], BF16) for _ in range(2)]
        R_uP = [Reg(), Reg()]
        gtmps = [A.alloc([128, 512], F32) for _ in range(2)]
        R_gts = [Reg(), Reg()]
        gtmp = gtmps[0]
        R_gt = R_gts[0]
        R_wd = Reg(); R_knf = [Reg(), Reg()]
        R_wt34 = Reg()
        ktmp_t = A.alloc([128, 512], F32)
        R_kt = Reg()
        R_d = Reg(); R_sfar = Reg(); R_sfin = Reg(); R_s1s = Reg()
        sdT = V("s5d")
        w_glu_sb = A.alloc([128, 4, 512], BF16)
        R_wglu = Reg()
        S.dma("pool", w_glu_sb[:], w_glu.rearrange("(kt p) c -> p kt c", p=128), writes=[R_wglu])

        def s5_F(t):
            tb = t % 2
            prs = slice(4 * t, 4 * t + 4)
            for grp in range(4):
                pi = bank()
                pv = PBb(pi).rearrange("p (j c) -> p j c", c=128)
                for j in range(8):
                    idx = grp * 8 + j
                    k, ri = idx // 2, idx % 2
                    tr(pv[:, j, :], wst[tb][:, k, ri, :], identb[:, :], [R_wst[tb], R_cb], [PR[pi]], signal=(j == 7))
                cp("act", wd[:, 4 * grp:4 * grp + 4, :, :].rearrange("p k r c -> p (k r c)"), PBb(pi)[:, :],
                   [PR[pi]], [R_wd])
            dbanks = [bank(), bank(), bank(), bank()]
            ur = R_u
            for q in range(4):
                for ri in range(2):
                    db = dbanks[q]
                    o0 = 160 * ri
                    for s_ in range(16):
                        mm(PB(db)[:, o0:o0 + 129], wd[32 * q:32 * q + 32, 15 - s_, ri, :],
                           uP[tb][32 * q:32 * q + 32, s_, 0:129], s_ == 0, s_ == 15, [R_wd, R_uP[tb]], [PR[db]],
                           tp=(32 * q, 0))
                    mm(PB(db)[:, o0 + 130:o0 + 146], wd[32 * q:32 * q + 32, 0, ri, :],
                       uT[32 * q:32 * q + 32, t, 2064:2080], True, True, [R_wd, R_u[16]], [PR[db]], tp=(32 * q, 0))
            for q in range(4):
                db = dbanks[q]
                cp("act", dsb[:, q, :, :], PB(db)[:, 0:320].rearrange("p (r c) -> p r c", c=160)[:, :, 0:146],
                   [PR[db]], [R_d])

        def s5_K(t):
            tb = t % 2
            prs = slice(4 * t, 4 * t + 4)
            pi = bank()
            for q in range(4):
                pair = 4 * t + q
                for ri in range(2):
                    mm(PB(pi)[32 * q:32 * q + 32, :], bbb[:, pair, ri, :], pf[tb][:, 0:16, ri, 32 * q:32 * q + 32],
                       ri == 0, ri == 1, [R_pf[tb], R_s5], [PR[pi]], tp=(0, 32 * q), signal=(q == 3 and ri == 1))
            ktmp = ktmp_t[:, :]
            cp("act", ktmp, PB(pi)[:, :], [PR[pi]], [R_kt])
            stt("dve", ktmp[:, 0:32], C("blockident"), sdT[:, t:t + 1], ktmp[:, 0:32], ALU.mult, ALU.add,
                [R_kt, R_cst, R_vec], [R_kt])
            tt("dve", knf[tb][:].rearrange("p k (a c) -> p k a c", c=32),
               bc(ktmp.rearrange("p (k c) -> p k c", c=32).unsqueeze(2), [128, 16, 4, 32]),
               bc(C("qmask").rearrange("p (a c) -> p a c", c=32).unsqueeze(1), [128, 16, 4, 32]), ALU.mult,
               [R_kt, R_cst], [R_knf[tb]])

        def s5_M(t):
            tb = t % 2
            prs = slice(4 * t, 4 * t + 4)
            rd = [R_d] + s5r
            tcs = tabc[:, prs, 2:130]
            tss = tabs[:, prs, 2:130]
            dre = dsb[:, :, 0, 0:128]
            dim_ = dsb[:, :, 1, 0:128]
            tt("dve", dt1[:], dre, tcs, ALU.mult, rd, [R_d])
            tt("dve", dt2[:], dim_, tss, ALU.mult, rd, [R_d])
            tt("dve", drot[:, :, 0, :], dt1[:], dt2[:], ALU.add, rd, [R_d])
            tt("dve", dt1[:], dim_, tcs, ALU.mult, rd, [R_d])
            tt("dve", dt2[:], dre, tss, ALU.mult, rd, [R_d])
            tt("dve", drot[:, :, 1, :], dt1[:], dt2[:], ALU.subtract, rd, [R_d])
            tc1 = tabc[:, prs, 1]
            ts1 = tabs[:, prs, 1]
            dmr = dsb[:, :, 0, 128]
            dmi = dsb[:, :, 1, 128]
            tt("dve", dt1[:, :, 0], dmr, tc1, ALU.mult, rd, [R_d])
            tt("dve", dt2[:, :, 0], dmi, ts1, ALU.mult, rd, [R_d])
            tt("dve", g1[:, :, 0], dt1[:, :, 0], dt2[:, :, 0], ALU.add, rd, [R_d])
            tt("dve", dt1[:, :, 0], dmi, tc1, ALU.mult, rd, [R_d])
            tt("dve", dt2[:, :, 0], dmr, ts1, ALU.mult, rd, [R_d])
            tt("dve", g1[:, :, 1], dt1[:, :, 0], dt2[:, :, 0], ALU.subtract, rd, [R_d])
            for q in range(4):
                pair = 4 * t + q
                for ri in range(2):
                    S.op("dve", lambda e, q=q, ri=ri, pair=pair: e.tensor_tensor_scan(
                        out=gsc[:, q, ri, :], data0=bc(mg16[:, pair:pair + 1], [128, 128]), data1=drot[:, q, ri, :],
                        initial=g1[:, q, ri:ri + 1], op0=ALU.mult, op1=ALU.add), reads=rd, writes=[R_d])
            gre = gsc[:, :, 0, :]
            gim = gsc[:, :, 1, :]
            tt("dve", dt1[:], gre, tcs, ALU.mult, rd, [R_d])
            tt("dve", dt2[:], gim, tss, ALU.mult, rd, [R_d])
            tt("dve", sbuf_[:, :, 0, 1:129], dt1[:], dt2[:], ALU.subtract, rd, [R_d])
            tt("dve", dt1[:], gim, tcs, ALU.mult, rd, [R_d])
            tt("dve", dt2[:], gre, tss, ALU.mult, rd, [R_d])
            tt("dve", sbuf_[:, :, 1, 1:129], dt1[:], dt2[:], ALU.add, rd, [R_d])
            cp("dve", sbuf_[:, :, :, 0], dsb[:, :, :, 128], rd, [R_d])
            cp("dve", sfar[:], sbuf_[:, :, :, 0:128], rd, [R_sfar])
            cp("dve", sfin[:, :, prs].rearrange("p r a -> p a r"), sbuf_[:, :, :, 128], rd, [R_sfin])
            arb = bc(ar[:, prs].unsqueeze(2), [128, 4, 16])
            aib = bc(ai[:, prs].unsqueeze(2), [128, 4, 16])
            h0r = h0[:, prs, 0, :]
            h0i = h0[:, prs, 1, :]
            e1 = dt1[:, :, 0:16]
            e2 = dt2[:, :, 0:16]
            rh = rd + [R_h0]
            tt("dve", e1, h0r, arb, ALU.mult, rh, [R_d])
            tt("dve", e2, h0i, aib, ALU.mult, rh, [R_d])
            tt("dve", e1, e1, e2, ALU.subtract, rd, [R_d])
            tt("dve", s1s[:, prs, 0, :], e1, dsb[:, :, 0, 130:146], ALU.add, rd, [R_s1s])
            tt("dve", e1, h0i, arb, ALU.mult, rh, [R_d])
            tt("dve", e2, h0r, aib, ALU.mult, rh, [R_d])
            tt("dve", e1, e1, e2, ALU.add, rd, [R_d])
            tt("dve", s1s[:, prs, 1, :], e1, dsb[:, :, 1, 130:146], ALU.add, rd, [R_s1s])

        def s5_B(t):
            tb = t % 2
            prs = slice(4 * t, 4 * t + 4)

            def gelu(x, g_a, mo, pr, mregs, R_g):
                act(g_a, x, AF.Square, [pr], [R_g])
                act(g_a, g_a, AF.Identity, [R_g], [R_g], scale=0.044715, bias=1.0)
                tt("dve", g_a, g_a, x, ALU.mult, [R_g, pr], [R_g])
                act(g_a, g_a, AF.Sigmoid, [R_g], [R_g], scale=1.5957691216057308)
                tt("dve", mo, g_a, x, ALU.mult, [R_g, pr], mregs)
            uv = uP[tb][:, :, 0:128]
            ureg = [R_uP[tb]]
            mview = mixT[:, t, 0:2048].rearrange("p (c s) -> p c s", s=16)
            for b_ in range(4):
                yb_ = bank()
                for k in range(0, 4 * b_ + 4):
                    s_lo = max(k, 4 * b_)
                    mm(PB(yb_)[:, (s_lo - 4 * b_) * 128:512], knf[tb][:, k, :], uv[:, s_lo - k:4 * b_ + 4 - k, :],
                       k == 0, False, [R_knf[tb]] + ureg, [PR[yb_]])
                for k in range(4 * b_, 4 * b_ + 4):
                    for q in range(4):
                        for ri in range(2):
                            last = (k == 4 * b_ + 3 and q == 3 and ri == 1)
                            mm(PB(yb_)[32 * q:32 * q + 32, (k - 4 * b_) * 128:(k - 4 * b_) * 128 + 128],
                               pf[tb][:, k + 1, ri, 32 * q:32 * q + 32], sfar[:, q, ri, :], False, last,
                               [R_pf[tb], R_sfar], [PR[yb_]], tp=(0, 32 * q), signal=last)
                gelu(PB(yb_).rearrange("p (s c) -> p c s", c=128),
                     gtmps[b_ % 2][:, :].rearrange("p (c s) -> p c s", s=4),
                     mview[:, :, 4 * b_:4 * b_ + 4], PR[yb_], creg(R_mix, 0, 2048), R_gts[b_ % 2])
            p4 = bank()
            ureg4 = [R_u[16]]
            mm(PB(p4, 32), knf[tb][:, 0, :], uT[:, t, 2048:2080], True, False, [R_knf[tb]] + ureg4, [PR[p4]])
            for k in range(1, 16):
                mm(PB(p4)[:, k:16], knf[tb][:, k, :], uT[:, t, 2048:2048 + 16 - k], False, False,
                   [R_knf[tb]] + ureg4, [PR[p4]])
            for q in range(4):
                pair = 4 * t + q
                for ri in range(2):
                    last = (q == 3 and ri == 1)
                    mm(PB(p4)[32 * q:32 * q + 32, 16:32], pf[tb][:, 1, ri, 32 * q:32 * q + 32],
                       h0b[:, pair, ri, :], False, last, [R_pf[tb], R_h0], [PR[p4]], tp=(0, 32 * q), signal=last)
            gelu(PB(p4, 32), gtmps[0][:, 0:32], mixT[:, t, 2048:2080], PR[p4], [R_mix[16]], R_gts[0])

        if K_STOP == 'T0':
            S.finish()
            return nc
        try:
            s5_tables(0, "up")
            s5_tables(0, "dve")
            s5_tables(1, "wst")
            s5_tables(1, "pfp")
            s5_tables(1, "up")
            s5_F(0)
            s5_K(0)
            for t in range(4):
                s5_M(t)
                if t + 1 < 4:
                    s5_tables(t + 1, "dve")
                    if t + 2 < 4:
                        s5_tables(t + 2, "wst")
                    s5_F(t + 1)
                s5_B(t)
                if t + 2 < 4:
                    s5_tables(t + 2, "pfp")
                    s5_tables(t + 2, "up")
                if t + 1 < 4:
                    s5_K(t + 1)
        except _Stop:
            S.finish()
            return nc

        gates = [nc.alloc_sbuf_tensor_at("gate0", [128, 4, 512], BF16, offset=wt1_off),
                 nc.alloc_sbuf_tensor_at("gate1", [128, 4, 512], BF16, offset=wt2_off)]
        R_gates = [R_wt12, Reg()]
        bgT = V("bglu")
        for bi, (c0, c1) in enumerate(BLOCKS):
            W = c1 - c0
            mr = creg(R_mix, c0, c1)
            gate = gates[bi % 2]
            R_gate = R_gates[bi % 2]
            for to in range(4):
                pi = bank()
                for kt in range(4):
                    mm(PB(pi, W), w_glu_sb[:, kt, 128 * to:128 * to + 128], mixT[:, kt, c0:c1], kt == 0, kt == 3,
                       [R_wglu] + mr, [PR[pi]])
                act(gate[:, to, 0:W], PB(pi, W), AF.Sigmoid, [PR[pi], R_vec], [R_gate], bias=bgT[:, to:to + 1])
            tt("dve", mixT[:, 0:4, c0:c1], mixT[:, 0:4, c0:c1], gate[:, :, 0:W], ALU.mult, [R_gate] + mr, mr)

        pi = bank()
        for ri in range(2):
            tr(PB(pi)[0:16, 128 * ri:128 * ri + 128], sfin[:, ri, :], identf, [R_sfin, R_cst], [PR[pi]], signal=(ri == 1))
        so = A.alloc([16, 256], F32)
        R_so = Reg()
        cp("act", so[:], PB(pi)[0:16, 0:256], [PR[pi]], [R_so])
        for ri in range(2):
            S.dma("sp", o_s5p[ri], so[0:16, 128 * ri:128 * ri + 128], reads=[R_so])
        sos = s5tok
        R_sos = R_s5tok
        for ri in range(2):
            for qd_ in range(4):
                pi = bank()
                for j in range(4):
                    pair = 4 * qd_ + j
                    tr(PB(pi)[0:16, 128 * j:128 * j + 128], s1s[:, pair, ri, :], identf, [R_s1s, R_cst], [PR[pi]],
                       signal=(j == 3))
                cp("act", sos[0:16, ri, 512 * qd_:512 * qd_ + 512], PB(pi)[0:16, :], [PR[pi]], [R_sos])
        S.dma("sp", o_s5s.rearrange("r n c -> n r c"), sos[:], reads=[R_sos])

        S.barrier()
        A.l = mark_G

        if K_STOP == 'P2b':
            S.finish()
            return nc
        h1 = A.alloc([128, NTILES, D], F32)
        R_h1 = [Reg() for _ in range(NTILES)]
        hn2T = A.alloc([128, 8, NT], BF16)
        R_hn2 = [Reg() for _ in range(NTILES)]
        junk = A.alloc([128, D], BF16)
        ssb = [A.alloc([128, 2], F32) for _ in range(4)]
        R_junk = Reg()
        R_ss = [Reg() for _ in range(4)]
        mark_p3 = A.l
        w_out_sb = A.alloc([128, 8, D], BF16)
        R_woutk = [Reg() for _ in range(2)]
        w_out_v = w_out.rearrange("(kt p) c -> p kt c", p=128)
        for g_ in range(2):
            S.dma("pool", w_out_sb[:, :, 512 * g_:512 * g_ + 512], w_out_v[:, :, 512 * g_:512 * g_ + 512],
                  writes=[R_woutk[g_]])
        xt = [A.alloc([128, D], F32) for _ in range(4)]
        xnb = [A.alloc([128, D], BF16) for _ in range(4)]
        R_xt = [Reg() for _ in range(4)]
        R_xnb = [Reg() for _ in range(4)]
        gffnT = V("gffn")
        def p3_A(ti):
            c0, c1 = tile_cols(ti)
            R = c1 - c0
            b = ti % 4
            S.dma("sp", xt[b][0:R, :], xtok[c0:c1, :], writes=[R_xt[b]])
            p2 = bank2()
            for hf in range(2):
                for kt in range(8):
                    mm(PB(p2 + hf)[0:R, :], mixT[:, kt, c0:c1], w_out_sb[:, kt, 512 * hf:512 * hf + 512], kt == 0, kt == 7,
                       [R_mix[ti], R_woutk[hf]], [PR[p2 + hf]])
            hh = h1[0:R, ti, :]
            tt("dve", hh, PB(p2, n=2)[0:R, :], xt[b][0:R, :], ALU.add, [PR[p2], PR[p2 + 1], R_xt[b]], [R_h1[ti]])
            act(junk[0:R, :], hh, AF.Square, [R_h1[ti]], [R_junk, R_ss[b]], accum_out=ssb[b][0:R, 0:1])
            act(ssb[b][0:R, 1:2], ssb[b][0:R, 0:1], AF.Sqrt, [R_ss[b]], [R_ss[b]], scale=1.0 / D, bias=EPS)
            recip(ssb[b][0:R, 1:2], ssb[b][0:R, 1:2], [R_ss[b]], [R_ss[b]])
            ts("pool", xnb[b][0:R, :], hh, ssb[b][0:R, 1:2], 1.0, ALU.mult, ALU.mult, [R_h1[ti], R_ss[b]], [R_xnb[b]])

        def p3_B(ti):
            c0, c1 = tile_cols(ti)
            R = c1 - c0
            b = ti % 4
            pi = bank()
            pv = PBb(pi).rearrange("p (k r) -> p k r", r=128)
            for kt in range(8):
                tr(pv[:, kt, 0:R], xnb[b][0:R, kt * 128:(kt + 1) * 128], identb[0:R, 0:R],
                   [R_xnb[b], R_cb], [PR[pi]], signal=(kt == 7))
            tt("dve", hn2T[:, :, c0:c1], pv[:, :, 0:R], bc(gffnT.unsqueeze(2), [128, 8, R]), ALU.mult,
               [PR[pi], R_vec], [R_hn2[ti]])

        p3_A(0)
        p3_A(1)
        for ti in range(2, NTILES):
            p3_A(ti)
            p3_B(ti - 2)
        p3_B(NTILES - 2)
        p3_B(NTILES - 1)
        S.barrier()
        A.l = mark_p3
        A.r = SB_HI

        if K_STOP == 'P3':
            S.finish()
            return nc
        HWM = 1056
        mT_off = A.l
        mT = A.alloc([128, NJ, HWM], BF16)
        R_mT = Reg()
        cbuf = nc.alloc_sbuf_tensor_at("cbuf", [32, DFF], F32, offset=mT_off)
        wup_off = [A.l, A.l + 8192]
        wupb = [A.alloc([128, 8, 2, 256], BF16) for _ in range(2)]
        R_wup = [Reg() for _ in range(2)]
        wdnb = [nc.alloc_sbuf_tensor_at("wdn%d" % i, [128, NJ, 128], BF16, offset=wup_off[i]) for i in range(2)]
        R_wdn = R_wup
        arow = [A.alloc([128, 2 + HWM], F32) for _ in range(2)]
        crow = [A.alloc([128, 1040], F32) for _ in range(2)]
        vrow = [A.alloc([128, HWM], BF16) for _ in range(2)]
        R_ar = [Reg(), Reg()]
        R_cr = [Reg(), Reg()]
        R_vr = [Reg(), Reg()]
        carry = A.alloc([128, NJ, 2], F32)
        convp = A.alloc([128, 2, NJ], F32)
        aS = A.alloc([128, NJ, 16], F32)
        vS = A.alloc([128, NJ, 16], BF16)
        R_carry = Reg(); R_convp = Reg(); R_aS = Reg()
        yts = [A.alloc([128, 512], F32) for _ in range(2)]
        R_yt = [Reg(), Reg()]
        gfbc = A.alloc([128, D], F32)
        R_gf = Reg()
        S.dma("sp", gfbc[:], gfinal.partition_broadcast(128), writes=[R_gf])
        cwT = V("convw")
        cbT = V("convb")
        S.dma("sp", cbuf[:], cvs.rearrange("n r c -> (n r) c"), writes=[R_mT])
        bufT = A.alloc([128, NJ, 32], F32)
        R_bufT = Reg()
        for grp in range(2):
            pi = bank2()
            for jj in range(11):
                j = grp * 11 + jj
                tr(PB(pi, n=2)[:, 32 * jj:32 * jj + 32], cbuf[0:32, 128 * j:128 * j + 128], identf[0:32, 0:32],
                   [R_mT, R_cst], [PR[pi], PR[pi + 1]], signal=(jj == 10))
            cp("act", bufT[:, 11 * grp:11 * grp + 11, :].rearrange("p j c -> p (j c)"), PB(pi, n=2)[:, 0:352],
               [PR[pi], PR[pi + 1]], [R_bufT])
        S.dma("sp", o_cvs[:, 0, :], cvs[:, 1, :])

        w_up_v = w_up.rearrange("(kt p) c -> p kt c", p=128)
        w_dn_v = w_down.rearrange("(j p) c -> p j c", p=128)

        def final_norm(ti):
            c0, c1 = tile_cols(ti)
            R = c1 - c0
            b = ti % 2
            hh = h1[0:R, ti, :]
            act(junk[0:R, :], hh, AF.Square, [R_h1[ti]], [R_junk, R_ss[b]], accum_out=ssb[b][0:R, 0:1])
            act(ssb[b][0:R, 1:2], ssb[b][0:R, 0:1], AF.Sqrt, [R_ss[b]], [R_ss[b]], scale=1.0 / D, bias=EPS)
            recip(ssb[b][0:R, 1:2], ssb[b][0:R, 1:2], [R_ss[b]], [R_ss[b]])
            stt("dve", hh, hh, ssb[b][0:R, 1:2], gfbc[0:R, :], ALU.mult, ALU.mult,
                [R_h1[ti], R_ss[b], R_gf], [R_h1[ti]])
            S.dma("sp", y[c0:c1, :], hh, reads=[R_h1[ti]])

        for half in range(2):
            if half == 0:
                segs = [(0, 2048, 32), (16, 0, 512), (528, 512, 512)]
                nconv = 1040
            else:
                segs = [(0, 1024, 512), (512, 1536, 512)]
                nconv = 1024
            if half == 0:
                for b_ in range(2):
                    mset("pool", arow[b_][:, 0:2], 0.0, [R_ar[b_]])
            for j in range(NJ):
                wb = (j // 2) % 2
                jj = j % 2
                b = j % 2
                if jj == 0:
                    S.dma("pool", wupb[wb][:, :, 0, :], w_up_v[:, :, 128 * j:128 * j + 256], writes=[R_wup[wb]])
                    S.dma("pool", wupb[wb][:, :, 1, :], w_up_v[:, :, DFF + 128 * j:DFF + 128 * j + 256],
                          writes=[R_wup[wb]])
                for (m0, g0, W) in segs:
                    hr = creg(R_hn2, g0, g0 + W)
                    pa = bank()
                    for kt in range(8):
                        mm(PB(pa, W), wupb[wb][:, kt, 0, 128 * jj:128 * jj + 128], hn2T[:, kt, g0:g0 + W], kt == 0, kt == 7,
                           [R_wup[wb]] + hr, [PR[pa]])
                    pv_ = bank()
                    for kt in range(8):
                        mm(PB(pv_, W), wupb[wb][:, kt, 1, 128 * jj:128 * jj + 128], hn2T[:, kt, g0:g0 + W], kt == 0, kt == 7,
                           [R_wup[wb]] + hr, [PR[pv_]])
                    if g0 == 2048:
                        cp("act", arow[b][:, 2:18], PB(pa)[:, 0:16], [PR[pa]], [R_ar[b]])
                        cp("act", arow[b][:, 2 + 1040:2 + 1056], PB(pa)[:, 16:32], [PR[pa]], [R_ar[b]])
                        cp("act", vrow[b][:, 0:16], PB(pv_)[:, 0:16], [PR[pv_]], [R_vr[b]])
                        cp("act", vrow[b][:, 1040:1056], PB(pv_)[:, 16:32], [PR[pv_]], [R_vr[b]])
                        continue
                    cp("act", arow[b][:, 2 + m0:2 + m0 + W], PB(pa, W), [PR[pa]], [R_ar[b]])
                    cp("act", vrow[b][:, m0:m0 + W], PB(pv_, W), [PR[pv_]], [R_vr[b]])
                if half == 1:
                    cp("act", arow[b][:, 0:2], carry[:, j, :], [R_carry], [R_ar[b]])
                w0 = cwT[:, 0 * NJ + j:0 * NJ + j + 1]
                w1_ = cwT[:, 1 * NJ + j:1 * NJ + j + 1]
                w2_ = cwT[:, 2 * NJ + j:2 * NJ + j + 1]
                cc = crow[b][:, 0:nconv]
                ts("dve", cc, arow[b][:, 2:2 + nconv], w2_, cbT[:, j:j + 1], ALU.mult, ALU.add, [R_ar[b], R_vec], [R_cr[b]])
                stt("dve", cc, arow[b][:, 1:1 + nconv], w1_, cc, ALU.mult, ALU.add, [R_ar[b], R_vec, R_cr[b]], [R_cr[b]])
                stt("dve", cc, arow[b][:, 0:nconv], w0, cc, ALU.mult, ALU.add, [R_ar[b], R_vec, R_cr[b]], [R_cr[b]])
                act(cc, cc, AF.Silu, [R_cr[b]], [R_cr[b]])
                if half == 0:
                    tt("dve", mT[:, j, 0:16], crow[b][:, 0:16], vrow[b][:, 0:16], ALU.mult, [R_cr[b], R_vr[b]], [R_mT])
                    tt("dve", mT[:, j, 32:1056], crow[b][:, 16:1040], vrow[b][:, 16:1040], ALU.mult,
                       [R_cr[b], R_vr[b]], [R_mT])
                    cp("act", carry[:, j, :], arow[b][:, 2 + 1038:2 + 1040], [R_ar[b]], [R_carry])
                    cp("act", aS[:, j, :], arow[b][:, 2 + 1040:2 + 1056], [R_ar[b]], [R_aS])
                    cp("act", vS[:, j, :], vrow[b][:, 1040:1056], [R_vr[b]], [R_aS])
                else:
                    tt("dve", mT[:, j, 0:1024], cc, vrow[b][:, 0:1024], ALU.mult, [R_cr[b], R_vr[b]], [R_mT])
                    cp("act", convp[:, :, j], arow[b][:, 2 + 1022:2 + 1024], [R_ar[b]], [R_convp])
            if half == 0:
                cs1 = crow[0][:, 0:352].rearrange("p (j n) -> p j n", n=16)
                cs2 = crow[1][:, 0:352].rearrange("p (j n) -> p j n", n=16)
                bufv = bufT[:].rearrange("p j (n r) -> p j n r", r=2)

                def wbc(r):
                    return bc(cwT[:, r * NJ:(r + 1) * NJ].unsqueeze(2), [128, NJ, 16])
                rs_ = [R_aS, R_bufT, R_vec, R_cr[0], R_cr[1]]
                ws_ = [R_cr[0], R_cr[1]]
                tt("dve", cs1, aS[:], wbc(2), ALU.mult, rs_, ws_)
                tt("dve", cs1, cs1, bc(cbT.unsqueeze(2), [128, NJ, 16]), ALU.add, rs_, ws_)
                tt("dve", cs2, bufv[:, :, :, 1], wbc(1), ALU.mult, rs_, ws_)
                tt("dve", cs1, cs1, cs2, ALU.add, rs_, ws_)
                tt("dve", cs2, bufv[:, :, :, 0], wbc(0), ALU.mult, rs_, ws_)
                tt("dve", cs1, cs1, cs2, ALU.add, rs_, ws_)
                act(cs1, cs1, AF.Silu, rs_, ws_)
                tt("dve", mT[:, :, 16:32], cs1, vS[:], ALU.mult, rs_, [R_mT])
            if half == 0:
                groups = [(32, 512, [0, 1, 2, 3]), (544, 512, [4, 5, 6, 7]), (0, 32, [16])]
            else:
                groups = [(0, 512, [8, 9, 10, 11]), (512, 512, [12, 13, 14, 15])]
            def pb_mm(ft, gi_):
                m0, Wt, tiles = groups[gi_]
                wb = ft % 2
                if gi_ == 0:
                    S.dma("pool", wdnb[wb][:], w_dn_v[:, :, 128 * ft:128 * ft + 128], writes=[R_wdn[wb]])
                pi = bank()
                yb = (ft * len(groups) + gi_) % 2
                for j in range(NJ):
                    mm(PB(pi, Wt), wdnb[wb][:, j, :], mT[:, j, m0:m0 + Wt], j == 0, j == NJ - 1,
                       [R_wdn[wb], R_mT], [PR[pi]])
                cp("act", yts[yb][:, 0:Wt], PB(pi, Wt), [PR[pi]], [R_yt[yb]])

            def pb_tr(ft, gi_):
                m0, Wt, tiles = groups[gi_]
                yb = (ft * len(groups) + gi_) % 2
                pt = bank()
                for n_, ti in enumerate(tiles):
                    R = 32 if ti == 16 else 128
                    tr(PB(pt)[0:R, 128 * n_:128 * n_ + 128], yts[yb][:, 128 * n_:128 * n_ + R], identf,
                       [R_yt[yb], R_cst], [PR[pt]], signal=(n_ == len(tiles) - 1))
                for n_, ti in enumerate(tiles):
                    R = 32 if ti == 16 else 128
                    hs = h1[0:R, ti, 128 * ft:128 * ft + 128]
                    tt("dve", hs, hs, PB(pt)[0:R, 128 * n_:128 * n_ + 128], ALU.add, [R_h1[ti], PR[pt]], [R_h1[ti]])

            steps = [(ft, gi_) for ft in range(8) for gi_ in range(len(groups))]
            pb_mm(*steps[0])
            for i_ in range(1, len(steps)):
                pb_mm(*steps[i_])
                pb_tr(*steps[i_ - 1])
            pb_tr(*steps[-1])
            for (m0, Wt, tiles) in groups:
                for ti in tiles:
                    final_norm(ti)

        pi = bank()
        tr(PB(pi)[0:44, 0:128], convp[:].rearrange("p r j -> p (r j)"), identf, [R_convp, R_cst], [PR[pi]])
        cvo = A.alloc([44, 128], F32)
        R_cvo = Reg()
        cp("act", cvo[:], PB(pi)[0:44, 0:128], [PR[pi]], [R_cvo])
        for r in range(2):
            S.dma("sp", o_cvp[r].rearrange("(j f) -> j f", f=128), cvo[22 * r:22 * r + 22, :], reads=[R_cvo])
        for grp in range(6):
            pi = bank()
            js = list(range(4 * grp, min(4 * grp + 4, NJ)))
            for n_, j in enumerate(js):
                tr(PB(pi)[0:16, 128 * n_:128 * n_ + 128], aS[:, j, :], identf, [R_aS, R_cst], [PR[pi]],
                   signal=(n_ == len(js) - 1))
            cp("act", cbuf[0:16, 512 * grp:512 * grp + 128 * len(js)], PB(pi)[0:16, 0:128 * len(js)], [PR[pi]], [R_mT])
        S.dma("sp", o_cvs[:, 1, :], cbuf[0:16, :], reads=[R_mT])

        S.finish()
        print("built: ops", S.nops, "waits", S.nwaits, "sbuf peak", A.peak - SB_LO, "/", SB_HI - SB_LO)
    return nc


_CACHE = {}


def kernel(**inp):
    f = lambda k: np.asarray(inp[k], np.float32)
    x_prompt = f("x_prompt")
    x_sample = f("x_sample")
    meta = f("meta_tokens")
    vecs = make_vecs(inp)
    nvec = vecs.shape[1]
    if "nc" not in _CACHE:
        _CACHE["nc"] = build_nc(nvec)
    nc = _CACHE["nc"]
    lam = np.stack([f("s5_lambda_re")[0].reshape(16, 128), f("s5_lambda_im")[0].reshape(16, 128),
                    np.repeat(f("s5_log_dt")[0].reshape(16, 2), 64, axis=1)], axis=0)
    bst = np.stack([f("s5_b_re")[0].reshape(16, 128, 16), f("s5_b_im")[0].reshape(16, 128, 16)], axis=0)
    cch = np.stack([f("s5_c_re")[0].reshape(512, 64), f("s5_c_im")[0].reshape(512, 64)], axis=0)
    shared = {
        "w_in": f("w_in")[0], "w_out": f("w_out")[0], "w_up": f("ffn_w_up")[0], "w_down": f("ffn_w_down")[0],
        "w_glu": f("s5_w_glu")[0], "lamPP": np.ascontiguousarray(lam), "bst": np.ascontiguousarray(bst),
        "cch": np.ascontiguousarray(cch), "vecs": vecs, "hlbrows": f("hg_lower_bounds"),
        "gfinal": f("final_norm_g"), "consts": CONSTS,
    }
    in_maps = []
    for c in range(NCORES):
        sl = slice(16 * c, 16 * c + 16)
        m = dict(shared)
        m["xtok"] = np.ascontiguousarray(np.concatenate([x_prompt[c], meta, x_sample[sl, 0, :]], axis=0))
        m["s5s"] = np.ascontiguousarray(np.stack([f("state_s5_re")[0, sl].reshape(16, 2048),
                                                  f("state_s5_im")[0, sl].reshape(16, 2048)], axis=0))
        m["hgs"] = np.ascontiguousarray(f("state_hgrn")[0, sl])
        m["cvs"] = np.ascontiguousarray(f("state_ffn_conv")[0, sl])
        in_maps.append(m)
    res = run_bass_kernel_spmd(nc, in_maps, core_ids=list(range(NCORES)))
    rs = res.results
    y_prompt = np.stack([rs[c]["y"][0:2048] for c in range(NCORES)], axis=0)
    y_sample = np.concatenate([rs[c]["y"][2064:2080] for c in range(NCORES)], axis=0)[:, None, :]
    s5p_re = np.stack([rs[c]["o_s5p"][0].reshape(32, 64) for c in range(NCORES)], axis=0)[None]
    s5p_im = np.stack([rs[c]["o_s5p"][1].reshape(32, 64) for c in range(NCORES)], axis=0)[None]
    hgp = np.stack([rs[c]["o_hgp"] for c in range(NCORES)], axis=0)[None]
    cvp = np.stack([rs[c]["o_cvp"] for c in range(NCORES)], axis=0)[None]
    s5s_re = np.concatenate([rs[c]["o_s5s"][0].reshape(16, 32, 64) for c in range(NCORES)], axis=0)[None]
    s5s_im = np.concatenate([rs[c]["o_s5s"][1].reshape(16, 32, 64) for c in range(NCORES)], axis=0)[None]
    hgs = np.concatenate([rs[c]["o_hgs"] for c in range(NCORES)], axis=0)[None]
    cvs = np.concatenate([rs[c]["o_cvs"] for c in range(NCORES)], axis=0)[None]
    out = (y_prompt, y_sample, s5p_re, s5p_im, hgp, cvp, s5s_re, s5s_im, hgs, cvs)
    return tuple(np.ascontiguousarray(o, dtype=np.float32) for o in out)
```

```python
import math
K_STOP = ''
import numpy as np
import concourse.bass as bass
import concourse.mybir as mybir
from concourse.bass_utils import run_bass_kernel_spmd
from contextlib import ExitStack

F32 = mybir.dt.float32
BF16 = mybir.dt.bfloat16
AF = mybir.ActivationFunctionType
ALU = mybir.AluOpType
NDS = 48
NCORES = 8
D = 1024
NT = 2080
NTILES = 17
DFF = 2816
NJ = 22
EPS = 1e-6
SB_LO = 16512
SB_HI = 229344
SE = "pool"


class _Stop(Exception):
    pass


class Reg:
    __slots__ = ("w", "r")

    def __init__(self):
        self.w = None
        self.r = {}


class Sched:
    def __init__(self, nc, stack):
        self.nc = nc
        self.eng = {"pe": nc.tensor, "act": nc.scalar, "dve": nc.vector,
                    "pool": nc.gpsimd, "sp": nc.sync}
        self.sem = {k: stack.enter_context(nc.semaphore("s_" + k)) for k in self.eng}
        self.cnt = {k: 0 for k in self.eng}
        self.waited = {k: {} for k in self.eng}
        self.dsem = [stack.enter_context(nc.semaphore("d%d" % i)) for i in range(NDS)]
        self.dcnt = [0] * NDS
        self.dpool = {"sp": list(range(0, 24)), "pool": list(range(24, 40)), "act": list(range(40, NDS))}
        self.dnext = {"sp": 0, "pool": 0, "act": 0}
        self.pending = []
        self.nwaits = 0
        self.nops = 0

    def _wait(self, e, tok):
        sem, val, _ = tok
        assert val is not None, "dependency on unsignaled PE op"
        key = id(sem)
        if self.waited[e].get(key, 0) >= val:
            return
        self.waited[e][key] = val
        self.eng[e].wait_ge(sem, val)
        self.nwaits += 1

    def _deps(self, e, reads, writes):
        for R in reads:
            t = R.w
            if t is not None and not (t[2] == e and e == "pe"):
                self._wait(e, t)
        for R in writes:
            t = R.w
            if t is not None and not (t[2] == e and e == "pe"):
                self._wait(e, t)
            for t in R.r.values():
                if not (t[2] == e and e == "pe"):
                    self._wait(e, t)

    def _register(self, tok, reads, writes):
        for R in reads:
            R.r[id(tok[0])] = tok
        for R in writes:
            R.w = tok
            R.r = {}

    def op(self, e, fn, reads=(), writes=(), signal=True):
        self._deps(e, reads, writes)
        ins = fn(self.eng[e])
        self.nops += 1
        if signal:
            self.cnt[e] += 1
            ins.then_inc(self.sem[e], 1)
            tok = (self.sem[e], self.cnt[e], e)
            if e == "pe":
                for p in self.pending:
                    p[1] = self.cnt[e]
                self.pending = []
        else:
            assert e == "pe"
            tok = [self.sem[e], None, e]
            self.pending.append(tok)
        self._register(tok, reads, writes)
        return tok

    def dma(self, e, out, in_, reads=(), writes=(), **kw):
        pool_ = self.dpool[e]
        k = pool_[self.dnext[e]]
        self.dnext[e] = (self.dnext[e] + 1) % len(pool_)
        if self.dcnt[k] > 0:
            self._wait(e, (self.dsem[k], self.dcnt[k], None))
        self._deps(e, reads, writes)
        ins = self.eng[e].dma_start(out=out, in_=in_, **kw)
        self.dcnt[k] += 16
        ins.then_inc(self.dsem[k], 16)
        tok = (self.dsem[k], self.dcnt[k], None)
        self._register(tok, reads, writes)
        self.nops += 1
        return tok

    def barrier(self):
        for e in self.eng:
            for x in self.eng:
                if x != e and self.cnt[x] > 0:
                    self._wait(e, (self.sem[x], self.cnt[x], x))
            for k in range(NDS):
                if self.dcnt[k] > 0:
                    self._wait(e, (self.dsem[k], self.dcnt[k], None))

    def finish(self):
        e = "sp"
        for k in range(NDS):
            if self.dcnt[k] > 0:
                self._wait(e, (self.dsem[k], self.dcnt[k], None))
        for x in self.eng:
            if x != e and self.cnt[x] > 0:
                self._wait(e, (self.sem[x], self.cnt[x], x))


class Arena:
    def __init__(self, nc):
        self.nc = nc
        self.l = SB_LO
        self.r = SB_HI
        self.n = 0
        self.peak = 0

    def alloc(self, shape, dt, right=False):
        sz = 4 if dt == F32 else 2
        nb = sz
        for s in shape[1:]:
            nb *= s
        nb = (nb + 63) // 64 * 64
        if right:
            self.r -= nb
            off = self.r
        else:
            off = self.l
            self.l += nb
        assert self.l <= self.r, ("SBUF overflow", self.l, self.r)
        self.peak = max(self.peak, self.l + (SB_HI - self.r))
        self.n += 1
        return self.nc.alloc_sbuf_tensor_at("t%d" % self.n, list(shape), dt, offset=off)


def tile_cols(ti):
    return (2048, 2080) if ti == 16 else (128 * ti, 128 * ti + 128)


BLOCKS = [(0, 512), (512, 1024), (1024, 1536), (1536, 2048), (2048, 2080)]


CONST_OFF = {}


def make_consts():
    cols = []
    off = 0

    def add(name, arr):
        nonlocal off
        a = np.zeros((128, arr.shape[1]), np.float32)
        a[:arr.shape[0]] = arr
        CONST_OFF[name] = (off, arr.shape[1])
        cols.append(a)
        off += arr.shape[1]

    add("ident", np.eye(128, dtype=np.float32))
    s = np.arange(128)[:, None] % 64
    c = np.arange(64)[None, :]
    add("attmask", (s <= c).astype(np.float32))
    s = np.arange(128)[:, None]
    c = np.arange(128)[None, :]
    add("mrev", ((s > c) & (s // 64 == c // 64)).astype(np.float32))
    s = np.arange(32)[:, None]
    c = np.arange(32)[None, :]
    add("msmall", ((s > c) & (s < 16) & (c < 16)).astype(np.float32))
    p = np.arange(128)[:, None]
    col = np.arange(128)[None, :]
    add("blockmask", (((col // 16) % 2) == (p // 64)).astype(np.float32))
    col = np.arange(32)[None, :]
    add("blockident", ((p % 32) == col).astype(np.float32))
    n = np.arange(16)[None, :]
    add("onehot", ((p - 16) == n).astype(np.float32))
    col = np.arange(128)[None, :]
    add("qmask", ((col // 32) == (p // 32)).astype(np.float32))
    add("ones", np.ones((128, 128), np.float32))
    return np.concatenate(cols, axis=1)


CONSTS = make_consts()
NCONST = CONSTS.shape[1]

VEC_OFF = {}


def make_vecs(inp):
    cols = []
    off = 0

    def add(name, arr):
        nonlocal off
        VEC_OFF[name] = (off, arr.shape[1])
        cols.append(np.ascontiguousarray(arr, dtype=np.float32))
        off += arr.shape[1]

    def fm(v):
        return np.asarray(v, np.float32).reshape(-1, 128).T

    add("gmix", fm(inp["norm_mix_g"][0]))
    add("gffn", fm(inp["norm_ffn_g"][0]))
    add("hlb0", fm(inp["hg_lower_bounds"][0]))
    add("hlb1", fm(inp["hg_lower_bounds"][1]))
    add("s5d", fm(inp["s5_d"][0]))
    add("bglu", fm(inp["s5_b_glu"][0]))
    add("hgng", fm(inp["hg_norm_g"][0]))
    cw = np.asarray(inp["ffn_conv_w"][0], np.float32)
    add("convw", np.concatenate([fm(cw[r]) for r in range(3)], axis=1))
    add("convb", fm(inp["ffn_conv_b"][0]))
    return np.concatenate(cols, axis=1)


def build_nc(nvec):
    nc = bass.Bass("TRN2", target_bir_lowering=False)

    def din(name, shape):
        return nc.dram_tensor(name, list(shape), F32, kind="ExternalInput").ap()

    def dout(name, shape):
        return nc.dram_tensor(name, list(shape), F32, kind="ExternalOutput").ap()

    xtok = din("xtok", [NT, D])
    w_in = din("w_in", [D, 2560])
    w_out = din("w_out", [D, D])
    w_up = din("w_up", [D, 2 * DFF])
    w_down = din("w_down", [DFF, D])
    w_glu = din("w_glu", [512, 512])
    lamPP = din("lamPP", [3, 16, 128])
    bst = din("bst", [2, 16, 128, 16])
    cch = din("cch", [2, 512, 64])
    vecs = din("vecs", [128, nvec])
    hlbrows = din("hlbrows", [2, 512])
    gfinal = din("gfinal", [D])
    s5s = din("s5s", [2, 16, 2048])
    hgs = din("hgs", [16, 4, 128, 128])
    cvs = din("cvs", [16, 2, DFF])
    consts = din("consts", [128, NCONST])

    y = dout("y", [NT, D])
    o_s5p = dout("o_s5p", [2, 16, 128])
    o_hgp = dout("o_hgp", [4, 128, 128])
    o_cvp = dout("o_cvp", [2, DFF])
    o_s5s = dout("o_s5s", [2, 16, 2048])
    o_hgs = dout("o_hgs", [16, 4, 128, 128])
    o_cvs = dout("o_cvs", [16, 2, DFF])

    with ExitStack() as st:
        S = Sched(nc, st)
        A = Arena(nc)
        PS = nc.alloc_psum_tensor("ps", [128, 4096], F32)
        PR = [Reg() for _ in range(8)]
        pb_state = {"i": 0}

        reserved = set()

        def bank():
            i = pb_state["i"]
            while i in reserved:
                i = (i + 1) % 8
            pb_state["i"] = (i + 1) % 8
            return i

        def bank2():
            i = pb_state["i"]
            if i % 2:
                i = (i + 1) % 8
            while i in reserved or (i + 1) in reserved:
                i = (i + 2) % 8
            pb_state["i"] = (i + 2) % 8
            return i

        def PB(i, w=512, n=1):
            return PS[:, 512 * i:512 * i + w] if n == 1 else PS[:, 512 * i:512 * (i + n)]

        def PBb(i):
            return PS[:, 512 * i:512 * i + 512].bitcast(BF16)

        def mm(out, lhsT, rhs, start, stop, reads, writes, tp=None, signal=None):
            sig = stop if signal is None else signal
            kw = {}
            if tp is not None:
                kw["tile_position"] = tp
            return S.op("pe", lambda e: e.matmul(out, lhsT=lhsT, rhs=rhs, start=start, stop=stop,
                                                 skip_group_check=True, **kw),
                        reads=reads, writes=writes, signal=sig)

        def tr(out, in_, ident, reads, writes, signal=True):
            return S.op("pe", lambda e: e.transpose(out=out, in_=in_, identity=ident),
                        reads=reads, writes=writes, signal=signal)

        def act(out, in_, func, reads, writes, **kw):
            return S.op("act", lambda e: e.activation(out=out, in_=in_, func=func, **kw),
                        reads=reads, writes=writes)

        def tt(eng, out, in0, in1, op, reads, writes):
            return S.op(eng, lambda e: e.tensor_tensor(out=out, in0=in0, in1=in1, op=op),
                        reads=reads, writes=writes)

        def ts(eng, out, in0, s1, s2, op0, op1, reads, writes):
            if op1 is None:
                return S.op(eng, lambda e: e.tensor_scalar(out=out, in0=in0, scalar1=s1, scalar2=None, op0=op0),
                            reads=reads, writes=writes)
            return S.op(eng, lambda e: e.tensor_scalar(out=out, in0=in0, scalar1=s1, scalar2=s2, op0=op0, op1=op1),
                        reads=reads, writes=writes)

        def stt(eng, out, in0, scalar, in1, op0, op1, reads, writes):
            return S.op(eng, lambda e: e.scalar_tensor_tensor(out=out, in0=in0, scalar=scalar, in1=in1,
                                                              op0=op0, op1=op1),
                        reads=reads, writes=writes)

        def cp(eng, out, in_, reads, writes):
            if eng == "act":
                return S.op("act", lambda e: e.copy(out=out, in_=in_), reads=reads, writes=writes)
            return S.op(eng, lambda e: e.tensor_copy(out=out, in_=in_), reads=reads, writes=writes)

        def recip(out, in_, reads, writes):
            return S.op("dve", lambda e: e.reciprocal(out=out, in_=in_), reads=reads, writes=writes)

        def mset(eng, ap, val, writes):
            return S.op(eng, lambda e: e.memset(ap, val), writes=writes)

        def bc(ap, shape):
            return ap.to_broadcast(list(shape))

        cst = A.alloc([128, NCONST], F32)
        R_cst = Reg()
        S.dma("sp", cst[:], consts, writes=[R_cst])
        vec = A.alloc([128, nvec], F32)
        R_vec = Reg()
        S.dma("sp", vec[:], vecs, writes=[R_vec])

        def C(name, rows=128):
            o, w = CONST_OFF[name]
            return cst[0:rows, o:o + w]

        def V(name):
            o, w = VEC_OFF[name]
            return vec[:, o:o + w]

        identb = A.alloc([128, 128], BF16)
        onesb = A.alloc([128, 128], BF16)
        mrevb = A.alloc([128, 128], BF16)
        msmallb = A.alloc([32, 32], BF16)
        R_cb = Reg()
        cp("dve", identb[:], C("ident"), [R_cst], [R_cb])
        cp("dve", onesb[:], C("ones"), [R_cst], [R_cb])
        cp("dve", mrevb[:], C("mrev"), [R_cst], [R_cb])
        cp("dve", msmallb[:], C("msmall", 32), [R_cst], [R_cb])
        identf = C("ident")

        lbT = A.alloc([128, 4], F32)
        omlT = A.alloc([128, 4], F32)
        R_lb = Reg()
        tt("dve", lbT[:], V("hlb1"), V("hlb0"), ALU.subtract, [R_vec], [R_lb])
        act(lbT[:], lbT[:], AF.Exp, [R_lb], [R_lb])
        ts("dve", lbT[:], lbT[:], 1.0, None, ALU.add, None, [R_lb], [R_lb])
        recip(lbT[:], lbT[:], [R_lb], [R_lb])
        ts("dve", omlT[:], lbT[:], -1.0, 1.0, ALU.mult, ALU.add, [R_lb], [R_lb])

        mixT = A.alloc([128, 8, NT], BF16, right=True)
        R_mix = [Reg() for _ in range(NTILES)]

        def creg(regs, c0, c1):
            out = []
            for ti in range(NTILES):
                a, b = tile_cols(ti)
                if a < c1 and c0 < b:
                    out.append(regs[ti])
            return out

        mark_G = A.l

        uT = A.alloc([128, 4, NT], BF16)
        mark_R = A.r
        qT = A.alloc([128, 4, NT], BF16, right=True)
        kT = A.alloc([128, 4, NT], BF16, right=True)
        v_tok = A.alloc([128, NTILES, 512], BF16, right=True)
        kk_tok = A.alloc([128, NTILES, 512], BF16, right=True)
        ebl = A.alloc([128, 4, 33], F32)
        fsamp = A.alloc([128, 4, 16], F32)
        R_u = [Reg() for _ in range(NTILES)]
        R_q = [[Reg() for _ in range(NTILES)] for _ in range(4)]
        R_k = [[Reg() for _ in range(NTILES)] for _ in range(4)]
        R_v = [Reg() for _ in range(NTILES)]
        R_kk = [Reg() for _ in range(NTILES)]
        R_ebl = Reg()
        mark_P1out = A.l

        xnT = A.alloc([128, 8, NT], BF16)
        R_xn = [Reg() for _ in range(NTILES)]
        wtok = A.alloc([128, 8, 512], BF16)
        R_wtok = Reg()
        w_in_v = w_in.rearrange("(kt p) c -> p kt c", p=128)
        S.dma("pool", wtok[:], w_in_v[:, :, 1536:2048], writes=[R_wtok])

        def v_proj(ti):
            c0, c1 = tile_cols(ti)
            R = c1 - c0
            pi = bank()
            for kt in range(8):
                mm(PB(pi)[0:R, :], xnT[:, kt, c0:c1], wtok[:, kt, :], kt == 0, kt == 7,
                   [R_xn[ti], R_wtok], [PR[pi]])
            cp("act", v_tok[0:R, ti, :], PB(pi)[0:R, :], [PR[pi]], [R_v[ti]])

        mark_tmp = A.l
        xt = [A.alloc([128, D], F32) for _ in range(4)]
        xnb = [A.alloc([128, D], BF16) for _ in range(4)]
        junk = A.alloc([128, D], BF16)
        ssb = [A.alloc([128, 2], F32) for _ in range(4)]
        R_xt = [Reg() for _ in range(4)]
        R_xnb = [Reg() for _ in range(4)]
        R_junk = Reg()
        R_ss = [Reg() for _ in range(4)]
        gmixT = V("gmix")
        def p0_A(ti):
            c0, c1 = tile_cols(ti)
            R = c1 - c0
            b = ti % 4
            S.dma("sp", xt[b][0:R, :], xtok[c0:c1, :], writes=[R_xt[b]])
            act(junk[0:R, :], xt[b][0:R, :], AF.Square, [R_xt[b]], [R_junk, R_ss[b]], accum_out=ssb[b][0:R, 0:1])
            act(ssb[b][0:R, 1:2], ssb[b][0:R, 0:1], AF.Sqrt, [R_ss[b]], [R_ss[b]], scale=1.0 / D, bias=EPS)
            recip(ssb[b][0:R, 1:2], ssb[b][0:R, 1:2], [R_ss[b]], [R_ss[b]])
            ts("pool", xnb[b][0:R, :], xt[b][0:R, :], ssb[b][0:R, 1:2], 1.0, ALU.mult, ALU.mult, [R_xt[b], R_ss[b]], [R_xnb[b]])

        def p0_B(ti):
            c0, c1 = tile_cols(ti)
            R = c1 - c0
            b = ti % 4
            pi = bank()
            pv = PBb(pi).rearrange("p (k r) -> p k r", r=128)
            for kt in range(8):
                tr(pv[:, kt, 0:R], xnb[b][0:R, kt * 128:(kt + 1) * 128], identb[0:R, 0:R],
                   [R_xnb[b], R_cb], [PR[pi]], signal=(kt == 7))
            tt("dve", xnT[:, :, c0:c1], pv[:, :, 0:R], bc(gmixT.unsqueeze(2), [128, 8, R]), ALU.mult,
               [PR[pi], R_vec], [R_xn[ti]])

        p0_A(0)
        p0_A(1)
        for ti in range(2, NTILES):
            p0_A(ti)
            p0_B(ti - 2)
            if ti >= 3:
                v_proj(ti - 3)
        p0_B(NTILES - 2)
        v_proj(NTILES - 3)
        p0_B(NTILES - 1)
        v_proj(NTILES - 2)
        v_proj(NTILES - 1)
        S.barrier()
        A.l = mark_tmp

        if K_STOP == 'P0':
            S.finish()
            return nc
        wft = [A.alloc([128, 8, 128], BF16) for _ in range(2)]
        R_wft = [Reg(), Reg()]
        def fm_tile(col0, evac):
            b = fm_tile.n % 2
            fm_tile.n += 1
            S.dma("pool", wft[b][:], w_in_v[:, :, col0:col0 + 128], writes=[R_wft[b]])
            for bi, (c0, c1) in enumerate(BLOCKS):
                W = c1 - c0
                pi = bank()
                for kt in range(8):
                    mm(PB(pi, W), wft[b][:, kt, :], xnT[:, kt, c0:c1], kt == 0, kt == 7,
                       creg(R_xn, c0, c1) + [R_wft[b]], [PR[pi]])
                evac(bi, c0, c1, W, pi)
        fm_tile.n = 0

        def u_tile(t):
            fm_tile(0 + 128 * t, lambda bi, c0, c1, W, pi, t=t:
                    cp("act", uT[:, t, c0:c1], PB(pi, W), [PR[pi]], creg(R_u, c0, c1)))

        def g_tile(h):
            fm_tile(2048 + 128 * h, lambda bi, c0, c1, W, pi, h=h:
                    act(mixT[:, 4 + h, c0:c1], PB(pi, W), AF.Silu, [PR[pi]], creg(R_mix, c0, c1)))

        for h in range(4):
            fm_tile(512 + 128 * h, lambda bi, c0, c1, W, pi, h=h:
                    cp("dve", qT[:, h, c0:c1], PB(pi, W), [PR[pi]], creg(R_q[h], c0, c1)))
        HW_ = 1056
        lgf = [A.alloc([128, HW_], F32) for _ in range(2)]
        bTt = [A.alloc([128, HW_], F32) for _ in range(2)]
        etmp_ = [A.alloc([128, HW_], F32) for _ in range(2)]
        etmp = [etmp_, etmp_]
        smask = A.alloc([128, HW_], BF16)
        R_lgfb = [[Reg(), Reg()], [Reg(), Reg(), Reg()]]
        R_bT = [Reg(), Reg()]
        R_et_ = [Reg(), Reg()]
        R_et = [R_et_, R_et_]
        R_sm = Reg()
        mset("pool", smask[:], 1.0, [R_sm])
        mset("pool", smask[:, 0:1024:64], 0.0, [R_sm])
        mset("pool", smask[:, 1024:1025], 0.0, [R_sm])
        mset("pool", smask[:, 1040:1056], 0.0, [R_sm])

        def f_chains(h):
            ns = (1024, 1056)
            nprs = (1024, 1040)
            for half in range(2):
                n = ns[half]
                rl = R_lgfb[half]
                act(lgf[half][:, 0:n], lgf[half][:, 0:n], AF.Ln, rl, rl)
            for half in range(2):
                n = ns[half]
                S.op("dve", lambda e, half=half, n=n: e.tensor_tensor_scan(
                    out=bTt[half][:, 0:n], data0=smask[:, 0:n], data1=lgf[half][:, 0:n],
                    initial=0.0, op0=ALU.mult, op1=ALU.add),
                     reads=R_lgfb[half] + [R_sm], writes=[R_bT[half]])
            for half in range(2):
                npr, g0 = nprs[half], 1024 * half
                act(etmp_[half][:, 0:npr], bTt[half][:, 0:npr], AF.Exp, [R_bT[half]], [R_et_[half]])
            for half in range(2):
                npr, g0 = nprs[half], 1024 * half
                tt("dve", qT[:, h, g0:g0 + npr], qT[:, h, g0:g0 + npr], etmp_[half][:, 0:npr], ALU.mult,
                   [R_et_[half]] + creg(R_q[h], g0, g0 + npr), creg(R_q[h], g0, g0 + npr))
            for half in range(2):
                npr, g0 = nprs[half], 1024 * half
                act(etmp_[half][:, 0:npr], bTt[half][:, 0:npr], AF.Exp, [R_bT[half]], [R_et_[half]], scale=-1.0)
            for half in range(2):
                npr, g0 = nprs[half], 1024 * half
                tt("dve", kT[:, h, g0:g0 + npr], kT[:, h, g0:g0 + npr], etmp_[half][:, 0:npr], ALU.mult,
                   [R_et_[half]] + creg(R_k[h], g0, g0 + npr), creg(R_k[h], g0, g0 + npr))
            for half in range(2):
                act(ebl[:, h, 1 + 16 * half:17 + 16 * half], bTt[half][:, 63:1024:64], AF.Exp, [R_bT[half]], [R_ebl])
            act(ebl[:, h, 0:1], bTt[1][:, 1039:1040], AF.Exp, [R_bT[1]], [R_ebl])
            act(fsamp[:, h, :], bTt[1][:, 1040:1056], AF.Exp, [R_bT[1]], [R_ebl])

        for h in range(4):
            def evac_f(bi, c0, c1, W, pi, h=h):
                half = 0 if bi < 2 else 1
                l0 = c0 - 1024 * half
                rb = [R_lgfb[half][bi - 2 * half]]
                fl = lgf[half][:, l0:l0 + W]
                act(fl, PB(pi, W), AF.Sigmoid, [PR[pi]], rb)
                ts("dve", fl, fl, omlT[:, h:h + 1], lbT[:, h:h + 1], ALU.mult, ALU.add, rb + [R_lb], rb)
                ts("pool", kT[:, h, c0:c1], fl, -1.0, 1.0, ALU.mult, ALU.add, rb, creg(R_k[h], c0, c1))
            fm_tile(1024 + 128 * h, evac_f)
            f_chains(h)
            u_tile(h)
            g_tile(h)

        kkT = [A.alloc([128, NT], BF16) for _ in range(2)]
        R_kkT = [Reg(), Reg()]
        for h in range(4):
            kb = h % 2
            tt("dve", kkT[kb][:, 0:2048].rearrange("p (c s) -> p c s", s=64),
               kT[:, h, 0:2048].rearrange("p (c s) -> p c s", s=64),
               bc(ebl[:, h, 1:33].unsqueeze(2), [128, 32, 64]), ALU.mult,
               creg(R_k[h], 0, 2048) + [R_ebl], [R_kkT[kb]])
            ts("dve", kkT[kb][:, 2048:2064], kT[:, h, 2048:2064], ebl[:, h, 0:1], None, ALU.mult, None,
               [R_k[h][16], R_ebl], [R_kkT[kb]])
            cp("dve", kkT[kb][:, 2064:2080], kT[:, h, 2064:2080], [R_k[h][16]], [R_kkT[kb]])
            for grp, tiles in enumerate(([0, 1, 2, 3, 4, 5, 6, 7], [8, 9, 10, 11, 12, 13, 14, 15], [16])):
                pi = bank()
                pv = PBb(pi).rearrange("p (j c) -> p j c", c=128)
                for j, ti in enumerate(tiles):
                    c0, c1 = tile_cols(ti)
                    R = c1 - c0
                    tr(pv[0:R, j, :], kkT[kb][:, c0:c1], identb[:, :], [R_kkT[kb], R_cb], [PR[pi]],
                       signal=(j == len(tiles) - 1))
                R = 32 if grp == 2 else 128
                nt_ = len(tiles)
                cp("act", kk_tok[0:R, tiles[0]:tiles[0] + nt_, 128 * h:128 * h + 128], pv[0:R, 0:nt_, :],
                   [PR[pi]], [R_kk[ti_] for ti_ in tiles])

        S.barrier()
        A.l = mark_P1out

        if K_STOP == 'P1':
            S.finish()
            return nc
        def s5t(shape):
            return A.alloc(shape, F32)
        lr = s5t([128, 16]); li = s5t([128, 16]); dtt = s5t([128, 16])
        ar = s5t([128, 16]); ai = s5t([128, 16]); cr = s5t([128, 16]); ci = s5t([128, 16])
        mg16 = s5t([128, 16])
        t1 = s5t([128, 16]); t2 = s5t([128, 16]); t3 = s5t([128, 16]); t4 = s5t([128, 16])
        R_s5 = Reg()
        lpp = A.alloc([16, 3, 128], F32, right=True)
        R_lpp = Reg()
        S.dma("sp", lpp[:], lamPP.rearrange("a r c -> r a c"), writes=[R_lpp])
        pi = bank()
        for a_i in range(3):
            tr(PB(pi)[:, 16 * a_i:16 * a_i + 16], lpp[0:16, a_i, :], identf[0:16, 0:16], [R_lpp, R_cst], [PR[pi]],
               signal=(a_i == 2))
        cp("act", lr[:], PB(pi)[:, 0:16], [PR[pi]], [R_s5])
        cp("act", li[:], PB(pi)[:, 16:32], [PR[pi]], [R_s5])
        negone = s5t([128, 16])
        mset(SE, negone[:], -1.0, [R_s5])
        act(dtt[:], PB(pi)[:, 32:48], AF.Exp, [PR[pi]], [R_s5])
        R_ct = Reg()
        cdup = A.alloc([128, 2, 4, 128], F32, right=True)
        R_cdup = Reg()
        for ri in range(2):
            src = cch[ri].rearrange("(t p) s -> p t s", p=128)
            S.dma("sp", cdup[:, ri, :, 0:64], src, writes=[R_cdup])
            S.dma("sp", cdup[:, ri, :, 64:128], src, writes=[R_cdup])
        ct_re = A.alloc([128, 512], F32)
        nct_im = A.alloc([128, 512], F32)
        bbb = A.alloc([128, 16, 2, 32], BF16)
        for t in range(4):
            pi = bank()
            tr(PB(pi)[:, 0:128], cdup[:, 0, t, :], identf, [R_cdup, R_cst], [PR[pi]], signal=False)
            tr(PB(pi)[:, 128:256], cdup[:, 1, t, :], identf, [R_cdup, R_cst], [PR[pi]])
            cp("act", ct_re[:, 128 * t:128 * t + 128], PB(pi)[:, 0:128], [PR[pi]], [R_ct])
            act(nct_im[:, 128 * t:128 * t + 128], PB(pi)[:, 128:256], AF.Copy, [PR[pi]], [R_ct], scale=-1.0)
            tt(SE, ct_re[:, 128 * t:128 * t + 128], ct_re[:, 128 * t:128 * t + 128], C("blockmask"), ALU.mult,
               [R_ct, R_cst], [R_ct])
            tt(SE, nct_im[:, 128 * t:128 * t + 128], nct_im[:, 128 * t:128 * t + 128], C("blockmask"), ALU.mult,
               [R_ct, R_cst], [R_ct])
        s5r = [R_s5]

        def e_tt(out, a, b, op):
            return tt(SE, out, a, b, op, s5r, s5r)

        e_tt(t1[:], lr[:], dtt[:], ALU.mult)
        act(t2[:], t1[:], AF.Exp, s5r, s5r)
        act(mg16[:], t1[:], AF.Exp, s5r, s5r, scale=16.0)
        e_tt(t1[:], li[:], dtt[:], ALU.mult)
        act(ai[:], t1[:], AF.Sin, s5r, s5r, scale=1.0 / 8)
        act(t3[:], t1[:], AF.Sin, s5r, s5r, scale=1.0 / 16)
        e_tt(t3[:], t3[:], t3[:], ALU.mult)
        ts(SE, ar[:], t3[:], -2.0, 1.0, ALU.mult, ALU.add, s5r, s5r)
        for _ in range(3):
            e_tt(t3[:], ar[:], ar[:], ALU.mult)
            e_tt(t4[:], ai[:], ai[:], ALU.mult)
            e_tt(ai[:], ar[:], ai[:], ALU.mult)
            ts(SE, ai[:], ai[:], 2.0, None, ALU.mult, None, s5r, s5r)
            e_tt(ar[:], t3[:], t4[:], ALU.subtract)
        e_tt(ar[:], ar[:], t2[:], ALU.mult)
        e_tt(ai[:], ai[:], t2[:], ALU.mult)
        ts(SE, t1[:], ar[:], -1.0, None, ALU.add, None, s5r, s5r)
        e_tt(t3[:], lr[:], lr[:], ALU.mult)
        e_tt(t4[:], li[:], li[:], ALU.mult)
        e_tt(t3[:], t3[:], t4[:], ALU.add)
        e_tt(t3[:], t3[:], negone[:], ALU.pow)
        e_tt(cr[:], t1[:], lr[:], ALU.mult)
        e_tt(t4[:], ai[:], li[:], ALU.mult)
        e_tt(cr[:], cr[:], t4[:], ALU.add)
        e_tt(cr[:], cr[:], t3[:], ALU.mult)
        e_tt(ci[:], ai[:], lr[:], ALU.mult)
        e_tt(t4[:], t1[:], li[:], ALU.mult)
        e_tt(ci[:], ci[:], t4[:], ALU.subtract)
        e_tt(ci[:], ci[:], t3[:], ALU.mult)

        def pow_table(tr_, ti_, nmax, tmpa, tmpb):
            mset(SE, tr_[:, :, 0:1], 1.0, s5r)
            mset(SE, ti_[:, :, 0:1], 0.0, s5r)
            n = 1
            while n < nmax:
                m = min(n, nmax - n)
                mr = bc(tr_[:, :, n:n + 1], [128, 16, m])
                mi = bc(ti_[:, :, n:n + 1], [128, 16, m])
                sr = tr_[:, :, 1:1 + m]
                si = ti_[:, :, 1:1 + m]
                ta = tmpa[:, :, 0:m]
                tb = tmpb[:, :, 0:m]
                e_tt(ta, sr, mr, ALU.mult)
                e_tt(tb, si, mi, ALU.mult)
                e_tt(tr_[:, :, n + 1:n + 1 + m], ta, tb, ALU.subtract)
                e_tt(ta, sr, mi, ALU.mult)
                e_tt(tb, si, mr, ALU.mult)
                e_tt(ti_[:, :, n + 1:n + 1 + m], ta, tb, ALU.add)
                n += m

        apr = A.alloc([128, 16, 17], F32)
        api = A.alloc([128, 16, 17], F32)
        tabc = A.alloc([128, 16, 130], F32)
        tabs = A.alloc([128, 16, 130], F32)
        wt1_off = A.l
        wt1 = A.alloc([128, 8, 128], F32)
        wt2_off = A.l
        wt2 = A.alloc([128, 8, 128], F32)
        ptA = wt1[:].rearrange("p k c -> p (k c)").rearrange("p (a n) -> p a n", n=64)
        ptB = wt2[:].rearrange("p k c -> p (k c)").rearrange("p (a n) -> p a n", n=64)
        wst0 = A.alloc([128, 16, 2, 128], BF16)
        pf0 = A.alloc([128, 17, 2, 128], BF16)
        cp(SE, apr[:, :, 1:2], ar[:].unsqueeze(2), s5r, s5r)
        cp(SE, api[:, :, 1:2], ai[:].unsqueeze(2), s5r, s5r)
        pow_table(apr, api, 16, ptA, ptB)
        e_tt(t1[:], mg16[:], negone[:], ALU.pow)
        e_tt(tabc[:, :, 1:2], apr[:, :, 16:17], t1[:].unsqueeze(2), ALU.mult)
        e_tt(tabs[:, :, 1:2], api[:, :, 16:17], t1[:].unsqueeze(2), ALU.mult)
        pow_table(tabc, tabs, 129, ptA, ptB)

        bld = A.alloc([128, 2, 16, 16], F32, right=True)
        R_bld = Reg()
        for ri in range(2):
            S.dma("sp", bld[:, ri, :, :], bst[ri].rearrange("a p c -> p a c"), writes=[R_bld])
        bb_re = A.alloc([128, 16, 32], F32)
        bb_im = A.alloc([128, 16, 32], F32)
        bt1 = A.alloc([128, 16, 16], F32, right=True)
        bt2 = A.alloc([128, 16, 16], F32, right=True)
        mset(SE, bb_re[:], 0.0, s5r)
        mset(SE, bb_im[:], 0.0, s5r)
        rb = s5r + [R_bld]
        for two in range(2):
            ps_ = slice(64 * two, 64 * two + 64)
            cs_ = slice(16 * two, 16 * two + 16)
            crb = bc(cr[ps_, :].unsqueeze(2), [64, 16, 16])
            cib = bc(ci[ps_, :].unsqueeze(2), [64, 16, 16])
            tt(SE, bt1[ps_], bld[ps_, 0], crb, ALU.mult, rb, s5r)
            tt(SE, bt2[ps_], bld[ps_, 1], cib, ALU.mult, rb, s5r)
            tt(SE, bb_re[ps_, :, cs_], bt1[ps_], bt2[ps_], ALU.subtract, s5r, s5r)
            tt(SE, bt1[ps_], bld[ps_, 1], crb, ALU.mult, rb, s5r)
            tt(SE, bt2[ps_], bld[ps_, 0], cib, ALU.mult, rb, s5r)
            tt(SE, bb_im[ps_, :, cs_], bt1[ps_], bt2[ps_], ALU.add, s5r, s5r)

        cp(SE, bbb[:, :, 0, :], bb_re[:], s5r, s5r)
        cp(SE, bbb[:, :, 1, :], bb_im[:], s5r, s5r)

        s5r = [R_s5, R_ct]
        wst = [wst0]
        pf = [pf0]
        R_wst = [Reg(), Reg()]; R_pf = [Reg(), Reg()]
        R_wt12 = Reg()
        def s5_tables(t, part):
            tb = t % 2
            prs = slice(4 * t, 4 * t + 4)
            wgroups = [(0, 8, "pool"), (8, 8, "pool")]
            for (k0_, nk_, eng_) in wgroups:
                if part not in ("wst", "wst_all_pool"):
                    continue
                ks = slice(k0_, k0_ + nk_)
                a_r = bc(apr[:, prs, ks].rearrange("p a k -> p k a").unsqueeze(3), [128, nk_, 4, 32])
                a_i = bc(api[:, prs, ks].rearrange("p a k -> p k a").unsqueeze(3), [128, nk_, 4, 32])
                b_r = bc(bb_re[:, prs, :].unsqueeze(1), [128, nk_, 4, 32])
                b_i = bc(bb_im[:, prs, :].unsqueeze(1), [128, nk_, 4, 32])
                wa, wb_, R_w = wt1, wt2, R_wt12
                w1 = wa[:, 0:nk_, :].rearrange("p k (a c) -> p k a c", c=32)
                w2 = wb_[:, 0:nk_, :].rearrange("p k (a c) -> p k a c", c=32)
                o_r = wst[tb][:, ks, 0, :].rearrange("p k (a c) -> p k a c", c=32)
                o_i = wst[tb][:, ks, 1, :].rearrange("p k (a c) -> p k a c", c=32)
                rw = s5r + [R_w]
                tt(eng_, w1, a_r, b_r, ALU.mult, s5r, [R_w])
                tt(eng_, w2, a_i, b_i, ALU.mult, s5r, [R_w])
                tt(eng_, o_r, w1, w2, ALU.subtract, rw, [R_wst[tb]])
                tt(eng_, w1, a_r, b_i, ALU.mult, rw, [R_w])
                tt(eng_, w2, a_i, b_r, ALU.mult, rw, [R_w])
                tt(eng_, o_i, w1, w2, ALU.add, rw, [R_wst[tb]])
            if part == "up":
                cp("act", uP[tb][:, :, 0:128], uT[:, t, 0:2048].rearrange("p (c s) -> p s c", s=16),
                   creg(R_u, 0, 2048), [R_uP[tb]])
                cp("act", uP[tb][:, :, 128], uT[:, t, 2048:2064], [R_u[16]], [R_uP[tb]])
            for (k0, nk) in ((0, 8), (8, 4), (12, 4), (16, 1)):
                on_pool = (k0 < 16)
                if part not in ("pfp", "dve") or on_pool != (part == "pfp"):
                    continue
                eng_ = "pool" if on_pool else "dve"
                if on_pool:
                    wa, wb_, R_w = wt1, wt2, R_wt12
                else:
                    wa, wb_, R_w = wt3, wt4, R_wt34
                ks = slice(k0, k0 + nk)
                a_r = bc(apr[:, prs, ks].rearrange("p a k -> p k a").unsqueeze(3), [128, nk, 4, 32])
                a_i = bc(api[:, prs, ks].rearrange("p a k -> p k a").unsqueeze(3), [128, nk, 4, 32])
                c_r = bc(ct_re[:, 128 * t:128 * t + 128].rearrange("p (a c) -> p a c", c=32).unsqueeze(1), [128, nk, 4, 32])
                c_i = bc(nct_im[:, 128 * t:128 * t + 128].rearrange("p (a c) -> p a c", c=32).unsqueeze(1), [128, nk, 4, 32])
                w1 = wa[:, 0:nk, :].rearrange("p k (a c) -> p k a c", c=32)
                w2 = wb_[:, 0:nk, :].rearrange("p k (a c) -> p k a c", c=32)
                o_r = pf[tb][:, ks, 0, :].rearrange("p k (a c) -> p k a c", c=32)
                o_i = pf[tb][:, ks, 1, :].rearrange("p k (a c) -> p k a c", c=32)
                rw = s5r + [R_w]
                tt(eng_, w1, a_r, c_r, ALU.mult, s5r, [R_w])
                tt(eng_, w2, a_i, c_i, ALU.mult, s5r, [R_w])
                tt(eng_, o_r, w1, w2, ALU.add, rw, [R_pf[tb]])
                tt(eng_, w1, a_r, c_i, ALU.mult, rw, [R_w])
                tt(eng_, w2, a_i, c_r, ALU.mult, rw, [R_w])
                tt(eng_, o_i, w1, w2, ALU.subtract, rw, [R_pf[tb]])

        s5_tables(0, "wst_all_pool")
        s5_tables(0, "pfp")
        if K_STOP == 'P2base':
            S.finish()
            return nc
        mark_base = A.l
        Sf = A.alloc([128, 4, 128], F32)
        Sb = A.alloc([128, 4, 128], BF16)
        attsb = A.alloc([128, 4, 64], BF16)
        mix_off = SB_HI - 128 * 0 - (8 * NT * 2)
        oblk = nc.alloc_sbuf_tensor_at("oblk", [128, 4, 512], F32, offset=mix_off)
        sqb = nc.alloc_sbuf_tensor_at("sqb", [128, 4, 512], BF16, offset=mix_off + 8192)
        rsb = nc.alloc_sbuf_tensor_at("rsb", [128, 512], F32, offset=mix_off + 12288)
        otmp = nc.alloc_sbuf_tensor_at("otmp", [128, 512], F32, offset=mix_off + 14336)
        R_Sf = Reg(); R_Sb = Reg(); R_att = Reg(); R_ob = Reg(); R_sq = Reg(); R_rs = Reg(); R_ot = Reg()
        hgngT = V("hgng")

        def hg_norm_block(c0, W, rd_extra):
            for h in range(4):
                act(sqb[:, h, 0:W], oblk[:, h, 0:W], AF.Square, [R_ob] + rd_extra, [R_sq])
                pi = bank()
                mm(PB(pi, W), onesb[:, :], sqb[:, h, 0:W], True, True, [R_cb, R_sq], [PR[pi]])
                act(rsb[:, 0:W], PB(pi, W), AF.Ln, [PR[pi]], [R_rs], scale=1.0 / 128, bias=EPS)
                act(rsb[:, 0:W], rsb[:, 0:W], AF.Exp, [R_rs], [R_rs], scale=-0.5)
                stt("dve", otmp[:, 0:W], oblk[:, h, 0:W], hgngT[:, h:h + 1], rsb[:, 0:W], ALU.mult, ALU.mult,
                    [R_ob, R_rs, R_vec], [R_ot])
                mr = creg(R_mix, c0, c0 + W)
                tt("dve", mixT[:, 4 + h, c0:c0 + W], mixT[:, 4 + h, c0:c0 + W], otmp[:, 0:W], ALU.mult,
                   [R_ot] + mr, mr)

        attsbs = [attsb] + [A.alloc([128, 4, 64], BF16) for _ in range(2)]
        R_atts = [R_att, Reg(), Reg()]

        def hg_geo(ci):
            if ci < 0:
                return 2048, 16, 16, 0, 0
            return 64 * ci, 64, ci // 2, 64 * (ci % 2), ci + 1

        hg_state = {}

        def hg_X(ci):
            c0, Wc, ti, r0, ei = hg_geo(ci)
            rows = slice(r0, r0 + Wc)
            sl = (ci + 1) % 3
            pa = bank()
            for h in range(4):
                mm(PB(pa)[rows, 64 * h:64 * h + Wc], kT[:, h, c0:c0 + Wc], qT[:, h, c0:c0 + Wc], True, True,
                   creg(R_k[h], c0, c0 + Wc) + creg(R_q[h], c0, c0 + Wc), [PR[pa]], tp=(0, r0), signal=(h == 3))
            tt("dve", attsbs[sl][rows, :, 0:Wc], PB(pa)[rows, 0:256].rearrange("p (h c) -> p h c", c=64)[:, :, 0:Wc],
               bc(C("attmask")[rows, 0:Wc].unsqueeze(1), [Wc, 4, Wc]), ALU.mult, [PR[pa], R_cst], [R_atts[sl]])

        def hg_Y(ci):
            c0, Wc, ti, r0, ei = hg_geo(ci)
            rows = slice(r0, r0 + Wc)
            pS = bank()
            for h in range(4):
                hc = slice(128 * h, 128 * h + 128)
                mm(PB(pS)[:, hc], kk_tok[rows, ti, hc], v_tok[rows, ti, hc], True, True,
                   [R_kk[ti], R_v[ti]], [PR[pS]], tp=(r0, 0), signal=(h == 3))
            hg_state[ci] = pS
            reserved.add(pS)

        def hg_Z(ci):
            c0, Wc, ti, r0, ei = hg_geo(ci)
            rows = slice(r0, r0 + Wc)
            sl = (ci + 1) % 3
            po = bank()
            for h in range(4):
                hc = slice(128 * h, 128 * h + 128)
                if ci >= 0:
                    mm(PB(po)[:, 64 * h:64 * h + Wc], Sb[:, h, :], qT[:, h, c0:c0 + Wc], True, False,
                       [R_Sb] + creg(R_q[h], c0, c0 + Wc), [PR[po]])
                mm(PB(po)[:, 64 * h:64 * h + Wc], v_tok[rows, ti, hc], attsbs[sl][rows, h, 0:Wc], ci < 0, True,
                   [R_v[ti], R_atts[sl]], [PR[po]], tp=(r0, 0), signal=(h == 3))
            ocol = (c0 % 512) if ci >= 0 else 0
            cp("act", oblk[:, :, ocol:ocol + Wc], PB(po)[:, 0:256].rearrange("p (h c) -> p h c", c=64)[:, :, 0:Wc],
               [PR[po]], [R_ob])
            pS = hg_state.pop(ci)
            reserved.discard(pS)
            if ci < 0:
                cp("dve", Sf[:].rearrange("p h v -> p (h v)"), PB(pS)[:, 0:512], [PR[pS]], [R_Sf])
            else:
                for h in range(4):
                    stt("dve", Sf[:, h, :], Sf[:, h, :], ebl[:, h, ei:ei + 1], PB(pS)[:, 128 * h:128 * h + 128],
                        ALU.mult, ALU.add, [R_Sf, R_ebl, PR[pS]], [R_Sf])
            cp("act", Sb[:].rearrange("p h v -> p (h v)"), Sf[:].rearrange("p h v -> p (h v)"), [R_Sf], [R_Sb])

        NSB = 4
        s0 = [A.alloc([128, 4, 128], F32) for _ in range(NSB)]
        s1b = [A.alloc([128, 4, 128], BF16) for _ in range(2)]
        kkm = [A.alloc([32, 512], BF16) for _ in range(3)]
        R_s0 = [Reg() for _ in range(NSB)]
        R_s1b = [Reg(), Reg()]
        R_kkm = [Reg(), Reg(), Reg()]

        def hg_sample_load(n):
            S.dma("sp", s0[n % NSB][:], hgs[n].rearrange("h k v -> k h v"), writes=[R_s0[n % NSB]])

        def hg_kkm(n):
            act(kkm[n % 3][0:32, :], kk_tok[0:32, 16, :], AF.Copy, [R_kk[16], R_cst], [R_kkm[n % 3]],
                scale=C("onehot", 32)[:, n:n + 1])

        def hg_s1(n):
            b = n % 2
            kb = n % 3
            sb_ = n % NSB
            if n >= NSB:
                hg_sample_load(n)
            if n + 2 < 16:
                hg_kkm(n + 2)
            pk = bank()
            for h in range(4):
                hc = slice(128 * h, 128 * h + 128)
                mm(PB(pk)[:, hc], kkm[kb][0:32, hc], v_tok[0:32, 16, hc], True, True,
                   [R_kkm[kb], R_v[16]], [PR[pk]], signal=(h == 3))
            s0f = s0[sb_][:].rearrange("p h v -> p (h v)")
            tt("dve", s0[sb_][:], s0[sb_][:], bc(fsamp[:, :, n:n + 1], [128, 4, 128]), ALU.mult,
               [R_s0[sb_], R_ebl], [R_s0[sb_]])
            tt("dve", s0f, s0f, PB(pk)[:, 0:512], ALU.add, [R_s0[sb_], PR[pk]], [R_s0[sb_]])
            cp("act", s1b[b][:].rearrange("p h v -> p (h v)"), s0f, [R_s0[sb_]], [R_s1b[b]])
            S.dma("act", o_hgs[n].rearrange("h k v -> k h v"), s0[sb_][:], reads=[R_s0[sb_]])

        def hg_s2(n, po):
            b = n % 2
            for h in range(4):
                mm(PB(po)[:, 16 * h + n:16 * h + n + 1], s1b[b][:, h, :], qT[:, h, 2064 + n:2065 + n], True, True,
                   [R_s1b[b], R_q[h][16]], [PR[po]], signal=(h == 3))

        def hg_samples(n_list, po):
            hg_kkm(0)
            hg_kkm(1)
            hg_s1(n_list[0])
            for i_ in range(1, len(n_list)):
                hg_s1(n_list[i_])
                hg_s2(n_list[i_ - 1], po)
            hg_s2(n_list[-1], po)

        for n_ in range(NSB):
            hg_sample_load(n_)

        ometa = A.alloc([128, 4, 32], F32)
        R_om = Reg()
        order = [-1] + list(range(32))
        LA = 2
        for i_ in range(min(LA, len(order))):
            hg_X(order[i_])
            hg_Y(order[i_])
        po_s = bank()
        reserved.add(po_s)
        hg_kkm(0)
        hg_kkm(1)
        ns_done = 0
        for i_, ci in enumerate(order):
            if i_ + LA < len(order):
                hg_X(order[i_ + LA])
                hg_Y(order[i_ + LA])
            hg_Z(ci)
            if ci < 0:
                cp("dve", ometa[:, :, 0:16], oblk[:, :, 0:16], [R_ob], [R_om])
            elif ci % 8 == 7:
                hg_norm_block(512 * (ci // 8), 512, [])
            if ci >= 0 and ci % 2 == 0 and ns_done < 16:
                hg_s1(ns_done)
                if ns_done >= 1:
                    hg_s2(ns_done - 1, po_s)
                ns_done += 1
        hg_s2(15, po_s)
        reserved.discard(po_s)
        for h in range(4):
            S.dma("sp", o_hgp[h], Sf[:, h, :], reads=[R_Sf])
        cp("act", ometa[:, :, 16:32], PB(po_s)[:, 0:64].rearrange("p (h n) -> p h n", n=16), [PR[po_s]], [R_om])
        cp("dve", oblk[:, :, 0:32], ometa[:, :, :], [R_om, R_ob], [R_ob])
        hg_norm_block(2048, 32, [])

        S.barrier()
        A.l = mark_base
        A.r = mark_R
        if K_STOP == 'P2a':
            S.finish()
            return nc
        s5tok = A.alloc([16, 2, 2048], F32)
        R_s5tok = Reg()
        S.dma("sp", s5tok[:], s5s.rearrange("r n c -> n r c"), writes=[R_s5tok])
        h0 = A.alloc([128, 16, 2, 16], F32)
        h0b = A.alloc([128, 16, 2, 16], BF16)
        R_h0 = Reg()
        for half in range(2):
            pi = bank()
            for pl in range(8):
                pair = 8 * half + pl
                for ri in range(2):
                    tr(PB(pi)[:, (pl * 2 + ri) * 16:(pl * 2 + ri) * 16 + 16], s5tok[0:16, ri, pair * 128:(pair + 1) * 128],
                       identf[0:16, 0:16], [R_s5tok, R_cst], [PR[pi]], signal=(pl == 7 and ri == 1))
            cp("act", h0[:, 8 * half:8 * half + 8, :, :].rearrange("p a r n -> p (a r n)"), PB(pi)[:, 0:256],
               [PR[pi]], [R_h0])
        cp("dve", h0b[:].rearrange("p a r n -> p (a r n)"), h0[:].rearrange("p a r n -> p (a r n)"), [R_h0], [R_h0])

        wst.append(A.alloc([128, 16, 2, 128], BF16))
        pf.append(A.alloc([128, 17, 2, 128], BF16))
        wd = A.alloc([128, 16, 2, 128], BF16)
        knf = [A.alloc([128, 16, 128], BF16) for _ in range(2)]
        wt3 = A.alloc([128, 4, 128], F32)
        wt4 = A.alloc([128, 4, 128], F32)
        dsb = A.alloc([128, 4, 2, 146], F32)
        drot = A.alloc([128, 4, 2, 128], F32)
        gsc = A.alloc([128, 4, 2, 128], F32)
        sbuf_ = A.alloc([128, 4, 2, 129], F32)
        sfar = A.alloc([128, 4, 2, 128], BF16)
        dt1 = A.alloc([128, 4, 128], F32)
        dt2 = A.alloc([128, 4, 128], F32)
        g1 = A.alloc([128, 4, 2], F32)
        sfin = A.alloc([128, 2, 16], F32)
        s1s = A.alloc([128, 16, 2, 16], F32)
        uP = [A.alloc([128, 16, 130], BF16) for _ in range(2)]
        R_uP = [Reg(), Reg()]
        gtmps = [A.alloc([128, 512], F32) for _ in range(2)]
        R_gts = [Reg(), Reg()]
        gtmp = gtmps[0]
        R_gt = R_gts[0]
        R_wd = Reg(); R_knf = [Reg(), Reg()]
        R_wt34 = Reg()
        ktmp_t = A.alloc([128, 512], F32)
        R_kt = Reg()
        R_d = Reg(); R_sfar = Reg(); R_sfin = Reg(); R_s1s = Reg()
        sdT = V("s5d")
        w_glu_sb = A.alloc([128, 4, 512], BF16)
        R_wglu = Reg()
        S.dma("pool", w_glu_sb[:], w_glu.rearrange("(kt p) c -> p kt c", p=128), writes=[R_wglu])

        def s5_F(t):
            tb = t % 2
            prs = slice(4 * t, 4 * t + 4)
            for grp in range(4):
                pi = bank()
                pv = PBb(pi).rearrange("p (j c) -> p j c", c=128)
                for j in range(8):
                    idx = grp * 8 + j
                    k, ri = idx // 2, idx % 2
                    tr(pv[:, j, :], wst[tb][:, k, ri, :], identb[:, :], [R_wst[tb], R_cb], [PR[pi]], signal=(j == 7))
                cp("act", wd[:, 4 * grp:4 * grp + 4, :, :].rearrange("p k r c -> p (k r c)"), PBb(pi)[:, :],
                   [PR[pi]], [R_wd])
            dbanks = [bank(), bank(), bank(), bank()]
            ur = R_u
            for q in range(4):
                for ri in range(2):
                    db = dbanks[q]
                    o0 = 160 * ri
                    for s_ in range(16):
                        mm(PB(db)[:, o0:o0 + 129], wd[32 * q:32 * q + 32, 15 - s_, ri, :],
                           uP[tb][32 * q:32 * q + 32, s_, 0:129], s_ == 0, s_ == 15, [R_wd, R_uP[tb]], [PR[db]],
                           tp=(32 * q, 0))
                    mm(PB(db)[:, o0 + 130:o0 + 146], wd[32 * q:32 * q + 32, 0, ri, :],
                       uT[32 * q:32 * q + 32, t, 2064:2080], True, True, [R_wd, R_u[16]], [PR[db]], tp=(32 * q, 0))
            for q in range(4):
                db = dbanks[q]
                cp("act", dsb[:, q, :, :], PB(db)[:, 0:320].rearrange("p (r c) -> p r c", c=160)[:, :, 0:146],
                   [PR[db]], [R_d])

        def s5_K(t):
            tb = t % 2
            prs = slice(4 * t, 4 * t + 4)
            pi = bank()
            for q in range(4):
                pair = 4 * t + q
                for ri in range(2):
                    mm(PB(pi)[32 * q:32 * q + 32, :], bbb[:, pair, ri, :], pf[tb][:, 0:16, ri, 32 * q:32 * q + 32],
                       ri == 0, ri == 1, [R_pf[tb], R_s5], [PR[pi]], tp=(0, 32 * q), signal=(q == 3 and ri == 1))
            ktmp = ktmp_t[:, :]
            cp("act", ktmp, PB(pi)[:, :], [PR[pi]], [R_kt])
            stt("dve", ktmp[:, 0:32], C("blockident"), sdT[:, t:t + 1], ktmp[:, 0:32], ALU.mult, ALU.add,
                [R_kt, R_cst, R_vec], [R_kt])
            tt("dve", knf[tb][:].rearrange("p k (a c) -> p k a c", c=32),
               bc(ktmp.rearrange("p (k c) -> p k c", c=32).unsqueeze(2), [128, 16, 4, 32]),
               bc(C("qmask").rearrange("p (a c) -> p a c", c=32).unsqueeze(1), [128, 16, 4, 32]), ALU.mult,
               [R_kt, R_cst], [R_knf[tb]])

        def s5_M(t):
            tb = t % 2
            prs = slice(4 * t, 4 * t + 4)
            rd = [R_d] + s5r
            tcs = tabc[:, prs, 2:130]
            tss = tabs[:, prs, 2:130]
            dre = dsb[:, :, 0, 0:128]
            dim_ = dsb[:, :, 1, 0:128]
            tt("dve", dt1[:], dre, tcs, ALU.mult, rd, [R_d])
            tt("dve", dt2[:], dim_, tss, ALU.mult, rd, [R_d])
            tt("dve", drot[:, :, 0, :], dt1[:], dt2[:], ALU.add, rd, [R_d])
            tt("dve", dt1[:], dim_, tcs, ALU.mult, rd, [R_d])
            tt("dve", dt2[:], dre, tss, ALU.mult, rd, [R_d])
            tt("dve", drot[:, :, 1, :], dt1[:], dt2[:], ALU.subtract, rd, [R_d])
            tc1 = tabc[:, prs, 1]
            ts1 = tabs[:, prs, 1]
            dmr = dsb[:, :, 0, 128]
            dmi = dsb[:, :, 1, 128]
            tt("dve", dt1[:, :, 0], dmr, tc1, ALU.mult, rd, [R_d])
            tt("dve", dt2[:, :, 0], dmi, ts1, ALU.mult, rd, [R_d])
            tt("dve", g1[:, :, 0], dt1[:, :, 0], dt2[:, :, 0], ALU.add, rd, [R_d])
            tt("dve", dt1[:, :, 0], dmi, tc1, ALU.mult, rd, [R_d])
            tt("dve", dt2[:, :, 0], dmr, ts1, ALU.mult, rd, [R_d])
            tt("dve", g1[:, :, 1], dt1[:, :, 0], dt2[:, :, 0], ALU.subtract, rd, [R_d])
            for q in range(4):
                pair = 4 * t + q
                for ri in range(2):
                    S.op("dve", lambda e, q=q, ri=ri, pair=pair: e.tensor_tensor_scan(
                        out=gsc[:, q, ri, :], data0=bc(mg16[:, pair:pair + 1], [128, 128]), data1=drot[:, q, ri, :],
                        initial=g1[:, q, ri:ri + 1], op0=ALU.mult, op1=ALU.add), reads=rd, writes=[R_d])
            gre = gsc[:, :, 0, :]
            gim = gsc[:, :, 1, :]
            tt("dve", dt1[:], gre, tcs, ALU.mult, rd, [R_d])
            tt("dve", dt2[:], gim, tss, ALU.mult, rd, [R_d])
            tt("dve", sbuf_[:, :, 0, 1:129], dt1[:], dt2[:], ALU.subtract, rd, [R_d])
            tt("dve", dt1[:], gim, tcs, ALU.mult, rd, [R_d])
            tt("dve", dt2[:], gre, tss, ALU.mult, rd, [R_d])
            tt("dve", sbuf_[:, :, 1, 1:129], dt1[:], dt2[:], ALU.add, rd, [R_d])
            cp("dve", sbuf_[:, :, :, 0], dsb[:, :, :, 128], rd, [R_d])
            cp("dve", sfar[:], sbuf_[:, :, :, 0:128], rd, [R_sfar])
            cp("dve", sfin[:, :, prs].rearrange("p r a -> p a r"), sbuf_[:, :, :, 128], rd, [R_sfin])
            arb = bc(ar[:, prs].unsqueeze(2), [128, 4, 16])
            aib = bc(ai[:, prs].unsqueeze(2), [128, 4, 16])
            h0r = h0[:, prs, 0, :]
            h0i = h0[:, prs, 1, :]
            e1 = dt1[:, :, 0:16]
            e2 = dt2[:, :, 0:16]
            rh = rd + [R_h0]
            tt("dve", e1, h0r, arb, ALU.mult, rh, [R_d])
            tt("dve", e2, h0i, aib, ALU.mult, rh, [R_d])
            tt("dve", e1, e1, e2, ALU.subtract, rd, [R_d])
            tt("dve", s1s[:, prs, 0, :], e1, dsb[:, :, 0, 130:146], ALU.add, rd, [R_s1s])
            tt("dve", e1, h0i, arb, ALU.mult, rh, [R_d])
            tt("dve", e2, h0r, aib, ALU.mult, rh, [R_d])
            tt("dve", e1, e1, e2, ALU.add, rd, [R_d])
            tt("dve", s1s[:, prs, 1, :], e1, dsb[:, :, 1, 130:146], ALU.add, rd, [R_s1s])

        def s5_B(t):
            tb = t % 2
            prs = slice(4 * t, 4 * t + 4)

            def gelu(x, g_a, mo, pr, mregs, R_g):
                act(g_a, x, AF.Square, [pr], [R_g])
                act(g_a, g_a, AF.Identity, [R_g], [R_g], scale=0.044715, bias=1.0)
                tt("dve", g_a, g_a, x, ALU.mult, [R_g, pr], [R_g])
                act(g_a, g_a, AF.Sigmoid, [R_g], [R_g], scale=1.5957691216057308)
                tt("dve", mo, g_a, x, ALU.mult, [R_g, pr], mregs)
            uv = uP[tb][:, :, 0:128]
            ureg = [R_uP[tb]]
            mview = mixT[:, t, 0:2048].rearrange("p (c s) -> p c s", s=16)
            for b_ in range(4):
                yb_ = bank()
                for k in range(0, 4 * b_ + 4):
                    s_lo = max(k, 4 * b_)
                    mm(PB(yb_)[:, (s_lo - 4 * b_) * 128:512], knf[tb][:, k, :], uv[:, s_lo - k:4 * b_ + 4 - k, :],
                       k == 0, False, [R_knf[tb]] + ureg, [PR[yb_]])
                for k in range(4 * b_, 4 * b_ + 4):
                    for q in range(4):
                        for ri in range(2):
                            last = (k == 4 * b_ + 3 and q == 3 and ri == 1)
                            mm(PB(yb_)[32 * q:32 * q + 32, (k - 4 * b_) * 128:(k - 4 * b_) * 128 + 128],
                               pf[tb][:, k + 1, ri, 32 * q:32 * q + 32], sfar[:, q, ri, :], False, last,
                               [R_pf[tb], R_sfar], [PR[yb_]], tp=(0, 32 * q), signal=last)
                gelu(PB(yb_).rearrange("p (s c) -> p c s", c=128),
                     gtmps[b_ % 2][:, :].rearrange("p (c s) -> p c s", s=4),
                     mview[:, :, 4 * b_:4 * b_ + 4], PR[yb_], creg(R_mix, 0, 2048), R_gts[b_ % 2])
            p4 = bank()
            ureg4 = [R_u[16]]
            mm(PB(p4, 32), knf[tb][:, 0, :], uT[:, t, 2048:2080], True, False, [R_knf[tb]] + ureg4, [PR[p4]])
            for k in range(1, 16):
                mm(PB(p4)[:, k:16], knf[tb][:, k, :], uT[:, t, 2048:2048 + 16 - k], False, False,
                   [R_knf[tb]] + ureg4, [PR[p4]])
            for q in range(4):
                pair = 4 * t + q
                for ri in range(2):
                    last = (q == 3 and ri == 1)
                    mm(PB(p4)[32 * q:32 * q + 32, 16:32], pf[tb][:, 1, ri, 32 * q:32 * q + 32],
                       h0b[:, pair, ri, :], False, last, [R_pf[tb], R_h0], [PR[p4]], tp=(0, 32 * q), signal=last)
            gelu(PB(p4, 32), gtmps[0][:, 0:32], mixT[:, t, 2048:2080], PR[p4], [R_mix[16]], R_gts[0])

        if K_STOP == 'T0':
            S.finish()
            return nc
        try:
            s5_tables(0, "up")
            s5_tables(0, "dve")
            s5_tables(1, "wst")
            s5_tables(1, "pfp")
            s5_tables(1, "up")
            s5_F(0)
            s5_K(0)
            for t in range(4):
                s5_M(t)
                if t + 1 < 4:
                    s5_tables(t + 1, "dve")
                    if t + 2 < 4:
                        s5_tables(t + 2, "wst")
                    s5_F(t + 1)
                s5_B(t)
                if t + 2 < 4:
                    s5_tables(t + 2, "pfp")
                    s5_tables(t + 2, "up")
                if t + 1 < 4:
                    s5_K(t + 1)
        except _Stop:
            S.finish()
            return nc

        gates = [nc.alloc_sbuf_tensor_at("gate0", [128, 4, 512], BF16, offset=wt1_off),
                 nc.alloc_sbuf_tensor_at("gate1", [128, 4, 512], BF16, offset=wt2_off)]
        R_gates = [R_wt12, Reg()]
        bgT = V("bglu")
        for bi, (c0, c1) in enumerate(BLOCKS):
            W = c1 - c0
            mr = creg(R_mix, c0, c1)
            gate = gates[bi % 2]
            R_gate = R_gates[bi % 2]
            for to in range(4):
                pi = bank()
                for kt in range(4):
                    mm(PB(pi, W), w_glu_sb[:, kt, 128 * to:128 * to + 128], mixT[:, kt, c0:c1], kt == 0, kt == 3,
                       [R_wglu] + mr, [PR[pi]])
                act(gate[:, to, 0:W], PB(pi, W), AF.Sigmoid, [PR[pi], R_vec], [R_gate], bias=bgT[:, to:to + 1])
            tt("dve", mixT[:, 0:4, c0:c1], mixT[:, 0:4, c0:c1], gate[:, :, 0:W], ALU.mult, [R_gate] + mr, mr)

        pi = bank()
        for ri in range(2):
            tr(PB(pi)[0:16, 128 * ri:128 * ri + 128], sfin[:, ri, :], identf, [R_sfin, R_cst], [PR[pi]], signal=(ri == 1))
        so = A.alloc([16, 256], F32)
        R_so = Reg()
        cp("act", so[:], PB(pi)[0:16, 0:256], [PR[pi]], [R_so])
        for ri in range(2):
            S.dma("sp", o_s5p[ri], so[0:16, 128 * ri:128 * ri + 128], reads=[R_so])
        sos = s5tok
        R_sos = R_s5tok
        for ri in range(2):
            for qd_ in range(4):
                pi = bank()
                for j in range(4):
                    pair = 4 * qd_ + j
                    tr(PB(pi)[0:16, 128 * j:128 * j + 128], s1s[:, pair, ri, :], identf, [R_s1s, R_cst], [PR[pi]],
                       signal=(j == 3))
                cp("act", sos[0:16, ri, 512 * qd_:512 * qd_ + 512], PB(pi)[0:16, :], [PR[pi]], [R_sos])
        S.dma("sp", o_s5s.rearrange("r n c -> n r c"), sos[:], reads=[R_sos])

        S.barrier()
        A.l = mark_G

        if K_STOP == 'P2b':
            S.finish()
            return nc
        h1 = A.alloc([128, NTILES, D], F32)
        R_h1 = [Reg() for _ in range(NTILES)]
        hn2T = A.alloc([128, 8, NT], BF16)
        R_hn2 = [Reg() for _ in range(NTILES)]
        junk = A.alloc([128, D], BF16)
        ssb = [A.alloc([128, 2], F32) for _ in range(4)]
        R_junk = Reg()
        R_ss = [Reg() for _ in range(4)]
        mark_p3 = A.l
        w_out_sb = A.alloc([128, 8, D], BF16)
        R_woutk = [Reg() for _ in range(4)]
        w_out_v = w_out.rearrange("(kt p) c -> p kt c", p=128)
        for g_ in range(4):
            S.dma("pool", w_out_sb[:, 2 * g_:2 * g_ + 2, :], w_out_v[:, 2 * g_:2 * g_ + 2, :], writes=[R_woutk[g_]])
        xt = [A.alloc([128, D], F32) for _ in range(4)]
        xnb = [A.alloc([128, D], BF16) for _ in range(4)]
        R_xt = [Reg() for _ in range(4)]
        R_xnb = [Reg() for _ in range(4)]
        gffnT = V("gffn")
        def p3_A(ti):
            c0, c1 = tile_cols(ti)
            R = c1 - c0
            b = ti % 4
            S.dma("sp", xt[b][0:R, :], xtok[c0:c1, :], writes=[R_xt[b]])
            p2 = bank2()
            for hf in range(2):
                for kt in range(8):
                    mm(PB(p2 + hf)[0:R, :], mixT[:, kt, c0:c1], w_out_sb[:, kt, 512 * hf:512 * hf + 512], kt == 0, kt == 7,
                       [R_mix[ti], R_woutk[kt // 2]], [PR[p2 + hf]])
            hh = h1[0:R, ti, :]
            tt("dve", hh, PB(p2, n=2)[0:R, :], xt[b][0:R, :], ALU.add, [PR[p2], PR[p2 + 1], R_xt[b]], [R_h1[ti]])
            act(junk[0:R, :], hh, AF.Square, [R_h1[ti]], [R_junk, R_ss[b]], accum_out=ssb[b][0:R, 0:1])
            act(ssb[b][0:R, 1:2], ssb[b][0:R, 0:1], AF.Sqrt, [R_ss[b]], [R_ss[b]], scale=1.0 / D, bias=EPS)
            recip(ssb[b][0:R, 1:2], ssb[b][0:R, 1:2], [R_ss[b]], [R_ss[b]])
            ts("pool", xnb[b][0:R, :], hh, ssb[b][0:R, 1:2], 1.0, ALU.mult, ALU.mult, [R_h1[ti], R_ss[b]], [R_xnb[b]])

        def p3_B(ti):
            c0, c1 = tile_cols(ti)
            R = c1 - c0
            b = ti % 4
            pi = bank()
            pv = PBb(pi).rearrange("p (k r) -> p k r", r=128)
            for kt in range(8):
                tr(pv[:, kt, 0:R], xnb[b][0:R, kt * 128:(kt + 1) * 128], identb[0:R, 0:R],
                   [R_xnb[b], R_cb], [PR[pi]], signal=(kt == 7))
            tt("dve", hn2T[:, :, c0:c1], pv[:, :, 0:R], bc(gffnT.unsqueeze(2), [128, 8, R]), ALU.mult,
               [PR[pi], R_vec], [R_hn2[ti]])

        p3_A(0)
        p3_A(1)
        for ti in range(2, NTILES):
            p3_A(ti)
            p3_B(ti - 2)
        p3_B(NTILES - 2)
        p3_B(NTILES - 1)
        S.barrier()
        A.l = mark_p3
        A.r = SB_HI

        if K_STOP == 'P3':
            S.finish()
            return nc
        HWM = 1056
        mT_off = A.l
        mT = A.alloc([128, NJ, HWM], BF16)
        R_mT = Reg()
        cbuf = nc.alloc_sbuf_tensor_at("cbuf", [32, DFF], F32, offset=mT_off)
        wup_off = [A.l, A.l + 8192]
        wupb = [A.alloc([128, 8, 2, 256], BF16) for _ in range(2)]
        R_wup = [Reg() for _ in range(2)]
        wdnb = [nc.alloc_sbuf_tensor_at("wdn%d" % i, [128, NJ, 128], BF16, offset=wup_off[i]) for i in range(2)]
        R_wdn = R_wup
        arow = [A.alloc([128, 2 + HWM], F32) for _ in range(2)]
        crow = [A.alloc([128, 1040], F32) for _ in range(2)]
        vrow = [A.alloc([128, HWM], BF16) for _ in range(2)]
        R_ar = [Reg(), Reg()]
        R_cr = [Reg(), Reg()]
        R_vr = [Reg(), Reg()]
        carry = A.alloc([128, NJ, 2], F32)
        convp = A.alloc([128, 2, NJ], F32)
        aS = A.alloc([128, NJ, 16], F32)
        vS = A.alloc([128, NJ, 16], BF16)
        R_carry = Reg(); R_convp = Reg(); R_aS = Reg()
        yts = [A.alloc([128, 512], F32) for _ in range(2)]
        R_yt = [Reg(), Reg()]
        gfbc = A.alloc([128, D], F32)
        R_gf = Reg()
        S.dma("sp", gfbc[:], gfinal.partition_broadcast(128), writes=[R_gf])
        cwT = V("convw")
        cbT = V("convb")
        S.dma("sp", cbuf[:], cvs.rearrange("n r c -> (n r) c"), writes=[R_mT])
        bufT = A.alloc([128, NJ, 32], F32)
        R_bufT = Reg()
        for grp in range(2):
            pi = bank2()
            for jj in range(11):
                j = grp * 11 + jj
                tr(PB(pi, n=2)[:, 32 * jj:32 * jj + 32], cbuf[0:32, 128 * j:128 * j + 128], identf[0:32, 0:32],
                   [R_mT, R_cst], [PR[pi], PR[pi + 1]], signal=(jj == 10))
            cp("act", bufT[:, 11 * grp:11 * grp + 11, :].rearrange("p j c -> p (j c)"), PB(pi, n=2)[:, 0:352],
               [PR[pi], PR[pi + 1]], [R_bufT])
        S.dma("sp", o_cvs[:, 0, :], cvs[:, 1, :])

        w_up_v = w_up.rearrange("(kt p) c -> p kt c", p=128)
        w_dn_v = w_down.rearrange("(j p) c -> p j c", p=128)

        def final_norm(ti):
            c0, c1 = tile_cols(ti)
            R = c1 - c0
            b = ti % 2
            hh = h1[0:R, ti, :]
            act(junk[0:R, :], hh, AF.Square, [R_h1[ti]], [R_junk, R_ss[b]], accum_out=ssb[b][0:R, 0:1])
            act(ssb[b][0:R, 1:2], ssb[b][0:R, 0:1], AF.Sqrt, [R_ss[b]], [R_ss[b]], scale=1.0 / D, bias=EPS)
            recip(ssb[b][0:R, 1:2], ssb[b][0:R, 1:2], [R_ss[b]], [R_ss[b]])
            stt("dve", hh, hh, ssb[b][0:R, 1:2], gfbc[0:R, :], ALU.mult, ALU.mult,
                [R_h1[ti], R_ss[b], R_gf], [R_h1[ti]])
            S.dma("sp", y[c0:c1, :], hh, reads=[R_h1[ti]])

        for half in range(2):
            if half == 0:
                segs = [(0, 2048, 32), (16, 0, 512), (528, 512, 512)]
                nconv = 1040
            else:
                segs = [(0, 1024, 512), (512, 1536, 512)]
                nconv = 1024
            if half == 0:
                for b_ in range(2):
                    mset("pool", arow[b_][:, 0:2], 0.0, [R_ar[b_]])
            for j in range(NJ):
                wb = (j // 2) % 2
                jj = j % 2
                b = j % 2
                if jj == 0:
                    S.dma("pool", wupb[wb][:, :, 0, :], w_up_v[:, :, 128 * j:128 * j + 256], writes=[R_wup[wb]])
                    S.dma("pool", wupb[wb][:, :, 1, :], w_up_v[:, :, DFF + 128 * j:DFF + 128 * j + 256],
                          writes=[R_wup[wb]])
                for (m0, g0, W) in segs:
                    hr = creg(R_hn2, g0, g0 + W)
                    pa = bank()
                    for kt in range(8):
                        mm(PB(pa, W), wupb[wb][:, kt, 0, 128 * jj:128 * jj + 128], hn2T[:, kt, g0:g0 + W], kt == 0, kt == 7,
                           [R_wup[wb]] + hr, [PR[pa]])
                    pv_ = bank()
                    for kt in range(8):
                        mm(PB(pv_, W), wupb[wb][:, kt, 1, 128 * jj:128 * jj + 128], hn2T[:, kt, g0:g0 + W], kt == 0, kt == 7,
                           [R_wup[wb]] + hr, [PR[pv_]])
                    if g0 == 2048:
                        cp("act", arow[b][:, 2:18], PB(pa)[:, 0:16], [PR[pa]], [R_ar[b]])
                        cp("act", arow[b][:, 2 + 1040:2 + 1056], PB(pa)[:, 16:32], [PR[pa]], [R_ar[b]])
                        cp("act", vrow[b][:, 0:16], PB(pv_)[:, 0:16], [PR[pv_]], [R_vr[b]])
                        cp("act", vrow[b][:, 1040:1056], PB(pv_)[:, 16:32], [PR[pv_]], [R_vr[b]])
                        continue
                    cp("act", arow[b][:, 2 + m0:2 + m0 + W], PB(pa, W), [PR[pa]], [R_ar[b]])
                    cp("act", vrow[b][:, m0:m0 + W], PB(pv_, W), [PR[pv_]], [R_vr[b]])
                if half == 1:
                    cp("act", arow[b][:, 0:2], carry[:, j, :], [R_carry], [R_ar[b]])
                w0 = cwT[:, 0 * NJ + j:0 * NJ + j + 1]
                w1_ = cwT[:, 1 * NJ + j:1 * NJ + j + 1]
                w2_ = cwT[:, 2 * NJ + j:2 * NJ + j + 1]
                cc = crow[b][:, 0:nconv]
                ts("dve", cc, arow[b][:, 2:2 + nconv], w2_, cbT[:, j:j + 1], ALU.mult, ALU.add, [R_ar[b], R_vec], [R_cr[b]])
                stt("dve", cc, arow[b][:, 1:1 + nconv], w1_, cc, ALU.mult, ALU.add, [R_ar[b], R_vec, R_cr[b]], [R_cr[b]])
                stt("dve", cc, arow[b][:, 0:nconv], w0, cc, ALU.mult, ALU.add, [R_ar[b], R_vec, R_cr[b]], [R_cr[b]])
                act(cc, cc, AF.Silu, [R_cr[b]], [R_cr[b]])
                if half == 0:
                    tt("dve", mT[:, j, 0:16], crow[b][:, 0:16], vrow[b][:, 0:16], ALU.mult, [R_cr[b], R_vr[b]], [R_mT])
                    tt("dve", mT[:, j, 32:1056], crow[b][:, 16:1040], vrow[b][:, 16:1040], ALU.mult,
                       [R_cr[b], R_vr[b]], [R_mT])
                    cp("act", carry[:, j, :], arow[b][:, 2 + 1038:2 + 1040], [R_ar[b]], [R_carry])
                    cp("act", aS[:, j, :], arow[b][:, 2 + 1040:2 + 1056], [R_ar[b]], [R_aS])
                    cp("act", vS[:, j, :], vrow[b][:, 1040:1056], [R_vr[b]], [R_aS])
                else:
                    tt("dve", mT[:, j, 0:1024], cc, vrow[b][:, 0:1024], ALU.mult, [R_cr[b], R_vr[b]], [R_mT])
                    cp("act", convp[:, :, j], arow[b][:, 2 + 1022:2 + 1024], [R_ar[b]], [R_convp])
            if half == 0:
                cs1 = crow[0][:, 0:352].rearrange("p (j n) -> p j n", n=16)
                cs2 = crow[1][:, 0:352].rearrange("p (j n) -> p j n", n=16)
                bufv = bufT[:].rearrange("p j (n r) -> p j n r", r=2)

                def wbc(r):
                    return bc(cwT[:, r * NJ:(r + 1) * NJ].unsqueeze(2), [128, NJ, 16])
                rs_ = [R_aS, R_bufT, R_vec, R_cr[0], R_cr[1]]
                ws_ = [R_cr[0], R_cr[1]]
                tt("dve", cs1, aS[:], wbc(2), ALU.mult, rs_, ws_)
                tt("dve", cs1, cs1, bc(cbT.unsqueeze(2), [128, NJ, 16]), ALU.add, rs_, ws_)
                tt("dve", cs2, bufv[:, :, :, 1], wbc(1), ALU.mult, rs_, ws_)
                tt("dve", cs1, cs1, cs2, ALU.add, rs_, ws_)
                tt("dve", cs2, bufv[:, :, :, 0], wbc(0), ALU.mult, rs_, ws_)
                tt("dve", cs1, cs1, cs2, ALU.add, rs_, ws_)
                act(cs1, cs1, AF.Silu, rs_, ws_)
                tt("dve", mT[:, :, 16:32], cs1, vS[:], ALU.mult, rs_, [R_mT])
            if half == 0:
                groups = [(32, 512, [0, 1, 2, 3]), (544, 512, [4, 5, 6, 7]), (0, 32, [16])]
            else:
                groups = [(0, 512, [8, 9, 10, 11]), (512, 512, [12, 13, 14, 15])]
            def pb_mm(ft, gi_):
                m0, Wt, tiles = groups[gi_]
                wb = ft % 2
                if gi_ == 0:
                    S.dma("pool", wdnb[wb][:], w_dn_v[:, :, 128 * ft:128 * ft + 128], writes=[R_wdn[wb]])
                pi = bank()
                yb = (ft * len(groups) + gi_) % 2
                for j in range(NJ):
                    mm(PB(pi, Wt), wdnb[wb][:, j, :], mT[:, j, m0:m0 + Wt], j == 0, j == NJ - 1,
                       [R_wdn[wb], R_mT], [PR[pi]])
                cp("act", yts[yb][:, 0:Wt], PB(pi, Wt), [PR[pi]], [R_yt[yb]])

            def pb_tr(ft, gi_):
                m0, Wt, tiles = groups[gi_]
                yb = (ft * len(groups) + gi_) % 2
                pt = bank()
                for n_, ti in enumerate(tiles):
                    R = 32 if ti == 16 else 128
                    tr(PB(pt)[0:R, 128 * n_:128 * n_ + 128], yts[yb][:, 128 * n_:128 * n_ + R], identf,
                       [R_yt[yb], R_cst], [PR[pt]], signal=(n_ == len(tiles) - 1))
                for n_, ti in enumerate(tiles):
                    R = 32 if ti == 16 else 128
                    hs = h1[0:R, ti, 128 * ft:128 * ft + 128]
                    tt("dve", hs, hs, PB(pt)[0:R, 128 * n_:128 * n_ + 128], ALU.add, [R_h1[ti], PR[pt]], [R_h1[ti]])

            steps = [(ft, gi_) for ft in range(8) for gi_ in range(len(groups))]
            pb_mm(*steps[0])
            for i_ in range(1, len(steps)):
                pb_mm(*steps[i_])
                pb_tr(*steps[i_ - 1])
            pb_tr(*steps[-1])
            for (m0, Wt, tiles) in groups:
                for ti in tiles:
                    final_norm(ti)

        pi = bank()
        tr(PB(pi)[0:44, 0:128], convp[:].rearrange("p r j -> p (r j)"), identf, [R_convp, R_cst], [PR[pi]])
        cvo = A.alloc([44, 128], F32)
        R_cvo = Reg()
        cp("act", cvo[:], PB(pi)[0:44, 0:128], [PR[pi]], [R_cvo])
        for r in range(2):
            S.dma("sp", o_cvp[r].rearrange("(j f) -> j f", f=128), cvo[22 * r:22 * r + 22, :], reads=[R_cvo])
        for grp in range(6):
            pi = bank()
            js = list(range(4 * grp, min(4 * grp + 4, NJ)))
            for n_, j in enumerate(js):
                tr(PB(pi)[0:16, 128 * n_:128 * n_ + 128], aS[:, j, :], identf, [R_aS, R_cst], [PR[pi]],
                   signal=(n_ == len(js) - 1))
            cp("act", cbuf[0:16, 512 * grp:512 * grp + 128 * len(js)], PB(pi)[0:16, 0:128 * len(js)], [PR[pi]], [R_mT])
        S.dma("sp", o_cvs[:, 1, :], cbuf[0:16, :], reads=[R_mT])

        S.finish()
        print("built: ops", S.nops, "waits", S.nwaits, "sbuf peak", A.peak - SB_LO, "/", SB_HI - SB_LO)
    return nc


_CACHE = {}


def kernel(**inp):
    f = lambda k: np.asarray(inp[k], np.float32)
    x_prompt = f("x_prompt")
    x_sample = f("x_sample")
    meta = f("meta_tokens")
    vecs = make_vecs(inp)
    nvec = vecs.shape[1]
    if "nc" not in _CACHE:
        _CACHE["nc"] = build_nc(nvec)
    nc = _CACHE["nc"]
    lam = np.stack([f("s5_lambda_re")[0].reshape(16, 128), f("s5_lambda_im")[0].reshape(16, 128),
                    np.repeat(f("s5_log_dt")[0].reshape(16, 2), 64, axis=1)], axis=0)
    bst = np.stack([f("s5_b_re")[0].reshape(16, 128, 16), f("s5_b_im")[0].reshape(16, 128, 16)], axis=0)
    cch = np.stack([f("s5_c_re")[0].reshape(512, 64), f("s5_c_im")[0].reshape(512, 64)], axis=0)
    shared = {
        "w_in": f("w_in")[0], "w_out": f("w_out")[0], "w_up": f("ffn_w_up")[0], "w_down": f("ffn_w_down")[0],
        "w_glu": f("s5_w_glu")[0], "lamPP": np.ascontiguousarray(lam), "bst": np.ascontiguousarray(bst),
        "cch": np.ascontiguousarray(cch), "vecs": vecs, "hlbrows": f("hg_lower_bounds"),
        "gfinal": f("final_norm_g"), "consts": CONSTS,
    }
    in_maps = []
    for c in range(NCORES):
        sl = slice(16 * c, 16 * c + 16)
        m = dict(shared)
        m["xtok"] = np.ascontiguousarray(np.concatenate([x_prompt[c], meta, x_sample[sl, 0, :]], axis=0))
        m["s5s"] = np.ascontiguousarray(np.stack([f("state_s5_re")[0, sl].reshape(16, 2048),
                                                  f("state_s5_im")[0, sl].reshape(16, 2048)], axis=0))
        m["hgs"] = np.ascontiguousarray(f("state_hgrn")[0, sl])
        m["cvs"] = np.ascontiguousarray(f("state_ffn_conv")[0, sl])
        in_maps.append(m)
    res = run_bass_kernel_spmd(nc, in_maps, core_ids=list(range(NCORES)))
    rs = res.results
    y_prompt = np.stack([rs[c]["y"][0:2048] for c in range(NCORES)], axis=0)
    y_sample = np.concatenate([rs[c]["y"][2064:2080] for c in range(NCORES)], axis=0)[:, None, :]
    s5p_re = np.stack([rs[c]["o_s5p"][0].reshape(32, 64) for c in range(NCORES)], axis=0)[None]
    s5p_im = np.stack([rs[c]["o_s5p"][1].reshape(32, 64) for c in range(NCORES)], axis=0)[None]
    hgp = np.stack([rs[c]["o_hgp"] for c in range(NCORES)], axis=0)[None]
    cvp = np.stack([rs[c]["o_cvp"] for c in range(NCORES)], axis=0)[None]
    s5s_re = np.concatenate([rs[c]["o_s5s"][0].reshape(16, 32, 64) for c in range(NCORES)], axis=0)[None]
    s5s_im = np.concatenate([rs[c]["o_s5s"][1].reshape(16, 32, 64) for c in range(NCORES)], axis=0)[None]
    hgs = np.concatenate([rs[c]["o_hgs"] for c in range(NCORES)], axis=0)[None]
    cvs = np.concatenate([rs[c]["o_cvs"] for c in range(NCORES)], axis=0)[None]
    out = (y_prompt, y_sample, s5p_re, s5p_im, hgp, cvp, s5s_re, s5s_im, hgs, cvs)
    return tuple(np.ascontiguousarray(o, dtype=np.float32) for o in out)
```

```python
import math
K_STOP = ''
import numpy as np
import concourse.bass as bass
import concourse.mybir as mybir
from concourse.bass_utils import run_bass_kernel_spmd
from contextlib import ExitStack

F32 = mybir.dt.float32
BF16 = mybir.dt.bfloat16
AF = mybir.ActivationFunctionType
ALU = mybir.AluOpType
NDS = 48
NCORES = 8
D = 1024
NT = 2080
NTILES = 17
DFF = 2816
NJ = 22
EPS = 1e-6
SB_LO = 16512
SB_HI = 229344
SE = "pool"


class _Stop(Exception):
    pass


class Reg:
    __slots__ = ("w", "r")

    def __init__(self):
        self.w = None
        self.r = {}


class Sched:
    def __init__(self, nc, stack):
        self.nc = nc
        self.eng = {"pe": nc.tensor, "act": nc.scalar, "dve": nc.vector,
                    "pool": nc.gpsimd, "sp": nc.sync}
        self.sem = {k: stack.enter_context(nc.semaphore("s_" + k)) for k in self.eng}
        self.cnt = {k: 0 for k in self.eng}
        self.waited = {k: {} for k in self.eng}
        self.dsem = [stack.enter_context(nc.semaphore("d%d" % i)) for i in range(NDS)]
        self.dcnt = [0] * NDS
        self.dpool = {"sp": list(range(0, 24)), "pool": list(range(24, 40)), "act": list(range(40, NDS))}
        self.dnext = {"sp": 0, "pool": 0, "act": 0}
        self.pending = []
        self.nwaits = 0
        self.nops = 0

    def _wait(self, e, tok):
        sem, val, _ = tok
        assert val is not None, "dependency on unsignaled PE op"
        key = id(sem)
        if self.waited[e].get(key, 0) >= val:
            return
        self.waited[e][key] = val
        self.eng[e].wait_ge(sem, val)
        self.nwaits += 1

    def _deps(self, e, reads, writes):
        for R in reads:
            t = R.w
            if t is not None and not (t[2] == e and e == "pe"):
                self._wait(e, t)
        for R in writes:
            t = R.w
            if t is not None and not (t[2] == e and e == "pe"):
                self._wait(e, t)
            for t in R.r.values():
                if not (t[2] == e and e == "pe"):
                    self._wait(e, t)

    def _register(self, tok, reads, writes):
        for R in reads:
            R.r[id(tok[0])] = tok
        for R in writes:
            R.w = tok
            R.r = {}

    def op(self, e, fn, reads=(), writes=(), signal=True):
        self._deps(e, reads, writes)
        ins = fn(self.eng[e])
        self.nops += 1
        if signal:
            self.cnt[e] += 1
            ins.then_inc(self.sem[e], 1)
            tok = (self.sem[e], self.cnt[e], e)
            if e == "pe":
                for p in self.pending:
                    p[1] = self.cnt[e]
                self.pending = []
        else:
            assert e == "pe"
            tok = [self.sem[e], None, e]
            self.pending.append(tok)
        self._register(tok, reads, writes)
        return tok

    def dma(self, e, out, in_, reads=(), writes=(), **kw):
        pool_ = self.dpool[e]
        k = pool_[self.dnext[e]]
        self.dnext[e] = (self.dnext[e] + 1) % len(pool_)
        if self.dcnt[k] > 0:
            self._wait(e, (self.dsem[k], self.dcnt[k], None))
        self._deps(e, reads, writes)
        ins = self.eng[e].dma_start(out=out, in_=in_, **kw)
        self.dcnt[k] += 16
        ins.then_inc(self.dsem[k], 16)
        tok = (self.dsem[k], self.dcnt[k], None)
        self._register(tok, reads, writes)
        self.nops += 1
        return tok

    def barrier(self):
        for e in self.eng:
            for x in self.eng:
                if x != e and self.cnt[x] > 0:
                    self._wait(e, (self.sem[x], self.cnt[x], x))
            for k in range(NDS):
                if self.dcnt[k] > 0:
                    self._wait(e, (self.dsem[k], self.dcnt[k], None))

    def finish(self):
        e = "sp"
        for k in range(NDS):
            if self.dcnt[k] > 0:
                self._wait(e, (self.dsem[k], self.dcnt[k], None))
        for x in self.eng:
            if x != e and self.cnt[x] > 0:
                self._wait(e, (self.sem[x], self.cnt[x], x))


class Arena:
    def __init__(self, nc):
        self.nc = nc
        self.l = SB_LO
        self.r = SB_HI
        self.n = 0
        self.peak = 0

    def alloc(self, shape, dt, right=False):
        sz = 4 if dt == F32 else 2
        nb = sz
        for s in shape[1:]:
            nb *= s
        nb = (nb + 63) // 64 * 64
        if right:
            self.r -= nb
            off = self.r
        else:
            off = self.l
            self.l += nb
        assert self.l <= self.r, ("SBUF overflow", self.l, self.r)
        self.peak = max(self.peak, self.l + (SB_HI - self.r))
        self.n += 1
        return self.nc.alloc_sbuf_tensor_at("t%d" % self.n, list(shape), dt, offset=off)


def tile_cols(ti):
    return (2048, 2080) if ti == 16 else (128 * ti, 128 * ti + 128)


BLOCKS = [(0, 512), (512, 1024), (1024, 1536), (1536, 2048), (2048, 2080)]


CONST_OFF = {}


def make_consts():
    cols = []
    off = 0

    def add(name, arr):
        nonlocal off
        a = np.zeros((128, arr.shape[1]), np.float32)
        a[:arr.shape[0]] = arr
        CONST_OFF[name] = (off, arr.shape[1])
        cols.append(a)
        off += arr.shape[1]

    add("ident", np.eye(128, dtype=np.float32))
    s = np.arange(128)[:, None] % 64
    c = np.arange(64)[None, :]
    add("attmask", (s <= c).astype(np.float32))
    s = np.arange(128)[:, None]
    c = np.arange(128)[None, :]
    add("mrev", ((s > c) & (s // 64 == c // 64)).astype(np.float32))
    s = np.arange(32)[:, None]
    c = np.arange(32)[None, :]
    add("msmall", ((s > c) & (s < 16) & (c < 16)).astype(np.float32))
    p = np.arange(128)[:, None]
    col = np.arange(128)[None, :]
    add("blockmask", (((col // 16) % 2) == (p // 64)).astype(np.float32))
    col = np.arange(32)[None, :]
    add("blockident", ((p % 32) == col).astype(np.float32))
    n = np.arange(16)[None, :]
    add("onehot", ((p - 16) == n).astype(np.float32))
    col = np.arange(128)[None, :]
    add("qmask", ((col // 32) == (p // 32)).astype(np.float32))
    add("ones", np.ones((128, 128), np.float32))
    return np.concatenate(cols, axis=1)


CONSTS = make_consts()
NCONST = CONSTS.shape[1]

VEC_OFF = {}


def make_vecs(inp):
    cols = []
    off = 0

    def add(name, arr):
        nonlocal off
        VEC_OFF[name] = (off, arr.shape[1])
        cols.append(np.ascontiguousarray(arr, dtype=np.float32))
        off += arr.shape[1]

    def fm(v):
        return np.asarray(v, np.float32).reshape(-1, 128).T

    add("gmix", fm(inp["norm_mix_g"][0]))
    add("gffn", fm(inp["norm_ffn_g"][0]))
    add("hlb0", fm(inp["hg_lower_bounds"][0]))
    add("hlb1", fm(inp["hg_lower_bounds"][1]))
    add("s5d", fm(inp["s5_d"][0]))
    add("bglu", fm(inp["s5_b_glu"][0]))
    add("hgng", fm(inp["hg_norm_g"][0]))
    cw = np.asarray(inp["ffn_conv_w"][0], np.float32)
    add("convw", np.concatenate([fm(cw[r]) for r in range(3)], axis=1))
    add("convb", fm(inp["ffn_conv_b"][0]))
    return np.concatenate(cols, axis=1)


def build_nc(nvec):
    nc = bass.Bass("TRN2", target_bir_lowering=False)

    def din(name, shape):
        return nc.dram_tensor(name, list(shape), F32, kind="ExternalInput").ap()

    def dout(name, shape):
        return nc.dram_tensor(name, list(shape), F32, kind="ExternalOutput").ap()

    xtok = din("xtok", [NT, D])
    w_in = din("w_in", [D, 2560])
    w_out = din("w_out", [D, D])
    w_up = din("w_up", [D, 2 * DFF])
    w_down = din("w_down", [DFF, D])
    w_glu = din("w_glu", [512, 512])
    lamPP = din("lamPP", [3, 16, 128])
    bst = din("bst", [2, 16, 128, 16])
    cch = din("cch", [2, 512, 64])
    vecs = din("vecs", [128, nvec])
    hlbrows = din("hlbrows", [2, 512])
    gfinal = din("gfinal", [D])
    s5s = din("s5s", [2, 16, 2048])
    hgs = din("hgs", [16, 4, 128, 128])
    cvs = din("cvs", [16, 2, DFF])
    consts = din("consts", [128, NCONST])

    y = dout("y", [NT, D])
    o_s5p = dout("o_s5p", [2, 16, 128])
    o_hgp = dout("o_hgp", [4, 128, 128])
    o_cvp = dout("o_cvp", [2, DFF])
    o_s5s = dout("o_s5s", [2, 16, 2048])
    o_hgs = dout("o_hgs", [16, 4, 128, 128])
    o_cvs = dout("o_cvs", [16, 2, DFF])

    with ExitStack() as st:
        S = Sched(nc, st)
        A = Arena(nc)
        PS = nc.alloc_psum_tensor("ps", [128, 4096], F32)
        PR = [Reg() for _ in range(8)]
        pb_state = {"i": 0}

        reserved = set()

        def bank():
            i = pb_state["i"]
            while i in reserved:
                i = (i + 1) % 8
            pb_state["i"] = (i + 1) % 8
            return i

        def bank2():
            i = pb_state["i"]
            if i % 2:
                i = (i + 1) % 8
            while i in reserved or (i + 1) in reserved:
                i = (i + 2) % 8
            pb_state["i"] = (i + 2) % 8
            return i

        def PB(i, w=512, n=1):
            return PS[:, 512 * i:512 * i + w] if n == 1 else PS[:, 512 * i:512 * (i + n)]

        def PBb(i):
            return PS[:, 512 * i:512 * i + 512].bitcast(BF16)

        def mm(out, lhsT, rhs, start, stop, reads, writes, tp=None, signal=None):
            sig = stop if signal is None else signal
            kw = {}
            if tp is not None:
                kw["tile_position"] = tp
            return S.op("pe", lambda e: e.matmul(out, lhsT=lhsT, rhs=rhs, start=start, stop=stop,
                                                 skip_group_check=True, **kw),
                        reads=reads, writes=writes, signal=sig)

        def tr(out, in_, ident, reads, writes, signal=True):
            return S.op("pe", lambda e: e.transpose(out=out, in_=in_, identity=ident),
                        reads=reads, writes=writes, signal=signal)

        def act(out, in_, func, reads, writes, **kw):
            return S.op("act", lambda e: e.activation(out=out, in_=in_, func=func, **kw),
                        reads=reads, writes=writes)

        def tt(eng, out, in0, in1, op, reads, writes):
            return S.op(eng, lambda e: e.tensor_tensor(out=out, in0=in0, in1=in1, op=op),
                        reads=reads, writes=writes)

        def ts(eng, out, in0, s1, s2, op0, op1, reads, writes):
            if op1 is None:
                return S.op(eng, lambda e: e.tensor_scalar(out=out, in0=in0, scalar1=s1, scalar2=None, op0=op0),
                            reads=reads, writes=writes)
            return S.op(eng, lambda e: e.tensor_scalar(out=out, in0=in0, scalar1=s1, scalar2=s2, op0=op0, op1=op1),
                        reads=reads, writes=writes)

        def stt(eng, out, in0, scalar, in1, op0, op1, reads, writes):
            return S.op(eng, lambda e: e.scalar_tensor_tensor(out=out, in0=in0, scalar=scalar, in1=in1,
                                                              op0=op0, op1=op1),
                        reads=reads, writes=writes)

        def cp(eng, out, in_, reads, writes):
            if eng == "act":
                return S.op("act", lambda e: e.copy(out=out, in_=in_), reads=reads, writes=writes)
            return S.op(eng, lambda e: e.tensor_copy(out=out, in_=in_), reads=reads, writes=writes)

        def recip(out, in_, reads, writes):
            return S.op("dve", lambda e: e.reciprocal(out=out, in_=in_), reads=reads, writes=writes)

        def mset(eng, ap, val, writes):
            return S.op(eng, lambda e: e.memset(ap, val), writes=writes)

        def bc(ap, shape):
            return ap.to_broadcast(list(shape))

        cst = A.alloc([128, NCONST], F32)
        R_cst = Reg()
        S.dma("sp", cst[:], consts, writes=[R_cst])
        vec = A.alloc([128, nvec], F32)
        R_vec = Reg()
        S.dma("sp", vec[:], vecs, writes=[R_vec])

        def C(name, rows=128):
            o, w = CONST_OFF[name]
            return cst[0:rows, o:o + w]

        def V(name):
            o, w = VEC_OFF[name]
            return vec[:, o:o + w]

        identb = A.alloc([128, 128], BF16)
        onesb = A.alloc([128, 128], BF16)
        mrevb = A.alloc([128, 128], BF16)
        msmallb = A.alloc([32, 32], BF16)
        R_cb = Reg()
        cp("dve", identb[:], C("ident"), [R_cst], [R_cb])
        cp("dve", onesb[:], C("ones"), [R_cst], [R_cb])
        cp("dve", mrevb[:], C("mrev"), [R_cst], [R_cb])
        cp("dve", msmallb[:], C("msmall", 32), [R_cst], [R_cb])
        identf = C("ident")

        lbT = A.alloc([128, 4], F32)
        omlT = A.alloc([128, 4], F32)
        R_lb = Reg()
        tt("dve", lbT[:], V("hlb1"), V("hlb0"), ALU.subtract, [R_vec], [R_lb])
        act(lbT[:], lbT[:], AF.Exp, [R_lb], [R_lb])
        ts("dve", lbT[:], lbT[:], 1.0, None, ALU.add, None, [R_lb], [R_lb])
        recip(lbT[:], lbT[:], [R_lb], [R_lb])
        ts("dve", omlT[:], lbT[:], -1.0, 1.0, ALU.mult, ALU.add, [R_lb], [R_lb])

        mixT = A.alloc([128, 8, NT], BF16, right=True)
        R_mix = [Reg() for _ in range(NTILES)]

        def creg(regs, c0, c1):
            out = []
            for ti in range(NTILES):
                a, b = tile_cols(ti)
                if a < c1 and c0 < b:
                    out.append(regs[ti])
            return out

        mark_G = A.l

        uT = A.alloc([128, 4, NT], BF16)
        mark_R = A.r
        qT = A.alloc([128, 4, NT], BF16, right=True)
        kT = A.alloc([128, 4, NT], BF16, right=True)
        v_tok = A.alloc([128, NTILES, 512], BF16, right=True)
        kk_tok = A.alloc([128, NTILES, 512], BF16, right=True)
        ebl = A.alloc([128, 4, 33], F32)
        fsamp = A.alloc([128, 4, 16], F32)
        R_u = [Reg() for _ in range(NTILES)]
        R_q = [[Reg() for _ in range(NTILES)] for _ in range(4)]
        R_k = [[Reg() for _ in range(NTILES)] for _ in range(4)]
        R_v = [Reg() for _ in range(NTILES)]
        R_kk = [Reg() for _ in range(NTILES)]
        R_ebl = Reg()
        mark_P1out = A.l

        xnT = A.alloc([128, 8, NT], BF16)
        R_xn = [Reg() for _ in range(NTILES)]
        wtok = A.alloc([128, 8, 512], BF16)
        R_wtok = Reg()
        w_in_v = w_in.rearrange("(kt p) c -> p kt c", p=128)
        S.dma("pool", wtok[:], w_in_v[:, :, 1536:2048], writes=[R_wtok])

        def v_proj(ti):
            c0, c1 = tile_cols(ti)
            R = c1 - c0
            pi = bank()
            for kt in range(8):
                mm(PB(pi)[0:R, :], xnT[:, kt, c0:c1], wtok[:, kt, :], kt == 0, kt == 7,
                   [R_xn[ti], R_wtok], [PR[pi]])
            cp("act", v_tok[0:R, ti, :], PB(pi)[0:R, :], [PR[pi]], [R_v[ti]])

        mark_tmp = A.l
        xt = [A.alloc([128, D], F32) for _ in range(4)]
        xnb = [A.alloc([128, D], BF16) for _ in range(4)]
        junk = A.alloc([128, D], BF16)
        ssb = [A.alloc([128, 2], F32) for _ in range(4)]
        R_xt = [Reg() for _ in range(4)]
        R_xnb = [Reg() for _ in range(4)]
        R_junk = Reg()
        R_ss = [Reg() for _ in range(4)]
        gmixT = V("gmix")
        def p0_A(ti):
            c0, c1 = tile_cols(ti)
            R = c1 - c0
            b = ti % 4
            S.dma("sp", xt[b][0:R, :], xtok[c0:c1, :], writes=[R_xt[b]])
            act(junk[0:R, :], xt[b][0:R, :], AF.Square, [R_xt[b]], [R_junk, R_ss[b]], accum_out=ssb[b][0:R, 0:1])
            act(ssb[b][0:R, 1:2], ssb[b][0:R, 0:1], AF.Sqrt, [R_ss[b]], [R_ss[b]], scale=1.0 / D, bias=EPS)
            recip(ssb[b][0:R, 1:2], ssb[b][0:R, 1:2], [R_ss[b]], [R_ss[b]])
            ts("pool", xnb[b][0:R, :], xt[b][0:R, :], ssb[b][0:R, 1:2], 1.0, ALU.mult, ALU.mult, [R_xt[b], R_ss[b]], [R_xnb[b]])

        def p0_B(ti):
            c0, c1 = tile_cols(ti)
            R = c1 - c0
            b = ti % 4
            pi = bank()
            pv = PBb(pi).rearrange("p (k r) -> p k r", r=128)
            for kt in range(8):
                tr(pv[:, kt, 0:R], xnb[b][0:R, kt * 128:(kt + 1) * 128], identb[0:R, 0:R],
                   [R_xnb[b], R_cb], [PR[pi]], signal=(kt == 7))
            tt("dve", xnT[:, :, c0:c1], pv[:, :, 0:R], bc(gmixT.unsqueeze(2), [128, 8, R]), ALU.mult,
               [PR[pi], R_vec], [R_xn[ti]])

        p0_A(0)
        p0_A(1)
        for ti in range(2, NTILES):
            p0_A(ti)
            p0_B(ti - 2)
            if ti >= 3:
                v_proj(ti - 3)
        p0_B(NTILES - 2)
        v_proj(NTILES - 3)
        p0_B(NTILES - 1)
        v_proj(NTILES - 2)
        v_proj(NTILES - 1)
        S.barrier()
        A.l = mark_tmp

        if K_STOP == 'P0':
            S.finish()
            return nc
        wft = [A.alloc([128, 8, 128], BF16) for _ in range(2)]
        R_wft = [Reg(), Reg()]
        def fm_tile(col0, evac):
            b = fm_tile.n % 2
            fm_tile.n += 1
            S.dma("pool", wft[b][:], w_in_v[:, :, col0:col0 + 128], writes=[R_wft[b]])
            for bi, (c0, c1) in enumerate(BLOCKS):
                W = c1 - c0
                pi = bank()
                for kt in range(8):
                    mm(PB(pi, W), wft[b][:, kt, :], xnT[:, kt, c0:c1], kt == 0, kt == 7,
                       creg(R_xn, c0, c1) + [R_wft[b]], [PR[pi]])
                evac(bi, c0, c1, W, pi)
        fm_tile.n = 0

        def u_tile(t):
            fm_tile(0 + 128 * t, lambda bi, c0, c1, W, pi, t=t:
                    cp("act", uT[:, t, c0:c1], PB(pi, W), [PR[pi]], creg(R_u, c0, c1)))

        def g_tile(h):
            fm_tile(2048 + 128 * h, lambda bi, c0, c1, W, pi, h=h:
                    act(mixT[:, 4 + h, c0:c1], PB(pi, W), AF.Silu, [PR[pi]], creg(R_mix, c0, c1)))

        for h in range(4):
            fm_tile(512 + 128 * h, lambda bi, c0, c1, W, pi, h=h:
                    cp("dve", qT[:, h, c0:c1], PB(pi, W), [PR[pi]], creg(R_q[h], c0, c1)))
        HW_ = 1056
        lgf = [A.alloc([128, HW_], F32) for _ in range(2)]
        bTt = [A.alloc([128, HW_], F32) for _ in range(2)]
        etmp_ = [A.alloc([128, HW_], F32) for _ in range(2)]
        etmp = [etmp_, etmp_]
        smask = A.alloc([128, HW_], BF16)
        R_lgfb = [[Reg(), Reg()], [Reg(), Reg(), Reg()]]
        R_bT = [Reg(), Reg()]
        R_et_ = [Reg(), Reg()]
        R_et = [R_et_, R_et_]
        R_sm = Reg()
        mset("pool", smask[:], 1.0, [R_sm])
        mset("pool", smask[:, 0:1024:64], 0.0, [R_sm])
        mset("pool", smask[:, 1024:1025], 0.0, [R_sm])
        mset("pool", smask[:, 1040:1056], 0.0, [R_sm])

        def f_chains(h):
            ns = (1024, 1056)
            nprs = (1024, 1040)
            for half in range(2):
                n = ns[half]
                rl = R_lgfb[half]
                act(lgf[half][:, 0:n], lgf[half][:, 0:n], AF.Ln, rl, rl)
            for half in range(2):
                n = ns[half]
                S.op("dve", lambda e, half=half, n=n: e.tensor_tensor_scan(
                    out=bTt[half][:, 0:n], data0=smask[:, 0:n], data1=lgf[half][:, 0:n],
                    initial=0.0, op0=ALU.mult, op1=ALU.add),
                     reads=R_lgfb[half] + [R_sm], writes=[R_bT[half]])
            for half in range(2):
                npr, g0 = nprs[half], 1024 * half
                act(etmp_[half][:, 0:npr], bTt[half][:, 0:npr], AF.Exp, [R_bT[half]], [R_et_[half]])
            for half in range(2):
                npr, g0 = nprs[half], 1024 * half
                tt("dve", qT[:, h, g0:g0 + npr], qT[:, h, g0:g0 + npr], etmp_[half][:, 0:npr], ALU.mult,
                   [R_et_[half]] + creg(R_q[h], g0, g0 + npr), creg(R_q[h], g0, g0 + npr))
            for half in range(2):
                npr, g0 = nprs[half], 1024 * half
                act(etmp_[half][:, 0:npr], bTt[half][:, 0:npr], AF.Exp, [R_bT[half]], [R_et_[half]], scale=-1.0)
            for half in range(2):
                npr, g0 = nprs[half], 1024 * half
                tt("dve", kT[:, h, g0:g0 + npr], kT[:, h, g0:g0 + npr], etmp_[half][:, 0:npr], ALU.mult,
                   [R_et_[half]] + creg(R_k[h], g0, g0 + npr), creg(R_k[h], g0, g0 + npr))
            for half in range(2):
                act(ebl[:, h, 1 + 16 * half:17 + 16 * half], bTt[half][:, 63:1024:64], AF.Exp, [R_bT[half]], [R_ebl])
            act(ebl[:, h, 0:1], bTt[1][:, 1039:1040], AF.Exp, [R_bT[1]], [R_ebl])
            act(fsamp[:, h, :], bTt[1][:, 1040:1056], AF.Exp, [R_bT[1]], [R_ebl])

        for h in range(4):
            def evac_f(bi, c0, c1, W, pi, h=h):
                half = 0 if bi < 2 else 1
                l0 = c0 - 1024 * half
                rb = [R_lgfb[half][bi - 2 * half]]
                fl = lgf[half][:, l0:l0 + W]
                act(fl, PB(pi, W), AF.Sigmoid, [PR[pi]], rb)
                ts("dve", fl, fl, omlT[:, h:h + 1], lbT[:, h:h + 1], ALU.mult, ALU.add, rb + [R_lb], rb)
                ts("pool", kT[:, h, c0:c1], fl, -1.0, 1.0, ALU.mult, ALU.add, rb, creg(R_k[h], c0, c1))
            fm_tile(1024 + 128 * h, evac_f)
            f_chains(h)
            u_tile(h)
            g_tile(h)

        kkT = [A.alloc([128, NT], BF16) for _ in range(2)]
        R_kkT = [Reg(), Reg()]
        for h in range(4):
            kb = h % 2
            tt("dve", kkT[kb][:, 0:2048].rearrange("p (c s) -> p c s", s=64),
               kT[:, h, 0:2048].rearrange("p (c s) -> p c s", s=64),
               bc(ebl[:, h, 1:33].unsqueeze(2), [128, 32, 64]), ALU.mult,
               creg(R_k[h], 0, 2048) + [R_ebl], [R_kkT[kb]])
            ts("dve", kkT[kb][:, 2048:2064], kT[:, h, 2048:2064], ebl[:, h, 0:1], None, ALU.mult, None,
               [R_k[h][16], R_ebl], [R_kkT[kb]])
            cp("dve", kkT[kb][:, 2064:2080], kT[:, h, 2064:2080], [R_k[h][16]], [R_kkT[kb]])
            for grp, tiles in enumerate(([0, 1, 2, 3, 4, 5, 6, 7], [8, 9, 10, 11, 12, 13, 14, 15], [16])):
                pi = bank()
                pv = PBb(pi).rearrange("p (j c) -> p j c", c=128)
                for j, ti in enumerate(tiles):
                    c0, c1 = tile_cols(ti)
                    R = c1 - c0
                    tr(pv[0:R, j, :], kkT[kb][:, c0:c1], identb[:, :], [R_kkT[kb], R_cb], [PR[pi]],
                       signal=(j == len(tiles) - 1))
                R = 32 if grp == 2 else 128
                nt_ = len(tiles)
                cp("act", kk_tok[0:R, tiles[0]:tiles[0] + nt_, 128 * h:128 * h + 128], pv[0:R, 0:nt_, :],
                   [PR[pi]], [R_kk[ti_] for ti_ in tiles])

        S.barrier()
        A.l = mark_P1out

        if K_STOP == 'P1':
            S.finish()
            return nc
        def s5t(shape):
            return A.alloc(shape, F32)
        lr = s5t([128, 16]); li = s5t([128, 16]); dtt = s5t([128, 16])
        ar = s5t([128, 16]); ai = s5t([128, 16]); cr = s5t([128, 16]); ci = s5t([128, 16])
        mg16 = s5t([128, 16])
        t1 = s5t([128, 16]); t2 = s5t([128, 16]); t3 = s5t([128, 16]); t4 = s5t([128, 16])
        R_s5 = Reg()
        lpp = A.alloc([16, 3, 128], F32, right=True)
        R_lpp = Reg()
        S.dma("sp", lpp[:], lamPP.rearrange("a r c -> r a c"), writes=[R_lpp])
        pi = bank()
        for a_i in range(3):
            tr(PB(pi)[:, 16 * a_i:16 * a_i + 16], lpp[0:16, a_i, :], identf[0:16, 0:16], [R_lpp, R_cst], [PR[pi]],
               signal=(a_i == 2))
        cp("act", lr[:], PB(pi)[:, 0:16], [PR[pi]], [R_s5])
        cp("act", li[:], PB(pi)[:, 16:32], [PR[pi]], [R_s5])
        negone = s5t([128, 16])
        mset(SE, negone[:], -1.0, [R_s5])
        act(dtt[:], PB(pi)[:, 32:48], AF.Exp, [PR[pi]], [R_s5])
        R_ct = Reg()
        cdup = A.alloc([128, 2, 4, 128], F32, right=True)
        R_cdup = Reg()
        for ri in range(2):
            src = cch[ri].rearrange("(t p) s -> p t s", p=128)
            S.dma("sp", cdup[:, ri, :, 0:64], src, writes=[R_cdup])
            S.dma("sp", cdup[:, ri, :, 64:128], src, writes=[R_cdup])
        ct_re = A.alloc([128, 512], F32)
        nct_im = A.alloc([128, 512], F32)
        bbb = A.alloc([128, 16, 2, 32], BF16)
        for t in range(4):
            pi = bank()
            tr(PB(pi)[:, 0:128], cdup[:, 0, t, :], identf, [R_cdup, R_cst], [PR[pi]], signal=False)
            tr(PB(pi)[:, 128:256], cdup[:, 1, t, :], identf, [R_cdup, R_cst], [PR[pi]])
            cp("act", ct_re[:, 128 * t:128 * t + 128], PB(pi)[:, 0:128], [PR[pi]], [R_ct])
            act(nct_im[:, 128 * t:128 * t + 128], PB(pi)[:, 128:256], AF.Copy, [PR[pi]], [R_ct], scale=-1.0)
            tt(SE, ct_re[:, 128 * t:128 * t + 128], ct_re[:, 128 * t:128 * t + 128], C("blockmask"), ALU.mult,
               [R_ct, R_cst], [R_ct])
            tt(SE, nct_im[:, 128 * t:128 * t + 128], nct_im[:, 128 * t:128 * t + 128], C("blockmask"), ALU.mult,
               [R_ct, R_cst], [R_ct])
        s5r = [R_s5]

        def e_tt(out, a, b, op):
            return tt(SE, out, a, b, op, s5r, s5r)

        e_tt(t1[:], lr[:], dtt[:], ALU.mult)
        act(t2[:], t1[:], AF.Exp, s5r, s5r)
        act(mg16[:], t1[:], AF.Exp, s5r, s5r, scale=16.0)
        e_tt(t1[:], li[:], dtt[:], ALU.mult)
        act(ai[:], t1[:], AF.Sin, s5r, s5r, scale=1.0 / 8)
        act(t3[:], t1[:], AF.Sin, s5r, s5r, scale=1.0 / 16)
        e_tt(t3[:], t3[:], t3[:], ALU.mult)
        ts(SE, ar[:], t3[:], -2.0, 1.0, ALU.mult, ALU.add, s5r, s5r)
        for _ in range(3):
            e_tt(t3[:], ar[:], ar[:], ALU.mult)
            e_tt(t4[:], ai[:], ai[:], ALU.mult)
            e_tt(ai[:], ar[:], ai[:], ALU.mult)
            ts(SE, ai[:], ai[:], 2.0, None, ALU.mult, None, s5r, s5r)
            e_tt(ar[:], t3[:], t4[:], ALU.subtract)
        e_tt(ar[:], ar[:], t2[:], ALU.mult)
        e_tt(ai[:], ai[:], t2[:], ALU.mult)
        ts(SE, t1[:], ar[:], -1.0, None, ALU.add, None, s5r, s5r)
        e_tt(t3[:], lr[:], lr[:], ALU.mult)
        e_tt(t4[:], li[:], li[:], ALU.mult)
        e_tt(t3[:], t3[:], t4[:], ALU.add)
        e_tt(t3[:], t3[:], negone[:], ALU.pow)
        e_tt(cr[:], t1[:], lr[:], ALU.mult)
        e_tt(t4[:], ai[:], li[:], ALU.mult)
        e_tt(cr[:], cr[:], t4[:], ALU.add)
        e_tt(cr[:], cr[:], t3[:], ALU.mult)
        e_tt(ci[:], ai[:], lr[:], ALU.mult)
        e_tt(t4[:], t1[:], li[:], ALU.mult)
        e_tt(ci[:], ci[:], t4[:], ALU.subtract)
        e_tt(ci[:], ci[:], t3[:], ALU.mult)

        def pow_table(tr_, ti_, nmax, tmpa, tmpb):
            mset(SE, tr_[:, :, 0:1], 1.0, s5r)
            mset(SE, ti_[:, :, 0:1], 0.0, s5r)
            n = 1
            while n < nmax:
                m = min(n, nmax - n)
                mr = bc(tr_[:, :, n:n + 1], [128, 16, m])
                mi = bc(ti_[:, :, n:n + 1], [128, 16, m])
                sr = tr_[:, :, 1:1 + m]
                si = ti_[:, :, 1:1 + m]
                ta = tmpa[:, :, 0:m]
                tb = tmpb[:, :, 0:m]
                e_tt(ta, sr, mr, ALU.mult)
                e_tt(tb, si, mi, ALU.mult)
                e_tt(tr_[:, :, n + 1:n + 1 + m], ta, tb, ALU.subtract)
                e_tt(ta, sr, mi, ALU.mult)
                e_tt(tb, si, mr, ALU.mult)
                e_tt(ti_[:, :, n + 1:n + 1 + m], ta, tb, ALU.add)
                n += m

        apr = A.alloc([128, 16, 17], F32)
        api = A.alloc([128, 16, 17], F32)
        tabc = A.alloc([128, 16, 130], F32)
        tabs = A.alloc([128, 16, 130], F32)
        wt1_off = A.l
        wt1 = A.alloc([128, 8, 128], F32)
        wt2_off = A.l
        wt2 = A.alloc([128, 8, 128], F32)
        ptA = wt1[:].rearrange("p k c -> p (k c)").rearrange("p (a n) -> p a n", n=64)
        ptB = wt2[:].rearrange("p k c -> p (k c)").rearrange("p (a n) -> p a n", n=64)
        wst0 = A.alloc([128, 16, 2, 128], BF16)
        pf0 = A.alloc([128, 17, 2, 128], BF16)
        cp(SE, apr[:, :, 1:2], ar[:].unsqueeze(2), s5r, s5r)
        cp(SE, api[:, :, 1:2], ai[:].unsqueeze(2), s5r, s5r)
        pow_table(apr, api, 16, ptA, ptB)
        e_tt(t1[:], mg16[:], negone[:], ALU.pow)
        e_tt(tabc[:, :, 1:2], apr[:, :, 16:17], t1[:].unsqueeze(2), ALU.mult)
        e_tt(tabs[:, :, 1:2], api[:, :, 16:17], t1[:].unsqueeze(2), ALU.mult)
        pow_table(tabc, tabs, 129, ptA, ptB)

        bld = A.alloc([128, 2, 16, 16], F32, right=True)
        R_bld = Reg()
        for ri in range(2):
            S.dma("sp", bld[:, ri, :, :], bst[ri].rearrange("a p c -> p a c"), writes=[R_bld])
        bb_re = A.alloc([128, 16, 32], F32)
        bb_im = A.alloc([128, 16, 32], F32)
        bt1 = A.alloc([128, 16, 16], F32, right=True)
        bt2 = A.alloc([128, 16, 16], F32, right=True)
        mset(SE, bb_re[:], 0.0, s5r)
        mset(SE, bb_im[:], 0.0, s5r)
        rb = s5r + [R_bld]
        for two in range(2):
            ps_ = slice(64 * two, 64 * two + 64)
            cs_ = slice(16 * two, 16 * two + 16)
            crb = bc(cr[ps_, :].unsqueeze(2), [64, 16, 16])
            cib = bc(ci[ps_, :].unsqueeze(2), [64, 16, 16])
            tt(SE, bt1[ps_], bld[ps_, 0], crb, ALU.mult, rb, s5r)
            tt(SE, bt2[ps_], bld[ps_, 1], cib, ALU.mult, rb, s5r)
            tt(SE, bb_re[ps_, :, cs_], bt1[ps_], bt2[ps_], ALU.subtract, s5r, s5r)
            tt(SE, bt1[ps_], bld[ps_, 1], crb, ALU.mult, rb, s5r)
            tt(SE, bt2[ps_], bld[ps_, 0], cib, ALU.mult, rb, s5r)
            tt(SE, bb_im[ps_, :, cs_], bt1[ps_], bt2[ps_], ALU.add, s5r, s5r)

        cp(SE, bbb[:, :, 0, :], bb_re[:], s5r, s5r)
        cp(SE, bbb[:, :, 1, :], bb_im[:], s5r, s5r)

        s5r = [R_s5, R_ct]
        wst = [wst0]
        pf = [pf0]
        R_wst = [Reg(), Reg()]; R_pf = [Reg(), Reg()]
        R_wt12 = Reg()
        def s5_tables(t, part):
            tb = t % 2
            prs = slice(4 * t, 4 * t + 4)
            wgroups = [(0, 8, "pool"), (8, 8, "pool")]
            for (k0_, nk_, eng_) in wgroups:
                if part not in ("wst", "wst_all_pool"):
                    continue
                ks = slice(k0_, k0_ + nk_)
                a_r = bc(apr[:, prs, ks].rearrange("p a k -> p k a").unsqueeze(3), [128, nk_, 4, 32])
                a_i = bc(api[:, prs, ks].rearrange("p a k -> p k a").unsqueeze(3), [128, nk_, 4, 32])
                b_r = bc(bb_re[:, prs, :].unsqueeze(1), [128, nk_, 4, 32])
                b_i = bc(bb_im[:, prs, :].unsqueeze(1), [128, nk_, 4, 32])
                wa, wb_, R_w = wt1, wt2, R_wt12
                w1 = wa[:, 0:nk_, :].rearrange("p k (a c) -> p k a c", c=32)
                w2 = wb_[:, 0:nk_, :].rearrange("p k (a c) -> p k a c", c=32)
                o_r = wst[tb][:, ks, 0, :].rearrange("p k (a c) -> p k a c", c=32)
                o_i = wst[tb][:, ks, 1, :].rearrange("p k (a c) -> p k a c", c=32)
                rw = s5r + [R_w]
                tt(eng_, w1, a_r, b_r, ALU.mult, s5r, [R_w])
                tt(eng_, w2, a_i, b_i, ALU.mult, s5r, [R_w])
                tt(eng_, o_r, w1, w2, ALU.subtract, rw, [R_wst[tb]])
                tt(eng_, w1, a_r, b_i, ALU.mult, rw, [R_w])
                tt(eng_, w2, a_i, b_r, ALU.mult, rw, [R_w])
                tt(eng_, o_i, w1, w2, ALU.add, rw, [R_wst[tb]])
            if part == "up":
                cp("act", uP[tb][:, :, 0:128], uT[:, t, 0:2048].rearrange("p (c s) -> p s c", s=16),
                   creg(R_u, 0, 2048), [R_uP[tb]])
                cp("act", uP[tb][:, :, 128], uT[:, t, 2048:2064], [R_u[16]], [R_uP[tb]])
            for (k0, nk) in ((0, 8), (8, 4), (12, 4), (16, 1)):
                on_pool = (k0 < 12)
                if part not in ("pfp", "dve") or on_pool != (part == "pfp"):
                    continue
                eng_ = "pool" if on_pool else "dve"
                if on_pool:
                    wa, wb_, R_w = wt1, wt2, R_wt12
                else:
                    wa, wb_, R_w = wt3, wt4, R_wt34
                ks = slice(k0, k0 + nk)
                a_r = bc(apr[:, prs, ks].rearrange("p a k -> p k a").unsqueeze(3), [128, nk, 4, 32])
                a_i = bc(api[:, prs, ks].rearrange("p a k -> p k a").unsqueeze(3), [128, nk, 4, 32])
                c_r = bc(ct_re[:, 128 * t:128 * t + 128].rearrange("p (a c) -> p a c", c=32).unsqueeze(1), [128, nk, 4, 32])
                c_i = bc(nct_im[:, 128 * t:128 * t + 128].rearrange("p (a c) -> p a c", c=32).unsqueeze(1), [128, nk, 4, 32])
                w1 = wa[:, 0:nk, :].rearrange("p k (a c) -> p k a c", c=32)
                w2 = wb_[:, 0:nk, :].rearrange("p k (a c) -> p k a c", c=32)
                o_r = pf[tb][:, ks, 0, :].rearrange("p k (a c) -> p k a c", c=32)
                o_i = pf[tb][:, ks, 1, :].rearrange("p k (a c) -> p k a c", c=32)
                rw = s5r + [R_w]
                tt(eng_, w1, a_r, c_r, ALU.mult, s5r, [R_w])
                tt(eng_, w2, a_i, c_i, ALU.mult, s5r, [R_w])
                tt(eng_, o_r, w1, w2, ALU.add, rw, [R_pf[tb]])
                tt(eng_, w1, a_r, c_i, ALU.mult, rw, [R_w])
                tt(eng_, w2, a_i, c_r, ALU.mult, rw, [R_w])
                tt(eng_, o_i, w1, w2, ALU.subtract, rw, [R_pf[tb]])

        s5_tables(0, "wst_all_pool")
        s5_tables(0, "pfp")
        if K_STOP == 'P2base':
            S.finish()
            return nc
        mark_base = A.l
        Sf = A.alloc([128, 4, 128], F32)
        Sb = A.alloc([128, 4, 128], BF16)
        attsb = A.alloc([128, 4, 64], BF16)
        mix_off = SB_HI - 128 * 0 - (8 * NT * 2)
        oblk = nc.alloc_sbuf_tensor_at("oblk", [128, 4, 512], F32, offset=mix_off)
        sqb = nc.alloc_sbuf_tensor_at("sqb", [128, 4, 512], BF16, offset=mix_off + 8192)
        rsb = nc.alloc_sbuf_tensor_at("rsb", [128, 512], F32, offset=mix_off + 12288)
        otmp = nc.alloc_sbuf_tensor_at("otmp", [128, 512], F32, offset=mix_off + 14336)
        R_Sf = Reg(); R_Sb = Reg(); R_att = Reg(); R_ob = Reg(); R_sq = Reg(); R_rs = Reg(); R_ot = Reg()
        hgngT = V("hgng")

        def hg_norm_block(c0, W, rd_extra):
            for h in range(4):
                act(sqb[:, h, 0:W], oblk[:, h, 0:W], AF.Square, [R_ob] + rd_extra, [R_sq])
                pi = bank()
                mm(PB(pi, W), onesb[:, :], sqb[:, h, 0:W], True, True, [R_cb, R_sq], [PR[pi]])
                act(rsb[:, 0:W], PB(pi, W), AF.Ln, [PR[pi]], [R_rs], scale=1.0 / 128, bias=EPS)
                act(rsb[:, 0:W], rsb[:, 0:W], AF.Exp, [R_rs], [R_rs], scale=-0.5)
                stt("dve", otmp[:, 0:W], oblk[:, h, 0:W], hgngT[:, h:h + 1], rsb[:, 0:W], ALU.mult, ALU.mult,
                    [R_ob, R_rs, R_vec], [R_ot])
                mr = creg(R_mix, c0, c0 + W)
                tt("dve", mixT[:, 4 + h, c0:c0 + W], mixT[:, 4 + h, c0:c0 + W], otmp[:, 0:W], ALU.mult,
                   [R_ot] + mr, mr)

        attsbs = [attsb] + [A.alloc([128, 4, 64], BF16) for _ in range(2)]
        R_atts = [R_att, Reg(), Reg()]

        def hg_geo(ci):
            if ci < 0:
                return 2048, 16, 16, 0, 0
            return 64 * ci, 64, ci // 2, 64 * (ci % 2), ci + 1

        hg_state = {}

        def hg_X(ci):
            c0, Wc, ti, r0, ei = hg_geo(ci)
            rows = slice(r0, r0 + Wc)
            sl = (ci + 1) % 3
            pa = bank()
            for h in range(4):
                mm(PB(pa)[rows, 64 * h:64 * h + Wc], kT[:, h, c0:c0 + Wc], qT[:, h, c0:c0 + Wc], True, True,
                   creg(R_k[h], c0, c0 + Wc) + creg(R_q[h], c0, c0 + Wc), [PR[pa]], tp=(0, r0), signal=(h == 3))
            tt("dve", attsbs[sl][rows, :, 0:Wc], PB(pa)[rows, 0:256].rearrange("p (h c) -> p h c", c=64)[:, :, 0:Wc],
               bc(C("attmask")[rows, 0:Wc].unsqueeze(1), [Wc, 4, Wc]), ALU.mult, [PR[pa], R_cst], [R_atts[sl]])

        def hg_Y(ci):
            c0, Wc, ti, r0, ei = hg_geo(ci)
            rows = slice(r0, r0 + Wc)
            pS = bank()
            for h in range(4):
                hc = slice(128 * h, 128 * h + 128)
                mm(PB(pS)[:, hc], kk_tok[rows, ti, hc], v_tok[rows, ti, hc], True, True,
                   [R_kk[ti], R_v[ti]], [PR[pS]], tp=(r0, 0), signal=(h == 3))
            hg_state[ci] = pS
            reserved.add(pS)

        def hg_Z(ci):
            c0, Wc, ti, r0, ei = hg_geo(ci)
            rows = slice(r0, r0 + Wc)
            sl = (ci + 1) % 3
            po = bank()
            for h in range(4):
                hc = slice(128 * h, 128 * h + 128)
                if ci >= 0:
                    mm(PB(po)[:, 64 * h:64 * h + Wc], Sb[:, h, :], qT[:, h, c0:c0 + Wc], True, False,
                       [R_Sb] + creg(R_q[h], c0, c0 + Wc), [PR[po]])
                mm(PB(po)[:, 64 * h:64 * h + Wc], v_tok[rows, ti, hc], attsbs[sl][rows, h, 0:Wc], ci < 0, True,
                   [R_v[ti], R_atts[sl]], [PR[po]], tp=(r0, 0), signal=(h == 3))
            ocol = (c0 % 512) if ci >= 0 else 0
            cp("act", oblk[:, :, ocol:ocol + Wc], PB(po)[:, 0:256].rearrange("p (h c) -> p h c", c=64)[:, :, 0:Wc],
               [PR[po]], [R_ob])
            pS = hg_state.pop(ci)
            reserved.discard(pS)
            if ci < 0:
                cp("dve", Sf[:].rearrange("p h v -> p (h v)"), PB(pS)[:, 0:512], [PR[pS]], [R_Sf])
            else:
                for h in range(4):
                    stt("dve", Sf[:, h, :], Sf[:, h, :], ebl[:, h, ei:ei + 1], PB(pS)[:, 128 * h:128 * h + 128],
                        ALU.mult, ALU.add, [R_Sf, R_ebl, PR[pS]], [R_Sf])
            cp("act", Sb[:].rearrange("p h v -> p (h v)"), Sf[:].rearrange("p h v -> p (h v)"), [R_Sf], [R_Sb])

        NSB = 4
        s0 = [A.alloc([128, 4, 128], F32) for _ in range(NSB)]
        s1b = [A.alloc([128, 4, 128], BF16) for _ in range(2)]
        kkm = [A.alloc([32, 512], BF16) for _ in range(3)]
        R_s0 = [Reg() for _ in range(NSB)]
        R_s1b = [Reg(), Reg()]
        R_kkm = [Reg(), Reg(), Reg()]

        def hg_sample_load(n):
            S.dma("sp", s0[n % NSB][:], hgs[n].rearrange("h k v -> k h v"), writes=[R_s0[n % NSB]])

        def hg_kkm(n):
            act(kkm[n % 3][0:32, :], kk_tok[0:32, 16, :], AF.Copy, [R_kk[16], R_cst], [R_kkm[n % 3]],
                scale=C("onehot", 32)[:, n:n + 1])

        def hg_s1(n):
            b = n % 2
            kb = n % 3
            sb_ = n % NSB
            if n >= NSB:
                hg_sample_load(n)
            if n + 2 < 16:
                hg_kkm(n + 2)
            pk = bank()
            for h in range(4):
                hc = slice(128 * h, 128 * h + 128)
                mm(PB(pk)[:, hc], kkm[kb][0:32, hc], v_tok[0:32, 16, hc], True, True,
                   [R_kkm[kb], R_v[16]], [PR[pk]], signal=(h == 3))
            s0f = s0[sb_][:].rearrange("p h v -> p (h v)")
            tt("dve", s0[sb_][:], s0[sb_][:], bc(fsamp[:, :, n:n + 1], [128, 4, 128]), ALU.mult,
               [R_s0[sb_], R_ebl], [R_s0[sb_]])
            tt("dve", s0f, s0f, PB(pk)[:, 0:512], ALU.add, [R_s0[sb_], PR[pk]], [R_s0[sb_]])
            cp("act", s1b[b][:].rearrange("p h v -> p (h v)"), s0f, [R_s0[sb_]], [R_s1b[b]])
            S.dma("act", o_hgs[n].rearrange("h k v -> k h v"), s0[sb_][:], reads=[R_s0[sb_]])

        def hg_s2(n, po):
            b = n % 2
            for h in range(4):
                mm(PB(po)[:, 16 * h + n:16 * h + n + 1], s1b[b][:, h, :], qT[:, h, 2064 + n:2065 + n], True, True,
                   [R_s1b[b], R_q[h][16]], [PR[po]], signal=(h == 3))

        def hg_samples(n_list, po):
            hg_kkm(0)
            hg_kkm(1)
            hg_s1(n_list[0])
            for i_ in range(1, len(n_list)):
                hg_s1(n_list[i_])
                hg_s2(n_list[i_ - 1], po)
            hg_s2(n_list[-1], po)

        for n_ in range(NSB):
            hg_sample_load(n_)

        ometa = A.alloc([128, 4, 32], F32)
        R_om = Reg()
        order = [-1] + list(range(32))
        LA = 2
        for i_ in range(min(LA, len(order))):
            hg_X(order[i_])
            hg_Y(order[i_])
        po_s = bank()
        reserved.add(po_s)
        hg_kkm(0)
        hg_kkm(1)
        ns_done = 0
        for i_, ci in enumerate(order):
            if i_ + LA < len(order):
                hg_X(order[i_ + LA])
                hg_Y(order[i_ + LA])
            hg_Z(ci)
            if ci < 0:
                cp("dve", ometa[:, :, 0:16], oblk[:, :, 0:16], [R_ob], [R_om])
            elif ci % 8 == 7:
                hg_norm_block(512 * (ci // 8), 512, [])
            if ci >= 0 and ci % 2 == 0 and ns_done < 16:
                hg_s1(ns_done)
                if ns_done >= 1:
                    hg_s2(ns_done - 1, po_s)
                ns_done += 1
        hg_s2(15, po_s)
        reserved.discard(po_s)
        for h in range(4):
            S.dma("sp", o_hgp[h], Sf[:, h, :], reads=[R_Sf])
        cp("act", ometa[:, :, 16:32], PB(po_s)[:, 0:64].rearrange("p (h n) -> p h n", n=16), [PR[po_s]], [R_om])
        cp("dve", oblk[:, :, 0:32], ometa[:, :, :], [R_om, R_ob], [R_ob])
        hg_norm_block(2048, 32, [])

        S.barrier()
        A.l = mark_base
        A.r = mark_R
        if K_STOP == 'P2a':
            S.finish()
            return nc
        s5tok = A.alloc([16, 2, 2048], F32)
        R_s5tok = Reg()
        S.dma("sp", s5tok[:], s5s.rearrange("r n c -> n r c"), writes=[R_s5tok])
        h0 = A.alloc([128, 16, 2, 16], F32)
        h0b = A.alloc([128, 16, 2, 16], BF16)
        R_h0 = Reg()
        for half in range(2):
            pi = bank()
            for pl in range(8):
                pair = 8 * half + pl
                for ri in range(2):
                    tr(PB(pi)[:, (pl * 2 + ri) * 16:(pl * 2 + ri) * 16 + 16], s5tok[0:16, ri, pair * 128:(pair + 1) * 128],
                       identf[0:16, 0:16], [R_s5tok, R_cst], [PR[pi]], signal=(pl == 7 and ri == 1))
            cp("act", h0[:, 8 * half:8 * half + 8, :, :].rearrange("p a r n -> p (a r n)"), PB(pi)[:, 0:256],
               [PR[pi]], [R_h0])
        cp("dve", h0b[:].rearrange("p a r n -> p (a r n)"), h0[:].rearrange("p a r n -> p (a r n)"), [R_h0], [R_h0])

        wst.append(A.alloc([128, 16, 2, 128], BF16))
        pf.append(A.alloc([128, 17, 2, 128], BF16))
        wd = A.alloc([128, 16, 2, 128], BF16)
        knf = [A.alloc([128, 16, 128], BF16) for _ in range(2)]
        wt3 = A.alloc([128, 4, 128], F32)
        wt4 = A.alloc([128, 4, 128], F32)
        dsb = A.alloc([128, 4, 2, 146], F32)
        drot = A.alloc([128, 4, 2, 128], F32)
        gsc = A.alloc([128, 4, 2, 128], F32)
        sbuf_ = A.alloc([128, 4, 2, 129], F32)
        sfar = A.alloc([128, 4, 2, 128], BF16)
        dt1 = A.alloc([128, 4, 128], F32)
        dt2 = A.alloc([128, 4, 128], F32)
        g1 = A.alloc([128, 4, 2], F32)
        sfin = A.alloc([128, 2, 16], F32)
        s1s = A.alloc([128, 16, 2, 16], F32)
        uP = [A.alloc([128, 16, 130], BF16) for _ in range(2)]
        R_uP = [Reg(), Reg()]
        gtmps = [A.alloc([128, 512], F32) for _ in range(2)]
        R_gts = [Reg(), Reg()]
        gtmp = gtmps[0]
        R_gt = R_gts[0]
        R_wd = Reg(); R_knf = [Reg(), Reg()]
        R_wt34 = Reg()
        ktmp_t = A.alloc([128, 512], F32)
        R_kt = Reg()
        R_d = Reg(); R_sfar = Reg(); R_sfin = Reg(); R_s1s = Reg()
        sdT = V("s5d")
        w_glu_sb = A.alloc([128, 4, 512], BF16)
        R_wglu = Reg()
        S.dma("pool", w_glu_sb[:], w_glu.rearrange("(kt p) c -> p kt c", p=128), writes=[R_wglu])

        def s5_F(t):
            tb = t % 2
            prs = slice(4 * t, 4 * t + 4)
            for grp in range(4):
                pi = bank()
                pv = PBb(pi).rearrange("p (j c) -> p j c", c=128)
                for j in range(8):
                    idx = grp * 8 + j
                    k, ri = idx // 2, idx % 2
                    tr(pv[:, j, :], wst[tb][:, k, ri, :], identb[:, :], [R_wst[tb], R_cb], [PR[pi]], signal=(j == 7))
                cp("act", wd[:, 4 * grp:4 * grp + 4, :, :].rearrange("p k r c -> p (k r c)"), PBb(pi)[:, :],
                   [PR[pi]], [R_wd])
            dbanks = [bank(), bank(), bank(), bank()]
            ur = R_u
            for q in range(4):
                for ri in range(2):
                    db = dbanks[q]
                    o0 = 160 * ri
                    for s_ in range(16):
                        mm(PB(db)[:, o0:o0 + 129], wd[32 * q:32 * q + 32, 15 - s_, ri, :],
                           uP[tb][32 * q:32 * q + 32, s_, 0:129], s_ == 0, s_ == 15, [R_wd, R_uP[tb]], [PR[db]],
                           tp=(32 * q, 0))
                    mm(PB(db)[:, o0 + 130:o0 + 146], wd[32 * q:32 * q + 32, 0, ri, :],
                       uT[32 * q:32 * q + 32, t, 2064:2080], True, True, [R_wd, R_u[16]], [PR[db]], tp=(32 * q, 0))
            for q in range(4):
                db = dbanks[q]
                cp("act", dsb[:, q, :, :], PB(db)[:, 0:320].rearrange("p (r c) -> p r c", c=160)[:, :, 0:146],
                   [PR[db]], [R_d])

        def s5_K(t):
            tb = t % 2
            prs = slice(4 * t, 4 * t + 4)
            pi = bank()
            for q in range(4):
                pair = 4 * t + q
                for ri in range(2):
                    mm(PB(pi)[32 * q:32 * q + 32, :], bbb[:, pair, ri, :], pf[tb][:, 0:16, ri, 32 * q:32 * q + 32],
                       ri == 0, ri == 1, [R_pf[tb], R_s5], [PR[pi]], tp=(0, 32 * q), signal=(q == 3 and ri == 1))
            ktmp = ktmp_t[:, :]
            cp("act", ktmp, PB(pi)[:, :], [PR[pi]], [R_kt])
            stt("dve", ktmp[:, 0:32], C("blockident"), sdT[:, t:t + 1], ktmp[:, 0:32], ALU.mult, ALU.add,
                [R_kt, R_cst, R_vec], [R_kt])
            tt("dve", knf[tb][:].rearrange("p k (a c) -> p k a c", c=32),
               bc(ktmp.rearrange("p (k c) -> p k c", c=32).unsqueeze(2), [128, 16, 4, 32]),
               bc(C("qmask").rearrange("p (a c) -> p a c", c=32).unsqueeze(1), [128, 16, 4, 32]), ALU.mult,
               [R_kt, R_cst], [R_knf[tb]])

        def s5_M(t):
            tb = t % 2
            prs = slice(4 * t, 4 * t + 4)
            rd = [R_d] + s5r
            tcs = tabc[:, prs, 2:130]
            tss = tabs[:, prs, 2:130]
            dre = dsb[:, :, 0, 0:128]
            dim_ = dsb[:, :, 1, 0:128]
            tt("dve", dt1[:], dre, tcs, ALU.mult, rd, [R_d])
            tt("dve", dt2[:], dim_, tss, ALU.mult, rd, [R_d])
            tt("dve", drot[:, :, 0, :], dt1[:], dt2[:], ALU.add, rd, [R_d])
            tt("dve", dt1[:], dim_, tcs, ALU.mult, rd, [R_d])
            tt("dve", dt2[:], dre, tss, ALU.mult, rd, [R_d])
            tt("dve", drot[:, :, 1, :], dt1[:], dt2[:], ALU.subtract, rd, [R_d])
            tc1 = tabc[:, prs, 1]
            ts1 = tabs[:, prs, 1]
            dmr = dsb[:, :, 0, 128]
            dmi = dsb[:, :, 1, 128]
            tt("dve", dt1[:, :, 0], dmr, tc1, ALU.mult, rd, [R_d])
            tt("dve", dt2[:, :, 0], dmi, ts1, ALU.mult, rd, [R_d])
            tt("dve", g1[:, :, 0], dt1[:, :, 0], dt2[:, :, 0], ALU.add, rd, [R_d])
            tt("dve", dt1[:, :, 0], dmi, tc1, ALU.mult, rd, [R_d])
            tt("dve", dt2[:, :, 0], dmr, ts1, ALU.mult, rd, [R_d])
            tt("dve", g1[:, :, 1], dt1[:, :, 0], dt2[:, :, 0], ALU.subtract, rd, [R_d])
            for q in range(4):
                pair = 4 * t + q
                for ri in range(2):
                    S.op("dve", lambda e, q=q, ri=ri, pair=pair: e.tensor_tensor_scan(
                        out=gsc[:, q, ri, :], data0=bc(mg16[:, pair:pair + 1], [128, 128]), data1=drot[:, q, ri, :],
                        initial=g1[:, q, ri:ri + 1], op0=ALU.mult, op1=ALU.add), reads=rd, writes=[R_d])
            gre = gsc[:, :, 0, :]
            gim = gsc[:, :, 1, :]
            tt("dve", dt1[:], gre, tcs, ALU.mult, rd, [R_d])
            tt("dve", dt2[:], gim, tss, ALU.mult, rd, [R_d])
            tt("dve", sbuf_[:, :, 0, 1:129], dt1[:], dt2[:], ALU.subtract, rd, [R_d])
            tt("dve", dt1[:], gim, tcs, ALU.mult, rd, [R_d])
            tt("dve", dt2[:], gre, tss, ALU.mult, rd, [R_d])
            tt("dve", sbuf_[:, :, 1, 1:129], dt1[:], dt2[:], ALU.add, rd, [R_d])
            cp("dve", sbuf_[:, :, :, 0], dsb[:, :, :, 128], rd, [R_d])
            cp("dve", sfar[:], sbuf_[:, :, :, 0:128], rd, [R_sfar])
            cp("dve", sfin[:, :, prs].rearrange("p r a -> p a r"), sbuf_[:, :, :, 128], rd, [R_sfin])
            arb = bc(ar[:, prs].unsqueeze(2), [128, 4, 16])
            aib = bc(ai[:, prs].unsqueeze(2), [128, 4, 16])
            h0r = h0[:, prs, 0, :]
            h0i = h0[:, prs, 1, :]
            e1 = dt1[:, :, 0:16]
            e2 = dt2[:, :, 0:16]
            rh = rd + [R_h0]
            tt("dve", e1, h0r, arb, ALU.mult, rh, [R_d])
            tt("dve", e2, h0i, aib, ALU.mult, rh, [R_d])
            tt("dve", e1, e1, e2, ALU.subtract, rd, [R_d])
            tt("dve", s1s[:, prs, 0, :], e1, dsb[:, :, 0, 130:146], ALU.add, rd, [R_s1s])
            tt("dve", e1, h0i, arb, ALU.mult, rh, [R_d])
            tt("dve", e2, h0r, aib, ALU.mult, rh, [R_d])
            tt("dve", e1, e1, e2, ALU.add, rd, [R_d])
            tt("dve", s1s[:, prs, 1, :], e1, dsb[:, :, 1, 130:146], ALU.add, rd, [R_s1s])

        def s5_B(t):
            tb = t % 2
            prs = slice(4 * t, 4 * t + 4)

            def gelu(x, g_a, mo, pr, mregs, R_g):
                act(g_a, x, AF.Square, [pr], [R_g])
                act(g_a, g_a, AF.Identity, [R_g], [R_g], scale=0.044715, bias=1.0)
                tt("dve", g_a, g_a, x, ALU.mult, [R_g, pr], [R_g])
                act(g_a, g_a, AF.Sigmoid, [R_g], [R_g], scale=1.5957691216057308)
                tt("dve", mo, g_a, x, ALU.mult, [R_g, pr], mregs)
            uv = uP[tb][:, :, 0:128]
            ureg = [R_uP[tb]]
            mview = mixT[:, t, 0:2048].rearrange("p (c s) -> p c s", s=16)
            for b_ in range(4):
                yb_ = bank()
                for k in range(0, 4 * b_ + 4):
                    s_lo = max(k, 4 * b_)
                    mm(PB(yb_)[:, (s_lo - 4 * b_) * 128:512], knf[tb][:, k, :], uv[:, s_lo - k:4 * b_ + 4 - k, :],
                       k == 0, False, [R_knf[tb]] + ureg, [PR[yb_]])
                for k in range(4 * b_, 4 * b_ + 4):
                    for q in range(4):
                        for ri in range(2):
                            last = (k == 4 * b_ + 3 and q == 3 and ri == 1)
                            mm(PB(yb_)[32 * q:32 * q + 32, (k - 4 * b_) * 128:(k - 4 * b_) * 128 + 128],
                               pf[tb][:, k + 1, ri, 32 * q:32 * q + 32], sfar[:, q, ri, :], False, last,
                               [R_pf[tb], R_sfar], [PR[yb_]], tp=(0, 32 * q), signal=last)
                gelu(PB(yb_).rearrange("p (s c) -> p c s", c=128),
                     gtmps[b_ % 2][:, :].rearrange("p (c s) -> p c s", s=4),
                     mview[:, :, 4 * b_:4 * b_ + 4], PR[yb_], creg(R_mix, 0, 2048), R_gts[b_ % 2])
            p4 = bank()
            ureg4 = [R_u[16]]
            mm(PB(p4, 32), knf[tb][:, 0, :], uT[:, t, 2048:2080], True, False, [R_knf[tb]] + ureg4, [PR[p4]])
            for k in range(1, 16):
                mm(PB(p4)[:, k:16], knf[tb][:, k, :], uT[:, t, 2048:2048 + 16 - k], False, False,
                   [R_knf[tb]] + ureg4, [PR[p4]])
            for q in range(4):
                pair = 4 * t + q
                for ri in range(2):
                    last = (q == 3 and ri == 1)
                    mm(PB(p4)[32 * q:32 * q + 32, 16:32], pf[tb][:, 1, ri, 32 * q:32 * q + 32],
                       h0b[:, pair, ri, :], False, last, [R_pf[tb], R_h0], [PR[p4]], tp=(0, 32 * q), signal=last)
            gelu(PB(p4, 32), gtmps[0][:, 0:32], mixT[:, t, 2048:2080], PR[p4], [R_mix[16]], R_gts[0])

        if K_STOP == 'T0':
            S.finish()
            return nc
        try:
            s5_tables(0, "up")
            s5_tables(0, "dve")
            s5_tables(1, "wst")
            s5_tables(1, "pfp")
            s5_tables(1, "up")
            s5_F(0)
            s5_K(0)
            for t in range(4):
                s5_M(t)
                if t + 1 < 4:
                    s5_tables(t + 1, "dve")
                    if t + 2 < 4:
                        s5_tables(t + 2, "wst")
                    s5_F(t + 1)
                s5_B(t)
                if t + 2 < 4:
                    s5_tables(t + 2, "pfp")
                    s5_tables(t + 2, "up")
                if t + 1 < 4:
                    s5_K(t + 1)
        except _Stop:
            S.finish()
            return nc

        pi = bank()
        for ri in range(2):
            tr(PB(pi)[0:16, 128 * ri:128 * ri + 128], sfin[:, ri, :], identf, [R_sfin, R_cst], [PR[pi]], signal=(ri == 1))
        so = A.alloc([16, 256], F32)
        R_so = Reg()
        cp("act", so[:], PB(pi)[0:16, 0:256], [PR[pi]], [R_so])
        for ri in range(2):
            S.dma("sp", o_s5p[ri], so[0:16, 128 * ri:128 * ri + 128], reads=[R_so])
        sos = s5tok
        R_sos = R_s5tok
        for ri in range(2):
            for qd_ in range(4):
                pi = bank()
                for j in range(4):
                    pair = 4 * qd_ + j
                    tr(PB(pi)[0:16, 128 * j:128 * j + 128], s1s[:, pair, ri, :], identf, [R_s1s, R_cst], [PR[pi]],
                       signal=(j == 3))
                cp("act", sos[0:16, ri, 512 * qd_:512 * qd_ + 512], PB(pi)[0:16, :], [PR[pi]], [R_sos])
        S.dma("sp", o_s5s.rearrange("r n c -> n r c"), sos[:], reads=[R_sos])

        gates = [nc.alloc_sbuf_tensor_at("gate0", [128, 4, 512], BF16, offset=wt1_off),
                 nc.alloc_sbuf_tensor_at("gate1", [128, 4, 512], BF16, offset=wt2_off)]
        R_gates = [R_wt12, Reg()]
        bgT = V("bglu")
        for bi, (c0, c1) in enumerate(BLOCKS):
            W = c1 - c0
            mr = creg(R_mix, c0, c1)
            gate = gates[bi % 2]
            R_gate = R_gates[bi % 2]
            for to in range(4):
                pi = bank()
                for kt in range(4):
                    mm(PB(pi, W), w_glu_sb[:, kt, 128 * to:128 * to + 128], mixT[:, kt, c0:c1], kt == 0, kt == 3,
                       [R_wglu] + mr, [PR[pi]])
                act(gate[:, to, 0:W], PB(pi, W), AF.Sigmoid, [PR[pi], R_vec], [R_gate], bias=bgT[:, to:to + 1])
            tt("dve", mixT[:, 0:4, c0:c1], mixT[:, 0:4, c0:c1], gate[:, :, 0:W], ALU.mult, [R_gate] + mr, mr)

        S.barrier()
        A.l = mark_G

        if K_STOP == 'P2b':
            S.finish()
            return nc
        h1 = A.alloc([128, NTILES, D], F32)
        R_h1 = [Reg() for _ in range(NTILES)]
        hn2T = A.alloc([128, 8, NT], BF16)
        R_hn2 = [Reg() for _ in range(NTILES)]
        junk = A.alloc([128, D], BF16)
        ssb = [A.alloc([128, 2], F32) for _ in range(4)]
        R_junk = Reg()
        R_ss = [Reg() for _ in range(4)]
        mark_p3 = A.l
        w_out_sb = A.alloc([128, 8, D], BF16)
        R_woutk = [Reg() for _ in range(4)]
        w_out_v = w_out.rearrange("(kt p) c -> p kt c", p=128)
        for g_ in range(4):
            S.dma("pool", w_out_sb[:, 2 * g_:2 * g_ + 2, :], w_out_v[:, 2 * g_:2 * g_ + 2, :], writes=[R_woutk[g_]])
        xt = [A.alloc([128, D], F32) for _ in range(4)]
        xnb = [A.alloc([128, D], BF16) for _ in range(4)]
        R_xt = [Reg() for _ in range(4)]
        R_xnb = [Reg() for _ in range(4)]
        gffnT = V("gffn")
        def p3_A(ti):
            c0, c1 = tile_cols(ti)
            R = c1 - c0
            b = ti % 4
            S.dma("sp", xt[b][0:R, :], xtok[c0:c1, :], writes=[R_xt[b]])
            p2 = bank2()
            for hf in range(2):
                for kt in range(8):
                    mm(PB(p2 + hf)[0:R, :], mixT[:, kt, c0:c1], w_out_sb[:, kt, 512 * hf:512 * hf + 512], kt == 0, kt == 7,
                       [R_mix[ti], R_woutk[kt // 2]], [PR[p2 + hf]])
            hh = h1[0:R, ti, :]
            tt("dve", hh, PB(p2, n=2)[0:R, :], xt[b][0:R, :], ALU.add, [PR[p2], PR[p2 + 1], R_xt[b]], [R_h1[ti]])
            act(junk[0:R, :], hh, AF.Square, [R_h1[ti]], [R_junk, R_ss[b]], accum_out=ssb[b][0:R, 0:1])
            act(ssb[b][0:R, 1:2], ssb[b][0:R, 0:1], AF.Sqrt, [R_ss[b]], [R_ss[b]], scale=1.0 / D, bias=EPS)
            recip(ssb[b][0:R, 1:2], ssb[b][0:R, 1:2], [R_ss[b]], [R_ss[b]])
            ts("pool", xnb[b][0:R, :], hh, ssb[b][0:R, 1:2], 1.0, ALU.mult, ALU.mult, [R_h1[ti], R_ss[b]], [R_xnb[b]])

        def p3_B(ti):
            c0, c1 = tile_cols(ti)
            R = c1 - c0
            b = ti % 4
            pi = bank()
            pv = PBb(pi).rearrange("p (k r) -> p k r", r=128)
            for kt in range(8):
                tr(pv[:, kt, 0:R], xnb[b][0:R, kt * 128:(kt + 1) * 128], identb[0:R, 0:R],
                   [R_xnb[b], R_cb], [PR[pi]], signal=(kt == 7))
            tt("dve", hn2T[:, :, c0:c1], pv[:, :, 0:R], bc(gffnT.unsqueeze(2), [128, 8, R]), ALU.mult,
               [PR[pi], R_vec], [R_hn2[ti]])

        p3_A(0)
        p3_A(1)
        for ti in range(2, NTILES):
            p3_A(ti)
            p3_B(ti - 2)
        p3_B(NTILES - 2)
        p3_B(NTILES - 1)
        S.barrier()
        A.l = mark_p3
        A.r = SB_HI

        if K_STOP == 'P3':
            S.finish()
            return nc
        HWM = 1056
        mT_off = A.l
        mT = A.alloc([128, NJ, HWM], BF16)
        R_mT = Reg()
        cbuf = nc.alloc_sbuf_tensor_at("cbuf", [32, DFF], F32, offset=mT_off)
        wup_off = [A.l, A.l + 8192]
        wupb = [A.alloc([128, 8, 2, 256], BF16) for _ in range(2)]
        R_wup = [Reg() for _ in range(2)]
        wdnb = [nc.alloc_sbuf_tensor_at("wdn%d" % i, [128, NJ, 128], BF16, offset=wup_off[i]) for i in range(2)]
        R_wdn = R_wup
        arow = [A.alloc([128, 2 + HWM], F32) for _ in range(2)]
        crow = [A.alloc([128, 1040], F32) for _ in range(2)]
        vrow = [A.alloc([128, HWM], BF16) for _ in range(2)]
        R_ar = [Reg(), Reg()]
        R_cr = [Reg(), Reg()]
        R_vr = [Reg(), Reg()]
        carry = A.alloc([128, NJ, 2], F32)
        convp = A.alloc([128, 2, NJ], F32)
        aS = A.alloc([128, NJ, 16], F32)
        vS = A.alloc([128, NJ, 16], BF16)
        R_carry = Reg(); R_convp = Reg(); R_aS = Reg()
        yts = [A.alloc([128, 512], F32) for _ in range(2)]
        R_yt = [Reg(), Reg()]
        gfbc = A.alloc([128, D], F32)
        R_gf = Reg()
        S.dma("sp", gfbc[:], gfinal.partition_broadcast(128), writes=[R_gf])
        cwT = V("convw")
        cbT = V("convb")
        S.dma("sp", cbuf[:], cvs.rearrange("n r c -> (n r) c"), writes=[R_mT])
        bufT = A.alloc([128, NJ, 32], F32)
        R_bufT = Reg()
        for grp in range(2):
            pi = bank2()
            for jj in range(11):
                j = grp * 11 + jj
                tr(PB(pi, n=2)[:, 32 * jj:32 * jj + 32], cbuf[0:32, 128 * j:128 * j + 128], identf[0:32, 0:32],
                   [R_mT, R_cst], [PR[pi], PR[pi + 1]], signal=(jj == 10))
            cp("act", bufT[:, 11 * grp:11 * grp + 11, :].rearrange("p j c -> p (j c)"), PB(pi, n=2)[:, 0:352],
               [PR[pi], PR[pi + 1]], [R_bufT])
        S.dma("sp", o_cvs[:, 0, :], cvs[:, 1, :])

        w_up_v = w_up.rearrange("(kt p) c -> p kt c", p=128)
        w_dn_v = w_down.rearrange("(j p) c -> p j c", p=128)

        def final_norm(ti):
            c0, c1 = tile_cols(ti)
            R = c1 - c0
            b = ti % 2
            hh = h1[0:R, ti, :]
            act(junk[0:R, :], hh, AF.Square, [R_h1[ti]], [R_junk, R_ss[b]], accum_out=ssb[b][0:R, 0:1])
            act(ssb[b][0:R, 1:2], ssb[b][0:R, 0:1], AF.Sqrt, [R_ss[b]], [R_ss[b]], scale=1.0 / D, bias=EPS)
            recip(ssb[b][0:R, 1:2], ssb[b][0:R, 1:2], [R_ss[b]], [R_ss[b]])
            stt("dve", hh, hh, ssb[b][0:R, 1:2], gfbc[0:R, :], ALU.mult, ALU.mult,
                [R_h1[ti], R_ss[b], R_gf], [R_h1[ti]])
            S.dma("sp", y[c0:c1, :], hh, reads=[R_h1[ti]])

        for half in range(2):
            if half == 0:
                segs = [(0, 2048, 32), (16, 0, 512), (528, 512, 512)]
                nconv = 1040
            else:
                segs = [(0, 1024, 512), (512, 1536, 512)]
                nconv = 1024
            if half == 0:
                for b_ in range(2):
                    mset("pool", arow[b_][:, 0:2], 0.0, [R_ar[b_]])
            for j in range(NJ):
                wb = (j // 2) % 2
                jj = j % 2
                b = j % 2
                if jj == 0:
                    S.dma("pool", wupb[wb][:, :, 0, :], w_up_v[:, :, 128 * j:128 * j + 256], writes=[R_wup[wb]])
                    S.dma("pool", wupb[wb][:, :, 1, :], w_up_v[:, :, DFF + 128 * j:DFF + 128 * j + 256],
                          writes=[R_wup[wb]])
                for (m0, g0, W) in segs:
                    hr = creg(R_hn2, g0, g0 + W)
                    pa = bank()
                    for kt in range(8):
                        mm(PB(pa, W), wupb[wb][:, kt, 0, 128 * jj:128 * jj + 128], hn2T[:, kt, g0:g0 + W], kt == 0, kt == 7,
                           [R_wup[wb]] + hr, [PR[pa]])
                    pv_ = bank()
                    for kt in range(8):
                        mm(PB(pv_, W), wupb[wb][:, kt, 1, 128 * jj:128 * jj + 128], hn2T[:, kt, g0:g0 + W], kt == 0, kt == 7,
                           [R_wup[wb]] + hr, [PR[pv_]])
                    if g0 == 2048:
                        cp("act", arow[b][:, 2:18], PB(pa)[:, 0:16], [PR[pa]], [R_ar[b]])
                        cp("act", arow[b][:, 2 + 1040:2 + 1056], PB(pa)[:, 16:32], [PR[pa]], [R_ar[b]])
                        cp("act", vrow[b][:, 0:16], PB(pv_)[:, 0:16], [PR[pv_]], [R_vr[b]])
                        cp("act", vrow[b][:, 1040:1056], PB(pv_)[:, 16:32], [PR[pv_]], [R_vr[b]])
                        continue
                    cp("act", arow[b][:, 2 + m0:2 + m0 + W], PB(pa, W), [PR[pa]], [R_ar[b]])
                    cp("act", vrow[b][:, m0:m0 + W], PB(pv_, W), [PR[pv_]], [R_vr[b]])
                if half == 1:
                    cp("act", arow[b][:, 0:2], carry[:, j, :], [R_carry], [R_ar[b]])
                w0 = cwT[:, 0 * NJ + j:0 * NJ + j + 1]
                w1_ = cwT[:, 1 * NJ + j:1 * NJ + j + 1]
                w2_ = cwT[:, 2 * NJ + j:2 * NJ + j + 1]
                cc = crow[b][:, 0:nconv]
                ts("dve", cc, arow[b][:, 2:2 + nconv], w2_, cbT[:, j:j + 1], ALU.mult, ALU.add, [R_ar[b], R_vec], [R_cr[b]])
                stt("dve", cc, arow[b][:, 1:1 + nconv], w1_, cc, ALU.mult, ALU.add, [R_ar[b], R_vec, R_cr[b]], [R_cr[b]])
                stt("dve", cc, arow[b][:, 0:nconv], w0, cc, ALU.mult, ALU.add, [R_ar[b], R_vec, R_cr[b]], [R_cr[b]])
                act(cc, cc, AF.Silu, [R_cr[b]], [R_cr[b]])
                if half == 0:
                    tt("dve", mT[:, j, 0:16], crow[b][:, 0:16], vrow[b][:, 0:16], ALU.mult, [R_cr[b], R_vr[b]], [R_mT])
                    tt("dve", mT[:, j, 32:1056], crow[b][:, 16:1040], vrow[b][:, 16:1040], ALU.mult,
                       [R_cr[b], R_vr[b]], [R_mT])
                    cp("act", carry[:, j, :], arow[b][:, 2 + 1038:2 + 1040], [R_ar[b]], [R_carry])
                    cp("act", aS[:, j, :], arow[b][:, 2 + 1040:2 + 1056], [R_ar[b]], [R_aS])
                    cp("act", vS[:, j, :], vrow[b][:, 1040:1056], [R_vr[b]], [R_aS])
                else:
                    tt("dve", mT[:, j, 0:1024], cc, vrow[b][:, 0:1024], ALU.mult, [R_cr[b], R_vr[b]], [R_mT])
                    cp("act", convp[:, :, j], arow[b][:, 2 + 1022:2 + 1024], [R_ar[b]], [R_convp])
            if half == 0:
                cs1 = crow[0][:, 0:352].rearrange("p (j n) -> p j n", n=16)
                cs2 = crow[1][:, 0:352].rearrange("p (j n) -> p j n", n=16)
                bufv = bufT[:].rearrange("p j (n r) -> p j n r", r=2)

                def wbc(r):
                    return bc(cwT[:, r * NJ:(r + 1) * NJ].unsqueeze(2), [128, NJ, 16])
                rs_ = [R_aS, R_bufT, R_vec, R_cr[0], R_cr[1]]
                ws_ = [R_cr[0], R_cr[1]]
                tt("dve", cs1, aS[:], wbc(2), ALU.mult, rs_, ws_)
                tt("dve", cs1, cs1, bc(cbT.unsqueeze(2), [128, NJ, 16]), ALU.add, rs_, ws_)
                tt("dve", cs2, bufv[:, :, :, 1], wbc(1), ALU.mult, rs_, ws_)
                tt("dve", cs1, cs1, cs2, ALU.add, rs_, ws_)
                tt("dve", cs2, bufv[:, :, :, 0], wbc(0), ALU.mult, rs_, ws_)
                tt("dve", cs1, cs1, cs2, ALU.add, rs_, ws_)
                act(cs1, cs1, AF.Silu, rs_, ws_)
                tt("dve", mT[:, :, 16:32], cs1, vS[:], ALU.mult, rs_, [R_mT])
            if half == 0:
                groups = [(32, 512, [0, 1, 2, 3]), (544, 512, [4, 5, 6, 7]), (0, 32, [16])]
            else:
                groups = [(0, 512, [8, 9, 10, 11]), (512, 512, [12, 13, 14, 15])]
            def pb_mm(ft, gi_):
                m0, Wt, tiles = groups[gi_]
                wb = ft % 2
                if gi_ == 0:
                    S.dma("pool", wdnb[wb][:], w_dn_v[:, :, 128 * ft:128 * ft + 128], writes=[R_wdn[wb]])
                pi = bank()
                yb = (ft * len(groups) + gi_) % 2
                for j in range(NJ):
                    mm(PB(pi, Wt), wdnb[wb][:, j, :], mT[:, j, m0:m0 + Wt], j == 0, j == NJ - 1,
                       [R_wdn[wb], R_mT], [PR[pi]])
                cp("act", yts[yb][:, 0:Wt], PB(pi, Wt), [PR[pi]], [R_yt[yb]])

            def pb_tr(ft, gi_):
                m0, Wt, tiles = groups[gi_]
                yb = (ft * len(groups) + gi_) % 2
                pt = bank()
                for n_, ti in enumerate(tiles):
                    R = 32 if ti == 16 else 128
                    tr(PB(pt)[0:R, 128 * n_:128 * n_ + 128], yts[yb][:, 128 * n_:128 * n_ + R], identf,
                       [R_yt[yb], R_cst], [PR[pt]], signal=(n_ == len(tiles) - 1))
                for n_, ti in enumerate(tiles):
                    R = 32 if ti == 16 else 128
                    hs = h1[0:R, ti, 128 * ft:128 * ft + 128]
                    tt("dve", hs, hs, PB(pt)[0:R, 128 * n_:128 * n_ + 128], ALU.add, [R_h1[ti], PR[pt]], [R_h1[ti]])

            steps = [(ft, gi_) for ft in range(8) for gi_ in range(len(groups))]
            pb_mm(*steps[0])
            for i_ in range(1, len(steps)):
                pb_mm(*steps[i_])
                pb_tr(*steps[i_ - 1])
            pb_tr(*steps[-1])
            for (m0, Wt, tiles) in groups:
                for ti in tiles:
                    final_norm(ti)

        pi = bank()
        tr(PB(pi)[0:44, 0:128], convp[:].rearrange("p r j -> p (r j)"), identf, [R_convp, R_cst], [PR[pi]])
        cvo = A.alloc([44, 128], F32)
        R_cvo = Reg()
        cp("act", cvo[:], PB(pi)[0:44, 0:128], [PR[pi]], [R_cvo])
        for r in range(2):
            S.dma("sp", o_cvp[r].rearrange("(j f) -> j f", f=128), cvo[22 * r:22 * r + 22, :], reads=[R_cvo])
        for grp in range(6):
            pi = bank()
            js = list(range(4 * grp, min(4 * grp + 4, NJ)))
            for n_, j in enumerate(js):
                tr(PB(pi)[0:16, 128 * n_:128 * n_ + 128], aS[:, j, :], identf, [R_aS, R_cst], [PR[pi]],
                   signal=(n_ == len(js) - 1))
            cp("act", cbuf[0:16, 512 * grp:512 * grp + 128 * len(js)], PB(pi)[0:16, 0:128 * len(js)], [PR[pi]], [R_mT])
        S.dma("sp", o_cvs[:, 1, :], cbuf[0:16, :], reads=[R_mT])

        S.finish()
        print("built: ops", S.nops, "waits", S.nwaits, "sbuf peak", A.peak - SB_LO, "/", SB_HI - SB_LO)
    return nc


_CACHE = {}


def kernel(**inp):
    f = lambda k: np.asarray(inp[k], np.float32)
    x_prompt = f("x_prompt")
    x_sample = f("x_sample")
    meta = f("meta_tokens")
    vecs = make_vecs(inp)
    nvec = vecs.shape[1]
    if "nc" not in _CACHE:
        _CACHE["nc"] = build_nc(nvec)
    nc = _CACHE["nc"]
    lam = np.stack([f("s5_lambda_re")[0].reshape(16, 128), f("s5_lambda_im")[0].reshape(16, 128),
                    np.repeat(f("s5_log_dt")[0].reshape(16, 2), 64, axis=1)], axis=0)
    bst = np.stack([f("s5_b_re")[0].reshape(16, 128, 16), f("s5_b_im")[0].reshape(16, 128, 16)], axis=0)
    cch = np.stack([f("s5_c_re")[0].reshape(512, 64), f("s5_c_im")[0].reshape(512, 64)], axis=0)
    shared = {
        "w_in": f("w_in")[0], "w_out": f("w_out")[0], "w_up": f("ffn_w_up")[0], "w_down": f("ffn_w_down")[0],
        "w_glu": f("s5_w_glu")[0], "lamPP": np.ascontiguousarray(lam), "bst": np.ascontiguousarray(bst),
        "cch": np.ascontiguousarray(cch), "vecs": vecs, "hlbrows": f("hg_lower_bounds"),
        "gfinal": f("final_norm_g"), "consts": CONSTS,
    }
    in_maps = []
    for c in range(NCORES):
        sl = slice(16 * c, 16 * c + 16)
        m = dict(shared)
        m["xtok"] = np.ascontiguousarray(np.concatenate([x_prompt[c], meta, x_sample[sl, 0, :]], axis=0))
        m["s5s"] = np.ascontiguousarray(np.stack([f("state_s5_re")[0, sl].reshape(16, 2048),
                                                  f("state_s5_im")[0, sl].reshape(16, 2048)], axis=0))
        m["hgs"] = np.ascontiguousarray(f("state_hgrn")[0, sl])
        m["cvs"] = np.ascontiguousarray(f("state_ffn_conv")[0, sl])
        in_maps.append(m)
    res = run_bass_kernel_spmd(nc, in_maps, core_ids=list(range(NCORES)))
    rs = res.results
    y_prompt = np.stack([rs[c]["y"][0:2048] for c in range(NCORES)], axis=0)
    y_sample = np.concatenate([rs[c]["y"][2064:2080] for c in range(NCORES)], axis=0)[:, None, :]
    s5p_re = np.stack([rs[c]["o_s5p"][0].reshape(32, 64) for c in range(NCORES)], axis=0)[None]
    s5p_im = np.stack([rs[c]["o_s5p"][1].reshape(32, 64) for c in range(NCORES)], axis=0)[None]
    hgp = np.stack([rs[c]["o_hgp"] for c in range(NCORES)], axis=0)[None]
    cvp = np.stack([rs[c]["o_cvp"] for c in range(NCORES)], axis=0)[None]
    s5s_re = np.concatenate([rs[c]["o_s5s"][0].reshape(16, 32, 64) for c in range(NCORES)], axis=0)[None]
    s5s_im = np.concatenate([rs[c]["o_s5s"][1].reshape(16, 32, 64) for c in range(NCORES)], axis=0)[None]
    hgs = np.concatenate([rs[c]["o_hgs"] for c in range(NCORES)], axis=0)[None]
    cvs = np.concatenate([rs[c]["o_cvs"] for c in range(NCORES)], axis=0)[None]
    out = (y_prompt, y_sample, s5p_re, s5p_im, hgp, cvp, s5s_re, s5s_im, hgs, cvs)
    return tuple(np.ascontiguousarray(o, dtype=np.float32) for o in out)
```
